# Optimizing a Trainium2 kernel written in Bass

```python
import jax, jax.numpy as jnp
from jax import lax
import numpy as np

D_MODEL = 2048
BATCH = 8
SEQ = 4096
DEPTH = 4

N_A_LAYERS = DEPTH // 2
N_B_LAYERS = DEPTH - N_A_LAYERS
HG_HEADS = 16
HG_DK = D_MODEL // HG_HEADS
HG_DV = D_MODEL // HG_HEADS
HG_CHUNK = 64
MLA_HEADS = 16
MLA_NOPE = 128
MLA_ROPE = 64
MLA_V = 128
MLA_QK = MLA_NOPE + MLA_ROPE
Q_LORA = 512
KV_LORA = 512
ROPE_THETA = 10000.0
Q_BLOCK = 128
D_FF = 4 * D_MODEL
N_MOD = 6
EPS = 1e-6

kernel_name = "hybrid_hgrn2_mla_yoco_adaln"


def rms_norm(x, gain):
    xf = x.astype(jnp.float32)
    y = xf * lax.rsqrt(jnp.mean(xf * xf, axis=-1, keepdims=True) + EPS)
    return (y * gain.astype(jnp.float32)).astype(x.dtype)


def modulate(h, shift, scale):
    return h * (1 + scale[:, None, :]) + shift[:, None, :]


def rope_tables(positions):
    inv_freq = 1.0 / (ROPE_THETA ** (jnp.arange(0, MLA_ROPE, 2, dtype=jnp.float32) / MLA_ROPE))
    ang = positions.astype(jnp.float32)[..., None] * inv_freq
    return jnp.cos(ang)[:, :, None, :], jnp.sin(ang)[:, :, None, :]


def apply_rope(t, cos, sin):
    tf = t.astype(jnp.float32)
    t1, t2 = jnp.split(tf, 2, axis=-1)
    out = jnp.concatenate([t1 * cos - t2 * sin, t2 * cos + t1 * sin], axis=-1)
    return out.astype(t.dtype)


def squared_relu_mlp(h, w1, w2):
    a = jax.nn.relu(h @ w1)
    return (a * a) @ w2


def hgrn2_mixer(h, w_in, lb, o_gain, w_out):
    B, S, D = h.shape
    f32 = jnp.float32
    proj = h @ w_in
    q, fz, i, g = jnp.split(proj, 4, axis=-1)
    fz = fz.astype(f32)
    lbf = lb.astype(f32)
    log_f = jnp.logaddexp(jnp.log(lbf), jnp.log1p(-lbf) + jax.nn.log_sigmoid(fz))
    k = (1.0 - lbf) * jax.nn.sigmoid(-fz)

    n_chunks = S // HG_CHUNK

    def to_chunks(t, d):
        t = t.astype(f32).reshape(B, n_chunks, HG_CHUNK, HG_HEADS, d)
        return t.transpose(1, 0, 3, 2, 4)

    qc = to_chunks(q, HG_DK) * (HG_DK ** -0.5)
    kc = to_chunks(k, HG_DK)
    vc = to_chunks(i, HG_DV)
    lfc = to_chunks(log_f, HG_DK)
    tri = jnp.tril(jnp.ones((HG_CHUNK, HG_CHUNK), dtype=bool))

    def step(state, inp):
        q_, k_, v_, lf_ = inp
        b = jnp.cumsum(lf_, axis=2)
        diff = b[:, :, :, None, :] - b[:, :, None, :, :]
        decay = jnp.where(tri[:, :, None], jnp.exp(jnp.minimum(diff, 0.0)), 0.0)
        scores = jnp.einsum('bhtk,bhtsk,bhsk->bhts', q_, decay, k_)
        o = scores @ v_ + jnp.einsum('bhtk,bhkv->bhtv', q_ * jnp.exp(b), state)
        b_last = b[:, :, -1:, :]
        state = (jnp.exp(b_last[:, :, 0, :, None]) * state
                 + jnp.einsum('bhsk,bhsv->bhkv', k_ * jnp.exp(b_last - b), v_))
        return state, o

    state0 = jnp.zeros((B, HG_HEADS, HG_DK, HG_DV), f32)
    _, o = lax.scan(step, state0, (qc, kc, vc, lfc))
    o = o.transpose(1, 0, 3, 2, 4).reshape(B, S, HG_HEADS, HG_DV)
    o = rms_norm(o, o_gain).reshape(B, S, D)
    o = o.astype(h.dtype) * jax.nn.silu(g)
    return o @ w_out


def mla_shared_kv(x, cond, kv_ada_w, kv_ada_b, kv_in_norm, w_dkv, kv_lat_norm, w_ukv, k_gain, cos, sin):
    B, S, _ = x.shape
    shift, scale = jnp.split(cond @ kv_ada_w + kv_ada_b, 2, axis=-1)
    h = modulate(rms_norm(x, kv_in_norm), shift, scale)
    ckv = h @ w_dkv
    c_lat = rms_norm(ckv[..., :KV_LORA], kv_lat_norm)
    k_pe = ckv[..., KV_LORA:]
    kv = (c_lat @ w_ukv).reshape(B, S, MLA_HEADS, MLA_NOPE + MLA_V)
    k_nope, v = kv[..., :MLA_NOPE], kv[..., MLA_NOPE:]
    k = jnp.concatenate([k_nope, jnp.broadcast_to(k_pe[:, :, None, :], (B, S, MLA_HEADS, MLA_ROPE))], axis=-1)
    k = rms_norm(k, k_gain)
    k = jnp.concatenate([k[..., :MLA_NOPE], apply_rope(k[..., MLA_NOPE:], cos, sin)], axis=-1)
    return k.transpose(0, 2, 1, 3), v.transpose(0, 2, 1, 3)


def mla_mixer(h, k, v, w_dq, q_lat_norm, w_uq, q_gain, w_o, cos, sin):
    B, S, _ = h.shape
    cq = rms_norm(h @ w_dq, q_lat_norm)
    q = (cq @ w_uq).reshape(B, S, MLA_HEADS, MLA_QK)
    q = rms_norm(q, q_gain)
    q = jnp.concatenate([q[..., :MLA_NOPE], apply_rope(q[..., MLA_NOPE:], cos, sin)], axis=-1)
    n_blocks = S // Q_BLOCK
    qb = q.reshape(B, n_blocks, Q_BLOCK, MLA_HEADS, MLA_QK).transpose(1, 0, 3, 2, 4)
    starts = jnp.arange(n_blocks, dtype=jnp.int32) * Q_BLOCK
    key_idx = jnp.arange(S, dtype=jnp.int32)
    sm_scale = MLA_QK ** -0.5

    def attend(args):
        q_blk, start = args
        s = jnp.einsum('bhqd,bhkd->bhqk', q_blk, k, preferred_element_type=jnp.float32) * sm_scale
        q_idx = start + jnp.arange(Q_BLOCK, dtype=jnp.int32)
        s = jnp.where(key_idx[None, :] <= q_idx[:, None], s, -jnp.inf)
        p = jax.nn.softmax(s, axis=-1)
        return jnp.einsum('bhqk,bhkd->bhqd', p.astype(v.dtype), v)

    o = lax.map(attend, (qb, starts))
    o = o.transpose(1, 0, 3, 2, 4).reshape(B, S, MLA_HEADS * MLA_V)
    return o @ w_o


def setup_inputs(seed: int = 0) -> dict:
    key = jax.random.key(seed)
    ks = jax.random.split(key, 32)
    f32 = jnp.float32

    def nrm(k, shape, scale):
        return jax.random.normal(k, shape, f32) * scale

    def gain(k, shape):
        return 1.0 + 0.02 * jax.random.normal(k, shape, f32)

    D = D_MODEL
    offset = jax.random.randint(ks[2], (BATCH, 1), 0, 1024, dtype=jnp.int32)
    positions = (offset + jnp.arange(SEQ, dtype=jnp.int32)[None, :]).astype(jnp.int32)
    return {
        "x": nrm(ks[0], (BATCH, SEQ, D), 1.0),
        "c": nrm(ks[1], (BATCH, D), 1.0),
        "positions": positions,
        "ada_w": nrm(ks[3], (DEPTH, D, N_MOD * D), 0.5 * D ** -0.5),
        "ada_b": nrm(ks[4], (DEPTH, N_MOD * D), 0.02),
        "norm_mix": gain(ks[5], (DEPTH, D)),
        "norm_mlp": gain(ks[6], (DEPTH, D)),
        "mlp_w1": nrm(ks[7], (DEPTH, D, D_FF), D ** -0.5),
        "mlp_w2": nrm(ks[8], (DEPTH, D_FF, D), D_FF ** -0.5),
        "hg_w_in": nrm(ks[9], (N_A_LAYERS, D, 4 * D), D ** -0.5),
        "hg_lower": nrm(ks[10], (N_A_LAYERS + 1, HG_HEADS * HG_DK), 0.1),
        "hg_o_norm": gain(ks[11], (N_A_LAYERS, HG_DV)),
        "hg_w_out": nrm(ks[12], (N_A_LAYERS, D, D), D ** -0.5),
        "kv_ada_w": nrm(ks[13], (D, 2 * D), 0.5 * D ** -0.5),
        "kv_ada_b": nrm(ks[14], (2 * D,), 0.02),
        "kv_in_norm": gain(ks[15], (D,)),
        "mla_w_dkv": nrm(ks[16], (D, KV_LORA + MLA_ROPE), D ** -0.5),
        "mla_kv_norm": gain(ks[17], (KV_LORA,)),
        "mla_w_ukv": nrm(ks[18], (KV_LORA, MLA_HEADS * (MLA_NOPE + MLA_V)), KV_LORA ** -0.5),
        "mla_k_norm": gain(ks[19], (MLA_QK,)),
        "mla_w_dq": nrm(ks[20], (N_B_LAYERS, D, Q_LORA), D ** -0.5),
        "mla_q_lat_norm": gain(ks[21], (N_B_LAYERS, Q_LORA)),
        "mla_w_uq": nrm(ks[22], (N_B_LAYERS, Q_LORA, MLA_HEADS * MLA_QK), Q_LORA ** -0.5),
        "mla_q_norm": gain(ks[23], (N_B_LAYERS, MLA_QK)),
        "mla_w_o": nrm(ks[24], (N_B_LAYERS, MLA_HEADS * MLA_V, D), (MLA_HEADS * MLA_V) ** -0.5),
    }


def reference(x, c, positions, ada_w, ada_b, norm_mix, norm_mlp, mlp_w1, mlp_w2,
              hg_w_in, hg_lower, hg_o_norm, hg_w_out,
              kv_ada_w, kv_ada_b, kv_in_norm, mla_w_dkv, mla_kv_norm, mla_w_ukv, mla_k_norm,
              mla_w_dq, mla_q_lat_norm, mla_w_uq, mla_q_norm, mla_w_o):
    cond = jax.nn.silu(c)
    cos, sin = rope_tables(positions)
    lb_all = jnp.cumsum(jax.nn.softmax(hg_lower.astype(jnp.float32), axis=0), axis=0)[:N_A_LAYERS]
    k_shared = None
    v_shared = None
    for layer in range(DEPTH):
        mod = cond @ ada_w[layer] + ada_b[layer]
        sh1, sc1, g1, sh2, sc2, g2 = jnp.split(mod, N_MOD, axis=-1)
        if layer == N_A_LAYERS:
            k_shared, v_shared = mla_shared_kv(x, cond, kv_ada_w, kv_ada_b, kv_in_norm, mla_w_dkv,
                                               mla_kv_norm, mla_w_ukv, mla_k_norm, cos, sin)
        h = modulate(rms_norm(x, norm_mix[layer]), sh1, sc1)
        if layer < N_A_LAYERS:
            y = hgrn2_mixer(h, hg_w_in[layer], lb_all[layer], hg_o_norm[layer], hg_w_out[layer])
        else:
            j = layer - N_A_LAYERS
            y = mla_mixer(h, k_shared, v_shared, mla_w_dq[j], mla_q_lat_norm[j], mla_w_uq[j],
                          mla_q_norm[j], mla_w_o[j], cos, sin)
        x = x + g1[:, None, :] * y.astype(x.dtype)
        h = modulate(rms_norm(x, norm_mlp[layer]), sh2, sc2)
        x = x + g2[:, None, :] * squared_relu_mlp(h, mlp_w1[layer], mlp_w2[layer]).astype(x.dtype)
    return x
```

```python
import math
from contextlib import ExitStack

import numpy as np
import concourse.bass as bass
import concourse.mybir as mybir
from concourse.bass_utils import run_bass_kernel_spmd

F32 = mybir.dt.float32
BF16 = mybir.dt.bfloat16
I32 = mybir.dt.int32
AF = mybir.ActivationFunctionType
ALU = mybir.AluOpType

D = 2048
DC = 16
TT = 512
DFF = 8192
FC = 64
NH = 16
EPS = 1e-6
QK = 192
SLAB_BYTES = 16384


class Sem:
    __slots__ = ("h", "val", "name")

    def __init__(self, h, name):
        self.h = h
        self.val = 0
        self.name = name


class Buf:
    __slots__ = ("w", "r", "name")

    def __init__(self, name=""):
        self.w = {}
        self.r = {}
        self.name = name


class Eng:
    def __init__(self, name, eng, sem, is_pe=False):
        self.name = name
        self.eng = eng
        self.sem = sem
        self.waited = {}
        self.is_pe = is_pe
        self.dma_sems = []
        self.dma_rr = 0


class Prog:
    def __init__(self, nc, es, n_dma_sems=20):
        self.nc = nc
        self.es = es
        mk = lambda n: Sem(es.enter_context(nc.semaphore(n)), n)
        self.pe = Eng("pe", nc.tensor, mk("s_pe"), is_pe=True)
        self.act = Eng("act", nc.scalar, mk("s_act"))
        self.dve = Eng("dve", nc.vector, mk("s_dve"))
        self.pool = Eng("pool", nc.gpsimd, mk("s_pool"))
        self.sp = Eng("sp", nc.sync, mk("s_sp"))
        for q in (self.sp, self.pool, self.act):
            q.dma_sems = [mk(f"d_{q.name}{i}") for i in range(n_dma_sems)]
        self.n_inst = 0

    def _wait(self, E, sem, val):
        if E.waited.get(sem, 0) < val:
            E.eng.wait_ge(sem.h, val)
            E.waited[sem] = val

    def _deps(self, E, reads, writes):
        need = {}
        for b in reads:
            for s, v in b.w.items():
                if need.get(s, 0) < v:
                    need[s] = v
        for b in writes:
            for s, v in b.w.items():
                if need.get(s, 0) < v:
                    need[s] = v
            for s, v in b.r.items():
                if s is E.sem:
                    continue
                if need.get(s, 0) < v:
                    need[s] = v
        for s, v in need.items():
            if E.is_pe and s is E.sem:
                continue
            self._wait(E, s, v)

    def op(self, E, emit, reads=(), writes=()):
        self._deps(E, reads, writes)
        inst = emit()
        E.sem.val += 1
        inst.then_inc(E.sem.h, 1)
        v = E.sem.val
        for b in writes:
            b.w = {E.sem: v}
            b.r = {}
        for b in reads:
            b.r[E.sem] = v
        self.n_inst += 1
        return inst

    def dma(self, Q, out, in_, reads=(), writes=(), **kw):
        s = Q.dma_sems[Q.dma_rr % len(Q.dma_sems)]
        Q.dma_rr += 1
        self._wait(Q, s, s.val)
        self._deps(Q, reads, writes)
        inst = Q.eng.dma_start(out=out, in_=in_, **kw)
        s.val += 16
        inst.then_inc(s.h, 16)
        for b in writes:
            b.w = {s: s.val}
            b.r = {}
        for b in reads:
            b.r[s] = s.val
        self.n_inst += 1
        return s

    def merge(self, olds, news):
        w = {}
        for b in olds:
            for d in (b.w, b.r):
                for s, v in d.items():
                    if w.get(s, 0) < v:
                        w[s] = v
        for b in news:
            for s, v in w.items():
                if b.w.get(s, 0) < v:
                    b.w[s] = v

    def final_wait(self, E, bufs):
        need = {}
        for b in bufs:
            for s, v in b.w.items():
                need[s] = max(need.get(s, 0), v)
        for s, v in need.items():
            self._wait(E, s, v)


class Arr:
    def __init__(self, t, n, name):
        self.t = t
        self.n = n
        self.bufs = [Buf(f"{name}[{i}]") for i in range(n)]

    def b(self, i):
        return self.bufs[i]

    def all(self):
        return list(self.bufs)


class WSpec:
    def __init__(self, name, din, dout, sw, kgs=None):
        self.name = name
        self.din, self.dout, self.sw = din, dout, sw
        self.kc = din // 128
        self.kgs = kgs or self.kc
        self.nkg = self.kc // self.kgs
        self.ncg = dout // sw
        assert self.kgs * sw * 2 <= SLAB_BYTES + 2048, (name, self.kgs, sw)


def build_program(S, cfg):
    NT = S // TT
    n_layers = cfg.get("n_layers", 4)
    NA = cfg.get("n_a", 2)
    do_mix = cfg.get("mix", True)
    do_mlp = cfg.get("mlp", True)
    nc = bass.Bass("TRN2", target_bir_lowering=False, dynamic_dma_scratch_size=8192)
    es = ExitStack()
    p = Prog(nc, es)
    pe, act, dve, pool, sp = p.pe, p.act, p.dve, p.pool, p.sp
    p.in_names = []

    def dram_in(name, shape, dt=F32):
        p.in_names.append(name)
        return nc.dram_tensor(name, list(shape), dt, kind="ExternalInput").ap()

    def dram_tmp(name, shape, dt):
        return nc.dram_tensor(name, list(shape), dt, kind="Internal").ap()

    def sb(name, shape, dt):
        return es.enter_context(nc.sbuf_tensor(name, list(shape), dt))

    NL = n_layers
    NAL = min(NA, NL)
    xT = dram_in("xT", [D, S])
    outT = nc.dram_tensor("outT", [D, S], F32, kind="ExternalOutput").ap()
    c_pj = dram_in("c_pj", [128, DC])
    ada_w = dram_in("ada_w", [NL, D, 6 * D])
    ada_b = dram_in("ada_b", [NL, 128, 96])
    nmix = dram_in("norm_mix", [NL, 128, DC])
    nmlp = dram_in("norm_mlp", [NL, 128, DC])
    mlp_w1 = dram_in("mlp_w1", [NL, D, DFF])
    mlp_w2 = dram_in("mlp_w2", [NL, DFF, D])
    consts = dram_in("consts", [4, 128, 128])
    if do_mix and NAL > 0:
        hg_w_in = dram_in("hg_w_in", [NAL, D, 4 * D])
        hg_lower = dram_in("hg_lower", [3, D])
        hg_o_norm = dram_in("hg_o_norm", [NAL, 128, 1])
        hg_w_out = dram_in("hg_w_out", [NAL, D, D])

    do_mla = do_mix and NL > NA
    NBL = NL - NA if do_mla else 0
    if do_mla:
        kv_ada_w = dram_in("kv_ada_w", [D, 2 * D])
        kv_ada_b = dram_in("kv_ada_b", [128, 32])
        kv_in_norm = dram_in("kv_in_norm", [128, DC])
        w_dkv = dram_in("mla_w_dkv", [D, 576])
        kv_norm = dram_in("mla_kv_norm", [128, 4])
        w_ukv = dram_in("mla_w_ukv", [512, 4096])
        k_norm = dram_in("mla_k_norm", [128, 3])
        w_dq = dram_in("mla_w_dq", [NBL, D, 512])
        q_lat_norm = dram_in("mla_q_lat_norm", [NBL, 128, 4])
        w_uq = dram_in("mla_w_uq", [NBL, 512, 3072])
        q_norm = dram_in("mla_q_norm", [NBL, 128, 3])
        w_o = dram_in("mla_w_o", [NBL, D, D])
        positions = dram_in("positions", [S], I32)
        rope_c = dram_in("rope_c", [128, 130])

    cst = sb("cst", [128, 4, 128], F32)
    b_cst = Buf("cst")
    p.dma(sp, cst[:], consts.rearrange("a p c -> p a c"), writes=[b_cst])
    ones32 = cst[:, 0, :]
    Uinc = cst[:, 1, :]
    Vsuf = cst[:, 2, :]
    ident_bf = sb("ident_bf", [128, 128], BF16)
    b_ident = Buf("ident")
    p.op(dve, lambda: nc.vector.tensor_copy(out=ident_bf[:], in_=cst[:, 3, :]), reads=[b_cst], writes=[b_ident])
    ones_bf = sb("ones_bf", [128, 128], BF16)
    p.op(dve, lambda: nc.vector.tensor_copy(out=ones_bf[:], in_=cst[:, 0, :]), reads=[b_cst, b_ident], writes=[b_ident])
    b_ones32 = b_cst
    eps_t = sb("eps_t", [128, 1], F32)
    b_eps = Buf("eps")
    p.op(dve, lambda: nc.vector.memset(eps_t[:], EPS), writes=[b_eps])
    cond = sb("cond", [128, DC], F32)
    b_cond = Buf("cond")
    craw = sb("craw", [128, DC], F32)
    b_craw = Buf("craw")
    p.dma(sp, craw[:], c_pj, writes=[b_craw])
    p.op(act, lambda: nc.scalar.activation(out=cond[:], in_=craw[:], func=AF.Silu),
         reads=[b_craw], writes=[b_cond])

    modv = [sb(f"modv{l}", [128, 96], F32) for l in range(NL)]
    b_modv = [Buf(f"modv{l}") for l in range(NL)]
    A1m = [sb(f"A1_{l}", [128, DC], F32) for l in range(NL)]
    A2m = [sb(f"A2_{l}", [128, DC], F32) for l in range(NL)]
    b_A = [Buf(f"A{l}") for l in range(NL)]

    NB = 8
    banks = [es.enter_context(nc.psum_tensor(f"bank{i}", [128, 512], F32)) for i in range(NB)]
    bank_b = [Buf(f"bank{i}") for i in range(NB)]
    free_banks = list(range(NB))

    def balloc():
        assert free_banks, "out of PSUM banks"
        return free_banks.pop(0)

    def bfree(i):
        free_banks.append(i)

    NSLAB = cfg.get("nslab", 3)
    slab_t = [sb(f"slab{i}", [128, SLAB_BYTES // 2], BF16) for i in range(NSLAB)]
    slab_b = [Buf(f"slab{i}") for i in range(NSLAB)]
    rr = {}

    def nxt(key, n):
        i = rr.get(key, 0)
        rr[key] = i + 1
        return i % n

    wsc = {}

    def cast_weight(key, src2d, spec):
        dst = dram_tmp("wb_" + key, [spec.ncg, spec.nkg, 128, spec.kgs, spec.sw], BF16)
        bufs = {}
        for cg in range(spec.ncg):
            for kg in range(spec.nkg):
                src = src2d[kg * spec.kgs * 128:(kg + 1) * spec.kgs * 128,
                            cg * spec.sw:(cg + 1) * spec.sw].rearrange("(k p) c -> p k c", p=128)
                b = Buf(f"wb_{key}_{cg}_{kg}")
                p.dma(pool, dst[cg, kg], src, writes=[b])
                bufs[(cg, kg)] = b
        wsc[key] = (dst, spec, bufs)

    def load_slab(key, cg, kg=0, q=None):
        dst, spec, bufs = wsc[key]
        i = nxt("slab", NSLAB)
        view = slab_t[i][:, 0:spec.kgs * spec.sw].rearrange("p (k c) -> p k c", c=spec.sw)
        p.dma(q or sp, view, dst[cg, kg], reads=[bufs[(cg, kg)]], writes=[slab_b[i]])
        return view, slab_b[i]

    def cast_layer(l):
        if do_mix and l < NAL:
            cast_weight(f"win_{l}", hg_w_in[l], WSpec("win", D, 4 * D, 512))
            cast_weight(f"wout_{l}", hg_w_out[l], WSpec("wout", D, D, 512))
        if do_mla and l == NA:
            cast_weight("wdkv_c", w_dkv[:, 0:512], WSpec("wdkv_c", D, 512, 512))
            cast_weight("wdkv_r", w_dkv[:, 512:576], WSpec("wdkv_r", D, 64, 64))
            cast_weight("wukv", w_ukv, WSpec("wukv", 512, 4096, 2048))
        if do_mla and l >= NA:
            cast_weight(f"wdq_{l}", w_dq[l - NA], WSpec("wdq", D, 512, 512))
            cast_weight(f"wuq_{l}", w_uq[l - NA], WSpec("wuq", 512, 3072, 1536))
            cast_weight(f"wo_{l}", w_o[l - NA], WSpec("wo", D, D, 512))
        if do_mlp:
            cast_weight(f"w1_{l}", mlp_w1[l], WSpec("w1", D, DFF, 512))
            cast_weight(f"w2_{l}", mlp_w2[l], WSpec("w2", DFF, D, 512, kgs=16))

    for l_ in range(n_layers):
        cast_layer(l_)

    R_BYTES = 66 * 1024
    R_t = sb("R", [128, R_BYTES // 2], BF16)

    def R_f32(off, n):
        return R_t[:, off // 2: off // 2 + 2 * n].bitcast(F32)

    def R_bf(off, n):
        return R_t[:, off // 2: off // 2 + n]

    stage_b = []
    adab_t = sb("adab_t", [128, 96], F32)
    b_adab = Buf("adab")
    gain_t = sb("gain_t", [128, 2 * DC], F32)
    b_gain = Buf("gain")
    rowbuf = [sb(f"rowbuf{i}", [1, 256], F32) for i in range(2)]
    row_b = [Buf(f"row{i}") for i in range(2)]
    one1 = cst[0:1, 0, 0:1]
    if do_mla:
        kvmodv = sb("kvmodv", [128, 32], F32)
        Akv = sb("Akv", [128, DC], F32)
        b_kvmod = Buf("kvmod")

    def ada_block(l):
        is_kv = (l == n_layers)
        ncol = 32 if is_kv else 96
        wsrc = kv_ada_w if is_kv else ada_w[l]
        mb = balloc()
        pend_tr = []
        for sg_ in range(ncol // 2):
            i = nxt("slab", NSLAB)
            sv = slab_t[i][:].bitcast(F32).rearrange("p (k c) -> p k c", c=256)
            p.dma(sp, sv, wsrc[:, sg_ * 256:(sg_ + 1) * 256].rearrange("(k p) c -> p k c", p=128),
                  writes=[slab_b[i]])
            rb = balloc()
            for k in range(DC):
                p.op(pe, lambda: nc.tensor.matmul(
                    banks[rb][0:1, 0:256], lhsT=cond[:, k:k + 1], rhs=sv[:, k, :],
                    start=(k == 0), stop=(k == DC - 1)), reads=[slab_b[i], b_cond], writes=[bank_b[rb]])
            ri = nxt("row", 2)
            p.op(act, lambda: nc.scalar.copy(out=rowbuf[ri][0:1, :], in_=banks[rb][0:1, 0:256]),
                 reads=[bank_b[rb]], writes=[row_b[ri]])
            bfree(rb)

            def tr(ri=ri, sg_=sg_):
                for oc in range(2):
                    col = sg_ * 2 + oc
                    p.op(pe, lambda: nc.tensor.matmul(
                        banks[mb][:, col:col + 1], lhsT=rowbuf[ri][0:1, oc * 128:(oc + 1) * 128], rhs=one1,
                        start=True, stop=True), reads=[row_b[ri], b_cst], writes=[bank_b[mb]])
            if pend_tr:
                pend_tr.pop(0)()
            pend_tr.append(tr)
        pend_tr.pop(0)()
        if is_kv:
            p.dma(sp, adab_t[:, 0:32], kv_ada_b, writes=[b_adab])
            p.op(dve, lambda: nc.vector.tensor_tensor(out=kvmodv[:], in0=banks[mb][:, 0:32],
                                                      in1=adab_t[:, 0:32], op=ALU.add),
                 reads=[bank_b[mb], b_adab], writes=[b_kvmod])
            bfree(mb)
            p.dma(sp, gain_t[:, 0:DC], kv_in_norm, writes=[b_gain])
            p.op(dve, lambda: nc.vector.scalar_tensor_tensor(
                out=Akv[:], in0=kvmodv[:, 16:32], scalar=1.0, in1=gain_t[:, 0:DC],
                op0=ALU.add, op1=ALU.mult), reads=[b_kvmod, b_gain], writes=[b_kvmod])
            return
        p.dma(sp, adab_t[:], ada_b[l], writes=[b_adab])
        p.op(dve, lambda: nc.vector.tensor_tensor(out=modv[l][:], in0=banks[mb][:, 0:96],
                                                  in1=adab_t[:], op=ALU.add),
             reads=[bank_b[mb], b_adab], writes=[b_modv[l]])
        bfree(mb)
        p.dma(sp, gain_t[:, 0:DC], nmix[l], writes=[b_gain])
        p.dma(sp, gain_t[:, DC:2 * DC], nmlp[l], writes=[b_gain])
        p.op(dve, lambda: nc.vector.scalar_tensor_tensor(
            out=A1m[l][:], in0=modv[l][:, 16:32], scalar=1.0, in1=gain_t[:, 0:DC],
            op0=ALU.add, op1=ALU.mult), reads=[b_modv[l], b_gain], writes=[b_A[l]])
        p.op(dve, lambda: nc.vector.scalar_tensor_tensor(
            out=A2m[l][:], in0=modv[l][:, 64:80], scalar=1.0, in1=gain_t[:, DC:2 * DC],
            op0=ALU.add, op1=ALU.mult), reads=[b_modv[l], b_gain], writes=[b_A[l]])

    pro_bufs = list(stage_b)

    if do_mix and NAL > 0:
        lbs = dram_tmp("lbs", [2, 128, D], F32)
        b_lbs = Buf("lbs")
        OFF = 40 * 1024
        r3 = R_f32(OFF, 3 * D)
        r3v = r3.rearrange("p (a n) -> p a n", n=D)
        b_r3 = Buf("r3")
        p.merge(stage_b, [b_r3])
        p.dma(sp, r3, hg_lower.rearrange("a n -> (a n)").partition_broadcast(128), writes=[b_r3])
        mx = R_f32(0, D)
        sm = R_f32(8192, D)
        b_mx = Buf("mx")
        p.merge(stage_b, [b_mx])
        p.op(dve, lambda: nc.vector.tensor_tensor(out=mx, in0=r3v[:, 0, :], in1=r3v[:, 1, :], op=ALU.max),
             reads=[b_r3], writes=[b_mx])
        p.op(dve, lambda: nc.vector.tensor_tensor(out=mx, in0=mx, in1=r3v[:, 2, :], op=ALU.max),
             reads=[b_r3, b_mx], writes=[b_mx])
        for a_ in range(3):
            p.op(dve, lambda a_=a_: nc.vector.tensor_tensor(out=r3v[:, a_, :], in0=r3v[:, a_, :], in1=mx,
                                                            op=ALU.subtract), reads=[b_r3, b_mx], writes=[b_r3])
        p.op(act, lambda: nc.scalar.activation(out=r3, in_=r3, func=AF.Exp), reads=[b_r3], writes=[b_r3])
        p.op(dve, lambda: nc.vector.tensor_tensor(out=r3v[:, 1, :], in0=r3v[:, 0, :], in1=r3v[:, 1, :], op=ALU.add),
             reads=[b_r3], writes=[b_r3])
        p.op(dve, lambda: nc.vector.tensor_tensor(out=sm, in0=r3v[:, 1, :], in1=r3v[:, 2, :], op=ALU.add),
             reads=[b_r3, b_mx], writes=[b_mx])
        p.op(dve, lambda: nc.vector.reciprocal(out=sm, in_=sm), reads=[b_mx], writes=[b_mx])
        for a_ in range(2):
            p.op(dve, lambda a_=a_: nc.vector.tensor_tensor(out=r3v[:, a_, :], in0=r3v[:, a_, :], in1=sm,
                                                            op=ALU.mult), reads=[b_r3, b_mx], writes=[b_r3])
        p.dma(sp, lbs.rearrange("a p n -> p a n"), r3v[:, 0:2, :], reads=[b_r3], writes=[b_lbs])
        pro_bufs += [b_r3, b_mx]
        ogain = sb("ogain", [128, NAL], F32)
        b_ogain = Buf("ogain")
        for l in range(NAL):
            p.dma(sp, ogain[:, l:l + 1], hg_o_norm[l], writes=[b_ogain])

    x_t = sb("x_t", [128, DC, TT], F32)
    x = Arr(x_t, DC, "x")
    h_t = sb("h_t", [128, DC, TT], BF16)
    h = Arr(h_t, DC, "h")
    a_view = R_bf(0, FC * TT).rearrange("p (c t) -> p c t", t=TT)
    a = Arr(None, FC, "a")
    p.merge(pro_bufs, a.all())
    rstd_t = sb("rstd", [128, TT], F32)
    b_rstd = Buf("rstd")
    NTMP = 5
    tmp_t = [sb(f"tmp{i}", [128, TT], F32) for i in range(NTMP)]
    tmp_b = [Buf(f"tmp{i}") for i in range(NTMP)]

    def rms_stats(chunks, n_feat, out_ap, out_b):
        bk = balloc()
        n = len(chunks)
        for i, (ap, b, npart) in enumerate(chunks):
            if npart < 0:
                npart = -npart
                p.op(pe, lambda i=i, npart=npart, ap=ap: nc.tensor.matmul(
                    banks[bk][:], lhsT=ones_bf[0:npart, :], rhs=ap,
                    start=(i == 0), stop=(i == n - 1)), reads=[b, b_ident], writes=[bank_b[bk]])
                continue
            si = nxt("tmp", NTMP)
            sqv = tmp_t[si][:, 0:TT // 2].bitcast(BF16)
            p.op(act, lambda ap=ap, sqv=sqv, npart=npart: nc.scalar.activation(
                out=sqv[0:npart, :], in_=ap, func=AF.Square), reads=[b], writes=[tmp_b[si]])
            p.op(pe, lambda sqv=sqv, i=i, npart=npart: nc.tensor.matmul(
                banks[bk][:], lhsT=ones_bf[0:npart, :], rhs=sqv[0:npart, :],
                start=(i == 0), stop=(i == n - 1)), reads=[tmp_b[si], b_ident], writes=[bank_b[bk]])
        p.op(act, lambda: nc.scalar.activation(out=out_ap, in_=banks[bk][:], func=AF.Ln,
                                               scale=1.0 / n_feat, bias=eps_t[:, 0:1]),
             reads=[bank_b[bk], b_eps], writes=[out_b])
        bfree(bk)
        p.op(act, lambda: nc.scalar.activation(out=out_ap, in_=out_ap, func=AF.Exp, scale=-0.5),
             reads=[out_b], writes=[out_b])

    def norm_mod(Acols, shcols, bA, bsh):
        rms_stats([(x_t[:, c, :], x.b(c), 128) for c in range(DC)], D, rstd_t[:], b_rstd)
        for c in range(DC):
            ti = nxt("tmp", NTMP)
            p.op(dve, lambda c=c, ti=ti: nc.vector.scalar_tensor_tensor(
                out=tmp_t[ti][:], in0=x_t[:, c, :], scalar=Acols[:, c:c + 1], in1=rstd_t[:],
                op0=ALU.mult, op1=ALU.mult), reads=[x.b(c), bA, b_rstd], writes=[tmp_b[ti]])
            p.op(act, lambda c=c, ti=ti: nc.scalar.activation(
                out=h_t[:, c, :], in_=tmp_t[ti][:], func=AF.Identity, bias=shcols[:, c:c + 1]),
                reads=[tmp_b[ti], bsh], writes=[h.b(c)])

    def proj_epilogue(key, l, rhs_view, rhs_bufs, gcol0):
        for cg in range(4):
            wv, wb = load_slab(key, cg)
            for dcl in range(4):
                dc = cg * 4 + dcl
                bk = balloc()
                for k in range(16):
                    p.op(pe, lambda k=k, dcl=dcl, wv=wv, bk=bk: nc.tensor.matmul(
                        banks[bk][:], lhsT=wv[:, k, dcl * 128:(dcl + 1) * 128], rhs=rhs_view[:, k, :],
                        start=(k == 0), stop=(k == 15)), reads=[wb, rhs_bufs[k]], writes=[bank_b[bk]])
                p.op(dve, lambda dc=dc, bk=bk: nc.vector.scalar_tensor_tensor(
                    out=x_t[:, dc, :], in0=banks[bk][:], scalar=modv[l][:, gcol0 + dc:gcol0 + dc + 1],
                    in1=x_t[:, dc, :], op0=ALU.mult, op1=ALU.add),
                    reads=[bank_b[bk], b_modv[l], x.b(dc)], writes=[x.b(dc)])
                bfree(bk)

    def mlp_block(l, on_chunk_done=None):
        norm_mod(A2m[l], modv[l][:, 48:64], b_A[l], b_modv[l])
        for s_ in range(16):
            wv, wb = load_slab(f"w1_{l}", s_)
            for oc in range(4):
                bk = balloc()
                for k in range(DC):
                    p.op(pe, lambda k=k, oc=oc, wv=wv, bk=bk: nc.tensor.matmul(
                        banks[bk][:], lhsT=wv[:, k, oc * 128:(oc + 1) * 128], rhs=h_t[:, k, :],
                        start=(k == 0), stop=(k == DC - 1)),
                        reads=[wb, h.b(k)], writes=[bank_b[bk]])
                fc = s_ * 4 + oc
                ti = nxt("tmp", NTMP)
                p.op(act, lambda bk=bk, ti=ti: nc.scalar.activation(
                    out=tmp_t[ti][:], in_=banks[bk][:], func=AF.Relu),
                    reads=[bank_b[bk]], writes=[tmp_b[ti]])
                bfree(bk)
                p.op(dve, lambda fc=fc, ti=ti: nc.vector.tensor_tensor(
                    out=a_view[:, fc, :], in0=tmp_t[ti][:], in1=tmp_t[ti][:], op=ALU.mult),
                    reads=[tmp_b[ti]], writes=[a.b(fc)])
        for cg in range(4):
            bks = [balloc() for _ in range(4)]
            for kg in range(4):
                wv, wb = load_slab(f"w2_{l}", cg, kg)
                for dcl in range(4):
                    for k in range(16):
                        fc = kg * 16 + k
                        p.op(pe, lambda k=k, dcl=dcl, wv=wv, fc=fc, bk=bks[dcl], kg=kg: nc.tensor.matmul(
                            banks[bk][:], lhsT=wv[:, k, dcl * 128:(dcl + 1) * 128], rhs=a_view[:, fc, :],
                            start=(kg == 0 and k == 0), stop=(kg == 3 and k == 15)),
                            reads=[wb, a.b(fc)], writes=[bank_b[bks[dcl]]])
            for dcl in range(4):
                dc = cg * 4 + dcl
                p.op(dve, lambda dc=dc, bk=bks[dcl]: nc.vector.scalar_tensor_tensor(
                    out=x_t[:, dc, :], in0=banks[bk][:], scalar=modv[l][:, 80 + dc:81 + dc],
                    in1=x_t[:, dc, :], op0=ALU.mult, op1=ALU.add),
                    reads=[bank_b[bks[dcl]], b_modv[l], x.b(dc)], writes=[x.b(dc)])
                bfree(bks[dcl])
                if on_chunk_done is not None:
                    on_chunk_done(dc)

    if do_mix and NAL > 0:
        KB = 1024
        hA = [[R_f32((j * 4 + r * 2) * KB, 512) for r in range(2)] for j in range(4)]
        hA_b = [[Buf(f"hA{j}_{r}") for r in range(2)] for j in range(4)]
        ktil = R_bf(16 * KB, 4 * 512).rearrange("p (a c) -> p a c", c=512)
        khat = R_bf(20 * KB, 4 * 512).rearrange("p (a c) -> p a c", c=512)
        ivt = R_bf(24 * KB, 4 * 512).rearrange("p (a c) -> p a c", c=512)
        ktil_b = [Buf(f"ktil{i}") for i in range(4)]
        khat_b = [Buf(f"khat{i}") for i in range(4)]
        ivt_b = [Buf(f"ivt{i}") for i in range(4)]
        E_T = R_f32(28 * KB, 4 * 512).rearrange("p (a c) -> p a c", c=512)
        E_b = [Buf(f"E{i}") for i in range(4)]
        qtil = R_bf(36 * KB, 4 * 512).rearrange("p (a c) -> p a c", c=512)
        ktilT = R_bf(40 * KB, 4 * 512).rearrange("p (a c) -> p a c", c=512)
        sgt = R_bf(44 * KB, 4 * 512).rearrange("p (a c) -> p a c", c=512)
        qtil_b = [Buf(f"qtil{i}") for i in range(4)]
        ktilT_b = [Buf(f"ktilT{i}") for i in range(4)]
        sgt_b = [Buf(f"sg{i}") for i in range(4)]
        og = R_bf(48 * KB, 16 * 512).rearrange("p (a c) -> p a c", c=512)
        og_b = [Buf(f"og{i}") for i in range(16)]
        Pblk = [R_bf(64 * KB + i * 256, 128) for i in range(4)]
        Pblk_b = [Buf(f"P{i}") for i in range(4)]
        hg_bufs = ([b for row in hA_b for b in row] + ktil_b + khat_b + ivt_b + E_b + qtil_b + ktilT_b
                   + sgt_b + og_b + Pblk_b)
        lb_t = [sb(f"lb{i}", [128, 512], F32) for i in range(1)]
        lb_b = [Buf(f"lb{i}") for i in range(1)]
        Sb_t = sb("Sb_t", [128, 4, 128], F32)
        Sb_b = [Buf(f"Sb{i}") for i in range(4)]
        Sbm = R_bf(65 * 1024, 4 * 128).rearrange("p (a c) -> p a c", c=128)
        Sbm_b = [Buf(f"Sbm{i}") for i in range(4)]
        hg_bufs += Sbm_b
        rs_t = [sb(f"rs{i}", [128, TT], F32) for i in range(2)]
        rs_b = [Buf(f"rs{i}") for i in range(2)]
        S_t = [sb(f"S{l}", [128, NH, 128], F32) for l in range(NAL)]
        S_b = [[Buf(f"S{l}_{hd}") for hd in range(NH)] for l in range(NAL)]
        Sbf = sb("Sbf", [128, NH, 128], BF16)
        Sbf_b = [Buf(f"Sbf{hd}") for hd in range(NH)]
        for l in range(NAL):
            p.op(pool, lambda l=l: nc.gpsimd.memset(S_t[l][:], 0.0), writes=S_b[l])

    def hgrn_block(l):
        p.merge(a.all(), hg_bufs)
        p.op(act, lambda: nc.scalar.copy(out=Sbf[:], in_=S_t[l][:]), reads=S_b[l], writes=Sbf_b)
        norm_mod(A1m[l], modv[l][:, 0:16], b_A[l], b_modv[l])
        key = f"win_{l}"
        for g in range(4):
            li = nxt("lb", 1)
            p.dma(sp, lb_t[li][:], lbs[l][:, g * 512:(g + 1) * 512], reads=[b_lbs], writes=[lb_b[li]])
            slabs = {}

            def Zproj(tb):
                if "z" not in slabs:
                    slabs["z"] = load_slab(key, 4 + g)
                zv, zb = slabs["z"]
                r = tb % 2
                a1, a2 = hA[0][r], hA[1][r]
                b1, b2 = hA_b[0][r], hA_b[1][r]
                bk = balloc()
                for k in range(DC):
                    p.op(pe, lambda: nc.tensor.matmul(
                        banks[bk][:], lhsT=h_t[:, k, tb * 128:(tb + 1) * 128], rhs=zv[:, k, :],
                        start=(k == 0), stop=(k == DC - 1)), reads=[h.b(k), zb], writes=[bank_b[bk]])
                p.op(act, lambda: nc.scalar.activation(out=a1, in_=banks[bk][:], func=AF.Sigmoid),
                     reads=[bank_b[bk]], writes=[b1])
                bfree(bk)
                p.op(dve, lambda: nc.vector.scalar_tensor_tensor(
                    out=a2, in0=a1, scalar=-1.0, in1=lb_t[li][:], op0=ALU.add, op1=ALU.mult),
                    reads=[b1, lb_b[li]], writes=[b2])
                p.op(dve, lambda: nc.vector.tensor_tensor(out=a1, in0=a1, in1=a2, op=ALU.subtract),
                     reads=[b1, b2], writes=[b1])
                p.op(act, lambda: nc.scalar.activation(out=a2, in_=a1, func=AF.Ln), reads=[b1], writes=[b2])
                p.op(pool, lambda: nc.gpsimd.tensor_scalar(out=a1, in0=a1, scalar1=-1.0, scalar2=1.0,
                                                           op0=ALU.mult, op1=ALU.add), reads=[b1], writes=[b1])

            def cum(tb):
                r = tb % 2
                a1, a2, a3, a4 = (hA[j][r] for j in range(4))
                b1, b2, b3, b4 = (hA_b[j][r] for j in range(4))
                bk = balloc()
                p.op(pe, lambda: nc.tensor.matmul(banks[bk][:], lhsT=Uinc, rhs=a2, start=True, stop=True),
                     reads=[b_cst, b2], writes=[bank_b[bk]])
                p.op(act, lambda: nc.scalar.activation(out=a3, in_=banks[bk][:], func=AF.Exp, scale=-1.0),
                     reads=[bank_b[bk]], writes=[b3])
                bfree(bk)
                bk = balloc()
                p.op(pe, lambda: nc.tensor.matmul(banks[bk][:], lhsT=Vsuf, rhs=a2, start=True, stop=True),
                     reads=[b_cst, b2], writes=[bank_b[bk]])
                p.op(act, lambda: nc.scalar.activation(out=a4, in_=banks[bk][:], func=AF.Exp),
                     reads=[bank_b[bk]], writes=[b4])
                bfree(bk)
                bk = balloc()
                for hh in range(4):
                    p.op(pe, lambda: nc.tensor.matmul(
                        banks[bk][:, hh * 128:(hh + 1) * 128], lhsT=a2[:, hh * 128:(hh + 1) * 128], rhs=Uinc,
                        start=True, stop=True), reads=[b_cst, b2], writes=[bank_b[bk]])
                p.op(act, lambda: nc.scalar.activation(
                    out=E_T[:, :, tb * 128:(tb + 1) * 128],
                    in_=banks[bk][:].rearrange("p (a c) -> p a c", c=128), func=AF.Exp),
                    reads=[bank_b[bk]], writes=E_b)
                bfree(bk)
                p.op(dve, lambda: nc.vector.tensor_tensor(out=ktil[:, tb, :], in0=a1, in1=a3, op=ALU.mult),
                     reads=[b1, b3], writes=[ktil_b[tb]])
                p.op(pool, lambda: nc.gpsimd.tensor_tensor(out=khat[:, tb, :], in0=a1, in1=a4, op=ALU.mult),
                     reads=[b1, b4], writes=[khat_b[tb]])

            def Iproj(tb):
                if "i" not in slabs:
                    slabs["i"] = load_slab(key, 8 + g)
                iv_, ib = slabs["i"]
                bk = balloc()
                for k in range(DC):
                    p.op(pe, lambda: nc.tensor.matmul(
                        banks[bk][:], lhsT=h_t[:, k, tb * 128:(tb + 1) * 128], rhs=iv_[:, k, :],
                        start=(k == 0), stop=(k == DC - 1)), reads=[h.b(k), ib], writes=[bank_b[bk]])
                p.op(act, lambda: nc.scalar.copy(out=ivt[:, tb, :], in_=banks[bk][:]),
                     reads=[bank_b[bk]], writes=[ivt_b[tb]])
                bfree(bk)

            def Gproj(hh):
                if "g" not in slabs:
                    slabs["g"] = load_slab(key, 12 + g)
                gv, gb = slabs["g"]
                bk = balloc()
                for k in range(DC):
                    p.op(pe, lambda: nc.tensor.matmul(
                        banks[bk][:], lhsT=gv[:, k, hh * 128:(hh + 1) * 128], rhs=h_t[:, k, :],
                        start=(k == 0), stop=(k == DC - 1)), reads=[h.b(k), gb], writes=[bank_b[bk]])
                p.op(act, lambda: nc.scalar.activation(out=sgt[:, hh, :], in_=banks[bk][:], func=AF.Silu),
                     reads=[bank_b[bk]], writes=[sgt_b[hh]])
                bfree(bk)

            def Qproj(hh):
                if "q" not in slabs:
                    slabs["q"] = load_slab(key, g)
                qv, qb = slabs["q"]
                bk = balloc()
                for k in range(DC):
                    p.op(pe, lambda: nc.tensor.matmul(
                        banks[bk][:], lhsT=qv[:, k, hh * 128:(hh + 1) * 128], rhs=h_t[:, k, :],
                        start=(k == 0), stop=(k == DC - 1)), reads=[h.b(k), qb], writes=[bank_b[bk]])
                p.op(dve, lambda: nc.vector.scalar_tensor_tensor(
                    out=qtil[:, hh, :], in0=banks[bk][:], scalar=128 ** -0.5, in1=E_T[:, hh, :],
                    op0=ALU.mult, op1=ALU.mult), reads=[bank_b[bk], E_b[hh]], writes=[qtil_b[hh]])
                bfree(bk)

            def Ttrans(hh):
                bk = balloc()
                bv = banks[bk][:].bitcast(BF16)
                for tb in range(4):
                    p.op(pe, lambda: nc.tensor.transpose(
                        out=bv[:, tb * 128:(tb + 1) * 128], in_=ktil[:, tb, hh * 128:(hh + 1) * 128],
                        identity=ident_bf[:]), reads=[ktil_b[tb], b_ident], writes=[bank_b[bk]])
                p.op(dve, lambda: nc.vector.tensor_copy(out=ktilT[:, hh, :], in_=bv[:, 0:512]),
                     reads=[bank_b[bk]], writes=[ktilT_b[hh]])
                bfree(bk)

            Zproj(0); Zproj(1); Iproj(0); cum(0)
            Zproj(2); Iproj(1); cum(1)
            Zproj(3); Iproj(2); cum(2)
            Gproj(0); Iproj(3); cum(3)
            Gproj(1); Gproj(2); Gproj(3)
            for hh in range(4):
                Qproj(hh)
            for hh in range(4):
                Ttrans(hh)
            obk = [balloc() for _ in range(4)]
            for tb in range(4):
                sbk = balloc()
                for hh in range(4):
                    p.op(pe, lambda: nc.tensor.matmul(
                        banks[sbk][:, hh * 128:(hh + 1) * 128], lhsT=ktilT[:, hh, tb * 128:(tb + 1) * 128],
                        rhs=qtil[:, hh, tb * 128:(tb + 1) * 128], start=True, stop=True),
                        reads=[ktilT_b[hh], qtil_b[hh]], writes=[bank_b[sbk]])
                ubk = [balloc(), balloc()]
                for hh in range(4):
                    for hf in range(2):
                        uc = hh * 128
                        p.op(pe, lambda: nc.tensor.matmul(
                            banks[ubk[hf]][:, uc:uc + 128],
                            lhsT=khat[hf * 64:(hf + 1) * 64, tb, hh * 128:(hh + 1) * 128],
                            rhs=ivt[hf * 64:(hf + 1) * 64, tb, hh * 128:(hh + 1) * 128], start=True, stop=True),
                            reads=[khat_b[tb], ivt_b[tb]], writes=[bank_b[ubk[hf]]])
                for hh in range(4):
                    p.op(dve, lambda: nc.vector.tensor_tensor(out=Pblk[hh], in0=banks[sbk][:, hh * 128:(hh + 1) * 128],
                                                              in1=Uinc, op=ALU.mult),
                         reads=[bank_b[sbk], b_cst], writes=[Pblk_b[hh]])
                bfree(sbk)
                for hh in range(4):
                    head = 4 * g + hh
                    c0 = tb * 128
                    p.op(pe, lambda: nc.tensor.matmul(
                        banks[obk[hh]][:, c0:c0 + 128], lhsT=ivt[:, tb, hh * 128:(hh + 1) * 128],
                        rhs=Pblk[hh], start=True, stop=False),
                        reads=[ivt_b[tb], Pblk_b[hh]], writes=[bank_b[obk[hh]]])
                    p.op(pe, lambda: nc.tensor.matmul(
                        banks[obk[hh]][:, c0:c0 + 64], lhsT=Sbf[:, head, :], rhs=qtil[:, hh, c0:c0 + 64],
                        start=False, stop=False), reads=[Sbf_b[head], qtil_b[hh]], writes=[bank_b[obk[hh]]])
                for hh in range(4):
                    head = 4 * g + hh
                    c0 = tb * 128
                    uc = hh * 128
                    p.op(dve, lambda: nc.vector.scalar_tensor_tensor(
                        out=Sb_t[:, hh, :], in0=S_t[l][:, head, :], scalar=E_T[:, hh, c0 + 63:c0 + 64],
                        in1=banks[ubk[0]][:, uc:uc + 128], op0=ALU.mult, op1=ALU.add),
                        reads=[S_b[l][head], E_b[hh], bank_b[ubk[0]]], writes=[Sb_b[hh]])
                    p.op(act, lambda: nc.scalar.copy(out=Sbm[:, hh, :], in_=Sb_t[:, hh, :]),
                         reads=[Sb_b[hh]], writes=[Sbm_b[hh]])
                    p.op(dve, lambda: nc.vector.scalar_tensor_tensor(
                        out=S_t[l][:, head, :], in0=Sb_t[:, hh, :], scalar=E_T[:, hh, c0 + 127:c0 + 128],
                        in1=banks[ubk[1]][:, uc:uc + 128], op0=ALU.mult, op1=ALU.add),
                        reads=[Sb_b[hh], E_b[hh], bank_b[ubk[1]]], writes=[S_b[l][head]])
                    p.op(act, lambda: nc.scalar.copy(out=Sbf[:, head, :], in_=S_t[l][:, head, :]),
                         reads=[S_b[l][head]], writes=[Sbf_b[head]])
                bfree(ubk[0])
                bfree(ubk[1])
                for hh in range(4):
                    c0 = tb * 128 + 64
                    p.op(pe, lambda: nc.tensor.matmul(
                        banks[obk[hh]][:, c0:c0 + 64], lhsT=Sbm[:, hh, :], rhs=qtil[:, hh, c0:c0 + 64],
                        start=False, stop=True), reads=[Sbm_b[hh], qtil_b[hh]], writes=[bank_b[obk[hh]]])
            for hh in range(4):
                head = 4 * g + hh
                ri = nxt("rs", 2)
                rms_stats([(banks[obk[hh]][:], bank_b[obk[hh]], 128)], 128, rs_t[ri][:], rs_b[ri])
                ti = nxt("tmp", NTMP)
                p.op(dve, lambda: nc.vector.scalar_tensor_tensor(
                    out=tmp_t[ti][:], in0=banks[obk[hh]][:], scalar=ogain[:, l:l + 1], in1=rs_t[ri][:],
                    op0=ALU.mult, op1=ALU.mult), reads=[bank_b[obk[hh]], b_ogain, rs_b[ri]], writes=[tmp_b[ti]])
                bfree(obk[hh])
                p.op(pool, lambda: nc.gpsimd.tensor_tensor(out=og[:, head, :], in0=tmp_t[ti][:], in1=sgt[:, hh, :], op=ALU.mult),
                     reads=[tmp_b[ti], sgt_b[hh]], writes=[og_b[head]])
        proj_epilogue(f"wout_{l}", l, og, og_b, 32)
        p.merge(hg_bufs, a.all())

    if do_mla:
        KB = 1024
        SM_SCALE = QK ** -0.5
        ropec = sb("ropec", [128, 130], F32)
        b_ropec = Buf("ropec")
        p.dma(sp, ropec[:], rope_c, writes=[b_ropec])
        Rm = ropec[0:64, 0:64]
        mg = sb("mla_gains", [128, 4 + 3 + 4 * NBL + 3 * NBL], F32)
        b_mg = Buf("mg")
        p.dma(sp, mg[:, 0:4], kv_norm, writes=[b_mg])
        p.dma(sp, mg[:, 4:7], k_norm, writes=[b_mg])
        for j in range(NBL):
            p.dma(sp, mg[:, 7 + 7 * j:11 + 7 * j], q_lat_norm[j], writes=[b_mg])
            p.dma(sp, mg[:, 11 + 7 * j:14 + 7 * j], q_norm[j], writes=[b_mg])
        cs_t = sb("cossin", [64, 2, TT], F32)
        b_cs = Buf("cossin")
        negpi = sb("negpi", [128, 1], F32)
        utri_bf = sb("utri_bf", [128, 128], BF16)
        b_mc = Buf("mla_consts")
        p.op(dve, lambda: nc.vector.memset(negpi[:], -math.pi), writes=[b_mc])
        p.op(dve, lambda: nc.vector.tensor_copy(out=utri_bf[:], in_=cst[:, 1, :]), reads=[b_cst, b_mc], writes=[b_mc])
        p.op(dve, lambda: nc.vector.tensor_copy(out=utri_bf[0:64, 64:128], in_=cst[0:64, 0, 64:128]),
             reads=[b_cst, b_mc], writes=[b_mc])
        KN = dram_tmp("KN", [NH, 128, S], BF16)
        KR = dram_tmp("KR", [NH, 64, S], BF16)
        VV = dram_tmp("VV", [S // 128, 128, NH * 128], BF16)
        kvd_b = [[[Buf(f"kv{t}_{hp}_{i}") for i in range(3)] for hp in range(8)] for t in range(NT)]
        ao = R_bf(0, 16 * 512).rearrange("p (a c) -> p a c", c=512)
        ao_b = [Buf(f"ao{i}") for i in range(16)]
        latn = R_bf(16 * KB, 4 * 512).rearrange("p (a c) -> p a c", c=512)
        latn_b = [Buf(f"latn{i}") for i in range(4)]
        qn2 = [R_bf(20 * KB + r * 2 * KB, 2 * 512).rearrange("p (a c) -> p a c", c=512) for r in range(2)]
        qr2 = [R_bf(24 * KB + r * 2 * KB, 2 * 512).rearrange("p (a c) -> p a c", c=512) for r in range(2)]
        q2_b = [Buf(f"q2_{r}") for r in range(2)]
        kvc = [[R_bf(base + i * 2 * KB, 1024) for i in range(3)] for base in (28 * KB, 34 * KB, 60 * KB)]
        kvc_b = [[Buf(f"kvc{r}_{i}") for i in range(3)] for r in range(3)]
        NPT = 4
        ATT_DEPTH = cfg.get("att_depth", 2)
        Pt = [R_bf(40 * KB + i * KB, 512) for i in range(NPT)]
        Pt_b = [Buf(f"Pt{i}") for i in range(NPT)]
        csg = R_f32(44 * KB, 2 * 512).rearrange("p (a c) -> p a c", c=512)
        b_csg = Buf("csg")
        kr0 = R_f32(48 * KB, 512)
        b_kr0 = Buf("kr0")
        sqpe = R_bf(50 * KB, 512)
        b_sqpe = Buf("sqpe")
        rawr = [R_f32(52 * KB + i * 2 * KB, 512) for i in range(2)]
        rawr_b = [Buf(f"rawr{i}") for i in range(2)]
        rsh = [R_f32(56 * KB + i * 2 * KB, 512) for i in range(2)]
        rsh_b = [Buf(f"rsh{i}") for i in range(2)]
        mla_bufs = (ao_b + latn_b + q2_b + [b for row in kvc_b for b in row] + Pt_b
                    + [b_csg, b_kr0, b_sqpe] + rawr_b + rsh_b)

    def rope_tables(t):
        ti = nxt("tmp", NTMP)
        pos_i = tmp_t[ti][0:64, :].bitcast(I32)
        p.dma(sp, pos_i, positions[t * TT:(t + 1) * TT].partition_broadcast(64), writes=[tmp_b[ti]])
        ai = nxt("tmp", NTMP)
        ang = tmp_t[ai][0:64, :]
        p.op(dve, lambda: nc.vector.tensor_copy(out=ang, in_=pos_i), reads=[tmp_b[ti]], writes=[tmp_b[ai]])
        p.op(dve, lambda: nc.vector.tensor_scalar(out=ang, in0=ang, scalar1=ropec[0:64, 64:65], scalar2=None,
                                                  op0=ALU.mult), reads=[tmp_b[ai], b_ropec], writes=[tmp_b[ai]])
        for which, phase in ((0, 0.25), (1, 0.0)):
            yi_ = nxt("tmp", NTMP)
            y = tmp_t[yi_][0:64, :]
            p.op(dve, lambda: nc.vector.tensor_scalar(out=y, in0=ang, scalar1=1.0 / (2 * math.pi), scalar2=phase + 0.5,
                                                      op0=ALU.mult, op1=ALU.add), reads=[tmp_b[ai]], writes=[tmp_b[yi_]])
            ni_ = nxt("tmp", NTMP)
            n_i = tmp_t[ni_][0:64, :].bitcast(I32)
            nf_ = nxt("tmp", NTMP)
            n_f = tmp_t[nf_][0:64, :]
            p.op(dve, lambda: nc.vector.tensor_copy(out=n_i, in_=y), reads=[tmp_b[yi_]], writes=[tmp_b[ni_]])
            p.op(dve, lambda: nc.vector.tensor_copy(out=n_f, in_=n_i), reads=[tmp_b[ni_]], writes=[tmp_b[nf_]])
            p.op(dve, lambda: nc.vector.tensor_tensor(out=y, in0=y, in1=n_f, op=ALU.subtract),
                 reads=[tmp_b[yi_], tmp_b[nf_]], writes=[tmp_b[yi_]])
            p.op(dve, lambda: nc.vector.scalar_tensor_tensor(out=y, in0=y, scalar=0.0, in1=y, op0=ALU.is_lt, op1=ALU.add),
                 reads=[tmp_b[yi_]], writes=[tmp_b[yi_]])
            p.op(dve, lambda: nc.vector.tensor_scalar(out=y, in0=y, scalar1=2 * math.pi, scalar2=-math.pi,
                                                      op0=ALU.mult, op1=ALU.add), reads=[tmp_b[yi_]], writes=[tmp_b[yi_]])
            p.op(dve, lambda: nc.vector.tensor_scalar(out=y, in0=y, scalar1=-3.1415925, scalar2=3.1415925,
                                                      op0=ALU.max, op1=ALU.min), reads=[tmp_b[yi_]], writes=[tmp_b[yi_]])
            p.op(act, lambda: nc.scalar.activation(out=cs_t[:, which, :], in_=y, func=AF.Sin),
                 reads=[tmp_b[yi_]], writes=[b_cs])

    def fold_gain(gcol):
        p.op(dve, lambda: nc.vector.tensor_scalar(out=csg[0:64, 0, :], in0=cs_t[:, 0, :], scalar1=mg[0:64, gcol + 1:gcol + 2],
                                                  scalar2=None, op0=ALU.mult), reads=[b_cs, b_mg], writes=[b_csg])
        p.op(dve, lambda: nc.vector.tensor_scalar(out=csg[0:64, 1, :], in0=cs_t[:, 1, :], scalar1=mg[0:64, gcol + 2:gcol + 3],
                                                  scalar2=None, op0=ALU.mult), reads=[b_cs, b_mg, b_csg], writes=[b_csg])

    def roped(raw_bank, out_ap, out_reads, out_writes, rstd_ap, rstd_b):
        ri = nxt("rawr", 2)
        p.op(act, lambda: nc.scalar.copy(out=rawr[ri][0:64, :], in_=banks[raw_bank][0:64, :]),
             reads=[bank_b[raw_bank]], writes=[rawr_b[ri]])
        rb = balloc()
        p.op(pe, lambda: nc.tensor.matmul(banks[rb][0:64, :], lhsT=Rm, rhs=rawr[ri][0:64, :], start=True, stop=True),
             reads=[rawr_b[ri], b_ropec], writes=[bank_b[rb]])
        t1 = nxt("tmp", NTMP)
        p.op(dve, lambda: nc.vector.tensor_tensor(out=tmp_t[t1][0:64, :], in0=banks[rb][0:64, :], in1=csg[0:64, 1, :], op=ALU.mult),
             reads=[bank_b[rb], b_csg], writes=[tmp_b[t1]])
        bfree(rb)
        p.op(pool, lambda: nc.gpsimd.tensor_tensor(out=rawr[ri][0:64, :], in0=rawr[ri][0:64, :], in1=csg[0:64, 0, :], op=ALU.mult),
             reads=[rawr_b[ri], b_csg], writes=[rawr_b[ri]])
        p.op(dve, lambda: nc.vector.tensor_tensor(out=tmp_t[t1][0:64, :], in0=tmp_t[t1][0:64, :], in1=rawr[ri][0:64, :], op=ALU.add),
             reads=[tmp_b[t1], rawr_b[ri]], writes=[tmp_b[t1]])
        if rstd_ap is None:
            p.op(act, lambda: nc.scalar.copy(out=out_ap, in_=tmp_t[t1][0:64, :]), reads=[tmp_b[t1]] + out_reads, writes=out_writes)
        else:
            p.op(dve, lambda: nc.vector.tensor_tensor(out=out_ap, in0=tmp_t[t1][0:64, :], in1=rstd_ap, op=ALU.mult),
                 reads=[tmp_b[t1], rstd_b] + out_reads, writes=out_writes)

    def lat_proj(key, gcol0):
        wv, wb = load_slab(key, 0)
        bks = []
        for cc in range(4):
            bk = balloc()
            bks.append(bk)
            for k in range(DC):
                p.op(pe, lambda k=k, cc=cc, bk=bk: nc.tensor.matmul(
                    banks[bk][:], lhsT=wv[:, k, cc * 128:(cc + 1) * 128], rhs=h_t[:, k, :],
                    start=(k == 0), stop=(k == DC - 1)), reads=[wb, h.b(k)], writes=[bank_b[bk]])
        rms_stats([(banks[bk][:], bank_b[bk], 128) for bk in bks], 512, rstd_t[:], b_rstd)
        for cc in range(4):
            p.op(dve, lambda cc=cc: nc.vector.scalar_tensor_tensor(
                out=latn[:, cc, :], in0=banks[bks[cc]][:], scalar=mg[:, gcol0 + cc:gcol0 + cc + 1], in1=rstd_t[:],
                op0=ALU.mult, op1=ALU.mult), reads=[bank_b[bks[cc]], b_mg, b_rstd], writes=[latn_b[cc]])
            bfree(bks[cc])

    def head_stats(nope_bank, rope_sq_ap, rope_sq_b, ri):
        rms_stats([(banks[nope_bank][:], bank_b[nope_bank], 128), (rope_sq_ap, rope_sq_b, -64)], QK, rsh[ri], rsh_b[ri])

    def kv_block(t):
        p.merge(a.all(), mla_bufs)
        norm_mod(Akv, kvmodv[:, 0:16], b_kvmod, b_kvmod)
        lat_proj("wdkv_c", 0)
        wv, wb = load_slab("wdkv_r", 0)
        pb = balloc()
        for k in range(DC):
            p.op(pe, lambda k=k: nc.tensor.matmul(banks[pb][0:64, :], lhsT=wv[:, k, 0:64], rhs=h_t[:, k, :],
                                                  start=(k == 0), stop=(k == DC - 1)),
                 reads=[wb, h.b(k)], writes=[bank_b[pb]])
        p.op(act, lambda: nc.scalar.activation(out=sqpe[0:64, :], in_=banks[pb][0:64, :], func=AF.Square),
             reads=[bank_b[pb]], writes=[b_sqpe])
        fold_gain(4)
        roped(pb, kr0[0:64, :], [], [b_kr0], None, None)
        bfree(pb)
        uv = [load_slab("wukv", 0), None]
        for hp in range(8):
            if hp == 4:
                uv[1] = load_slab("wukv", 1)
            wv, wb = uv[hp // 4]
            r = nxt("kvc", 3)
            kn2s = kvc[r][0].rearrange("p (a c) -> p a c", c=512)
            kr2s = kvc[r][1].rearrange("p (a c) -> p a c", c=512)
            v2s = kvc[r][2].rearrange("p (a c) -> p a c", c=256)
            for hh in range(2):
                hd = 2 * hp + hh
                c0 = (hd % 8) * 256
                bk = balloc()
                for kc in range(4):
                    p.op(pe, lambda kc=kc: nc.tensor.matmul(
                        banks[bk][:], lhsT=wv[:, kc, c0:c0 + 128], rhs=latn[:, kc, :],
                        start=(kc == 0), stop=(kc == 3)), reads=[wb, latn_b[kc]], writes=[bank_b[bk]])
                ri = nxt("rsh", 2)
                head_stats(bk, sqpe[0:64, :], b_sqpe, ri)
                p.op(dve, lambda: nc.vector.scalar_tensor_tensor(
                    out=kn2s[:, hh, :], in0=banks[bk][:], scalar=mg[:, 4:5], in1=rsh[ri],
                    op0=ALU.mult, op1=ALU.mult), reads=[bank_b[bk], b_mg, rsh_b[ri]], writes=[kvc_b[r][0]])
                bfree(bk)
                p.op(pool, lambda: nc.gpsimd.tensor_tensor(out=kr2s[0:64, hh, :], in0=kr0[0:64, :], in1=rsh[ri][0:64, :], op=ALU.mult),
                     reads=[b_kr0, rsh_b[ri]], writes=[kvc_b[r][1]])
            c0 = ((2 * hp) % 8) * 256
            for tb in range(4):
                bk = balloc()
                rhs_v = wv[:, :, c0:c0 + 512].rearrange("p k (h c) -> p k h c", c=256)
                for kc in range(4):
                    p.op(pe, lambda kc=kc: nc.tensor.matmul(
                        banks[bk][:, 0:256], lhsT=latn[:, kc, tb * 128:(tb + 1) * 128], rhs=rhs_v[:, kc, :, 128:256],
                        start=(kc == 0), stop=(kc == 3)), reads=[wb, latn_b[kc]], writes=[bank_b[bk]])
                p.op(act, lambda: nc.scalar.copy(out=v2s[:, tb, :], in_=banks[bk][:, 0:256]),
                     reads=[bank_b[bk]], writes=[kvc_b[r][2]])
                bfree(bk)
            p.dma(pool, KN[2 * hp:2 * hp + 2, :, t * TT:(t + 1) * TT].rearrange("a p s -> p a s"), kn2s,
                  reads=[kvc_b[r][0]], writes=[kvd_b[t][hp][0]])
            p.dma(pool, KR[2 * hp:2 * hp + 2, :, t * TT:(t + 1) * TT].rearrange("a p s -> p a s"), kr2s[0:64],
                  reads=[kvc_b[r][1]], writes=[kvd_b[t][hp][1]])
            p.dma(pool, VV[4 * t:4 * t + 4, :, hp * 256:(hp + 1) * 256].rearrange("a p c -> p a c"), v2s,
                  reads=[kvc_b[r][2]], writes=[kvd_b[t][hp][2]])
        p.merge(mla_bufs, a.all())

    def mla_block(l, t):
        j = l - NA
        g0 = 7 + 7 * j
        p.merge(a.all(), mla_bufs)
        for r_ in range(3):
            p.op(pool, lambda: nc.gpsimd.memset(kvc[r_][1][64:128, :], 0.0), writes=[kvc_b[r_][1]])
        for r_ in range(2):
            p.op(pool, lambda: nc.gpsimd.memset(qr2[r_][64:128, :, :], 0.0), writes=[q2_b[r_]])
        norm_mod(A1m[l], modv[l][:, 0:16], b_A[l], b_modv[l])
        lat_proj(f"wdq_{l}", g0)
        fold_gain(g0 + 4)
        uq = [load_slab(f"wuq_{l}", 0), None]

        def load_chunk(hp, jt):
            r = nxt("kvc", 3)
            kn2c = kvc[r][0].rearrange("p (a c) -> p a c", c=512)
            kr2c = kvc[r][1].rearrange("p (a c) -> p a c", c=512)
            v2c = kvc[r][2].rearrange("p (a c) -> p a c", c=256)
            p.dma(sp, kn2c, KN[2 * hp:2 * hp + 2, :, jt * TT:(jt + 1) * TT].rearrange("a p s -> p a s"),
                  reads=[kvd_b[jt][hp][0]], writes=[kvc_b[r][0]])
            p.dma(sp, kr2c[0:64], KR[2 * hp:2 * hp + 2, :, jt * TT:(jt + 1) * TT].rearrange("a p s -> p a s"),
                  reads=[kvd_b[jt][hp][1]], writes=[kvc_b[r][1]])
            p.dma(sp, v2c, VV[4 * jt:4 * jt + 4, :, hp * 256:(hp + 1) * 256].rearrange("a p c -> p a c"),
                  reads=[kvd_b[jt][hp][2]], writes=[kvc_b[r][2]])
            return (r, kn2c, kr2c, v2c)

        def qproj(hp):
            if hp == 4:
                uq[1] = load_slab(f"wuq_{l}", 1)
            wv, wb = uq[hp // 4]
            qi = nxt("q2", 2)
            for hh in range(2):
                hd = 2 * hp + hh
                c0 = (hd % 8) * 192
                bn = balloc()
                for kc in range(4):
                    p.op(pe, lambda kc=kc: nc.tensor.matmul(
                        banks[bn][:], lhsT=wv[:, kc, c0:c0 + 128], rhs=latn[:, kc, :],
                        start=(kc == 0), stop=(kc == 3)), reads=[wb, latn_b[kc]], writes=[bank_b[bn]])
                br = balloc()
                for kc in range(4):
                    p.op(pe, lambda kc=kc: nc.tensor.matmul(
                        banks[br][0:64, :], lhsT=wv[:, kc, c0 + 128:c0 + 192], rhs=latn[:, kc, :],
                        start=(kc == 0), stop=(kc == 3)), reads=[wb, latn_b[kc]], writes=[bank_b[br]])
                ri = nxt("rsh", 2)
                rms_stats([(banks[bn][:], bank_b[bn], 128), (banks[br][0:64, :], bank_b[br], 64)], QK, rsh[ri], rsh_b[ri])
                p.op(dve, lambda: nc.vector.scalar_tensor_tensor(
                    out=qn2[qi][:, hh, :], in0=banks[bn][:], scalar=mg[:, g0 + 4:g0 + 5], in1=rsh[ri],
                    op0=ALU.mult, op1=ALU.mult), reads=[bank_b[bn], b_mg, rsh_b[ri]], writes=[q2_b[qi]])
                bfree(bn)
                roped(br, qr2[qi][0:64, hh, :], [], [q2_b[qi]], rsh[ri][0:64, :], rsh_b[ri])
                bfree(br)
            return qi

        def att(hp, qi):
            ch = {0: load_chunk(hp, 0)}
            obk = [balloc() for _ in range(2)]
            dbk = [balloc() for _ in range(2)]
            pend = []

            def flush_one():
                (pi, hh, kb, q0, first, last, v2c, rv) = pend.pop(0)
                p.op(pe, lambda: nc.tensor.matmul(
                    banks[obk[hh]][:, q0:TT], lhsT=v2c[:, kb, hh * 128:(hh + 1) * 128], rhs=Pt[pi][:, q0:TT],
                    start=first, stop=last), reads=[kvc_b[rv][2], Pt_b[pi]], writes=[bank_b[obk[hh]]])
                p.op(pe, lambda: nc.tensor.matmul(
                    banks[dbk[hh]][:, q0:TT], lhsT=ones_bf[:], rhs=Pt[pi][:, q0:TT],
                    start=first, stop=last), reads=[b_ident, Pt_b[pi]], writes=[bank_b[dbk[hh]]])

            for jt in range(t + 1):
                if jt + 1 <= t:
                    ch[jt + 1] = load_chunk(hp, jt + 1)
                r, kn2c, kr2c, v2c = ch[jt]
                for hh in range(2):
                    for kb in range(4):
                        diag = (jt == t)
                        q0 = kb * 128 if diag else 0
                        first = (jt == 0 and kb == 0)
                        last = (jt == t and kb == 3)
                        sbk = balloc()
                        p.op(pe, lambda: nc.tensor.matmul(
                            banks[sbk][:, q0:TT], lhsT=kn2c[:, hh, kb * 128:(kb + 1) * 128], rhs=qn2[qi][:, hh, q0:TT],
                            start=True, stop=False), reads=[kvc_b[r][0], q2_b[qi]], writes=[bank_b[sbk]])
                        p.op(pe, lambda: nc.tensor.matmul(
                            banks[sbk][:, q0:TT], lhsT=kr2c[:, hh, kb * 128:(kb + 1) * 128], rhs=qr2[qi][:, hh, q0:TT],
                            start=False, stop=True), reads=[kvc_b[r][1], q2_b[qi]], writes=[bank_b[sbk]])
                        pi = nxt("Pt", NPT)
                        p.op(act, lambda: nc.scalar.activation(out=Pt[pi][:, q0:TT], in_=banks[sbk][:, q0:TT],
                                                               func=AF.Exp, scale=SM_SCALE),
                             reads=[bank_b[sbk]], writes=[Pt_b[pi]])
                        bfree(sbk)
                        if diag:
                            p.op(pool, lambda: nc.gpsimd.tensor_tensor(out=Pt[pi][:, q0:q0 + 128], in0=Pt[pi][:, q0:q0 + 128],
                                                                       in1=utri_bf[:], op=ALU.mult),
                                 reads=[Pt_b[pi], b_mc], writes=[Pt_b[pi]])
                        pend.append((pi, hh, kb, q0, first, last, v2c, r))
                        if len(pend) > ATT_DEPTH:
                            flush_one()
            while pend:
                flush_one()
            for hh in range(2):
                hd = 2 * hp + hh
                ti = nxt("tmp", NTMP)
                p.op(act, lambda: nc.scalar.activation(out=tmp_t[ti][:], in_=banks[dbk[hh]][:], func=AF.Ln),
                     reads=[bank_b[dbk[hh]]], writes=[tmp_b[ti]])
                bfree(dbk[hh])
                p.op(act, lambda: nc.scalar.activation(out=tmp_t[ti][:], in_=tmp_t[ti][:], func=AF.Exp, scale=-1.0),
                     reads=[tmp_b[ti]], writes=[tmp_b[ti]])
                p.op(dve, lambda: nc.vector.tensor_tensor(out=ao[:, hd, :], in0=banks[obk[hh]][:], in1=tmp_t[ti][:], op=ALU.mult),
                     reads=[bank_b[obk[hh]], tmp_b[ti]], writes=[ao_b[hd]])
                bfree(obk[hh])

        qi_cur = qproj(0)
        for hp in range(8):
            qi_next = qproj(hp + 1) if hp + 1 < 8 else None
            att(hp, qi_cur)
            qi_cur = qi_next
        proj_epilogue(f"wo_{l}", l, ao, ao_b, 32)
        p.merge(mla_bufs, a.all())

    out_bufs = []
    ada_block(0)
    src0 = xT[:, 0:TT].rearrange("(c p) s -> p c s", p=128)
    p.dma(sp, x_t[:], src0, writes=x.all())
    for t in range(NT):
        if do_mla:
            rope_tables(t)

        def chunk_done(dc, t=t):
            ob = Buf(f"out{t}_{dc}")
            p.dma(act, outT[dc * 128:(dc + 1) * 128, t * TT:(t + 1) * TT], x_t[:, dc, :], reads=[x.b(dc)], writes=[ob])
            out_bufs.append(ob)
            if t + 1 < NT:
                p.dma(act, x_t[:, dc, :], xT[dc * 128:(dc + 1) * 128, (t + 1) * TT:(t + 2) * TT], writes=[x.b(dc)])

        for l in range(n_layers):
            last = (l == n_layers - 1)
            if do_mla and l == NA:
                kv_block(t)
            if do_mix and l < NAL:
                hgrn_block(l)
            if do_mla and l >= NA:
                mla_block(l, t)
            if t == 0 and l + 1 < n_layers:
                ada_block(l + 1)
            if do_mlp:
                mlp_block(l, chunk_done if last else None)
            if t == 0 and do_mla and l == 0:
                ada_block(n_layers)
        if not do_mlp:
            for dc in range(DC):
                chunk_done(dc)
    p.final_wait(sp, out_bufs)
    es.close()
    return nc, p


def _pj(v):
    v = np.asarray(v)
    n = v.shape[-1] // 128
    return np.ascontiguousarray(np.swapaxes(v.reshape(v.shape[:-1] + (n, 128)), -1, -2))


def _consts():
    i = np.arange(128)
    same = (i[:, None] // 64) == (i[None, :] // 64)
    ones = np.ones((128, 128), np.float32)
    uinc = (same & (i[:, None] <= i[None, :])).astype(np.float32)
    vsuf = (same & (i[:, None] > i[None, :])).astype(np.float32)
    ident = np.eye(128, dtype=np.float32)
    return np.stack([ones, uinc, vsuf, ident]).astype(np.float32)


def _qk_gain(g):
    g = np.asarray(g, np.float32)
    out = np.ones((128, 3), np.float32)
    out[:, 0] = g[0:128]
    out[0:64, 1] = g[128:192]
    out[0:64, 2] = np.concatenate([g[160:192], g[128:160]])
    return out


def _rope_consts():
    out = np.zeros((128, 130), np.float32)
    for m in range(32):
        out[m + 32, m] = -1.0
        out[m, m + 32] = 1.0
    inv = (1.0 / (np.float32(10000.0) ** (np.arange(0, 64, 2, dtype=np.float32) / np.float32(64)))).astype(np.float32)
    out[0:32, 64] = inv
    out[32:64, 64] = inv
    return out


def prep_core_inputs(inp, b, S, cfg, names=None):
    NL = cfg.get("n_layers", 4)
    NA = cfg.get("n_a", 2)
    gen = {
        "xT": lambda: np.ascontiguousarray(inp["x"][b, :S].T),
        "c_pj": lambda: _pj(inp["c"][b]),
        "ada_w": lambda: np.ascontiguousarray(inp["ada_w"][:NL]),
        "ada_b": lambda: _pj(inp["ada_b"][:NL]),
        "norm_mix": lambda: _pj(inp["norm_mix"][:NL]),
        "norm_mlp": lambda: _pj(inp["norm_mlp"][:NL]),
        "mlp_w1": lambda: np.ascontiguousarray(inp["mlp_w1"][:NL]),
        "mlp_w2": lambda: np.ascontiguousarray(inp["mlp_w2"][:NL]),
        "consts": lambda: _consts(),
        "kv_ada_w": lambda: np.ascontiguousarray(inp["kv_ada_w"]),
        "kv_ada_b": lambda: _pj(inp["kv_ada_b"]),
        "kv_in_norm": lambda: _pj(inp["kv_in_norm"]),
        "mla_w_dkv": lambda: np.ascontiguousarray(inp["mla_w_dkv"]),
        "mla_kv_norm": lambda: _pj(inp["mla_kv_norm"]),
        "mla_w_ukv": lambda: np.ascontiguousarray(inp["mla_w_ukv"]),
        "mla_k_norm": lambda: _qk_gain(inp["mla_k_norm"]),
        "mla_w_dq": lambda: np.ascontiguousarray(inp["mla_w_dq"][:NL - NA]),
        "mla_q_lat_norm": lambda: _pj(inp["mla_q_lat_norm"][:NL - NA]),
        "mla_w_uq": lambda: np.ascontiguousarray(inp["mla_w_uq"][:NL - NA]),
        "mla_q_norm": lambda: np.stack([_qk_gain(inp["mla_q_norm"][j]) for j in range(NL - NA)]),
        "mla_w_o": lambda: np.ascontiguousarray(inp["mla_w_o"][:NL - NA]),
        "positions": lambda: np.ascontiguousarray(inp["positions"][b, :S]).astype(np.int32),
        "rope_c": lambda: _rope_consts(),
        "hg_w_in": lambda: np.ascontiguousarray(inp["hg_w_in"][:min(NL, 2)]),
        "hg_lower": lambda: np.ascontiguousarray(inp["hg_lower"]),
        "hg_o_norm": lambda: np.ascontiguousarray(inp["hg_o_norm"][:min(NL, 2)].reshape(-1, 128, 1)),
        "hg_w_out": lambda: np.ascontiguousarray(inp["hg_w_out"][:min(NL, 2)]),
    }
    names = names or list(gen.keys())
    return {k: gen[k]() for k in names}


_CACHE = {}


def kernel(**inputs):
    S = 4096
    cfg = {}
    if "prog" not in _CACHE:
        _CACHE["prog"] = build_program(S, cfg)
    nc, p = _CACHE["prog"]
    inp = {k: np.asarray(v) for k, v in inputs.items()}
    in_maps = [prep_core_inputs(inp, b, S, cfg, p.in_names) for b in range(8)]
    res = run_bass_kernel_spmd(nc, in_maps, core_ids=list(range(8)))
    out = np.empty((8, S, D), np.float32)
    for b in range(8):
        out[b] = res.results[b]["outT"].T
    return out
```

```python
import math
from contextlib import ExitStack

import numpy as np
import concourse.bass as bass
import concourse.mybir as mybir
from concourse.bass_utils import run_bass_kernel_spmd

F32 = mybir.dt.float32
BF16 = mybir.dt.bfloat16
I32 = mybir.dt.int32
AF = mybir.ActivationFunctionType
ALU = mybir.AluOpType

D = 2048
DC = 16
TT = 512
DFF = 8192
FC = 64
NH = 16
EPS = 1e-6
QK = 192
SLAB_BYTES = 16384


class Sem:
    __slots__ = ("h", "val", "name")

    def __init__(self, h, name):
        self.h = h
        self.val = 0
        self.name = name


class Buf:
    __slots__ = ("w", "r", "name")

    def __init__(self, name=""):
        self.w = {}
        self.r = {}
        self.name = name


class Eng:
    def __init__(self, name, eng, sem, is_pe=False):
        self.name = name
        self.eng = eng
        self.sem = sem
        self.waited = {}
        self.is_pe = is_pe
        self.dma_sems = []
        self.dma_rr = 0


class Prog:
    def __init__(self, nc, es, n_dma_sems=20):
        self.nc = nc
        self.es = es
        mk = lambda n: Sem(es.enter_context(nc.semaphore(n)), n)
        self.pe = Eng("pe", nc.tensor, mk("s_pe"), is_pe=True)
        self.act = Eng("act", nc.scalar, mk("s_act"))
        self.dve = Eng("dve", nc.vector, mk("s_dve"))
        self.pool = Eng("pool", nc.gpsimd, mk("s_pool"))
        self.sp = Eng("sp", nc.sync, mk("s_sp"))
        for q in (self.sp, self.pool, self.act):
            q.dma_sems = [mk(f"d_{q.name}{i}") for i in range(n_dma_sems)]
        self.n_inst = 0

    def _wait(self, E, sem, val):
        if E.waited.get(sem, 0) < val:
            E.eng.wait_ge(sem.h, val)
            E.waited[sem] = val

    def _deps(self, E, reads, writes):
        need = {}
        for b in reads:
            for s, v in b.w.items():
                if need.get(s, 0) < v:
                    need[s] = v
        for b in writes:
            for s, v in b.w.items():
                if need.get(s, 0) < v:
                    need[s] = v
            for s, v in b.r.items():
                if s is E.sem:
                    continue
                if need.get(s, 0) < v:
                    need[s] = v
        for s, v in need.items():
            if E.is_pe and s is E.sem:
                continue
            self._wait(E, s, v)

    def op(self, E, emit, reads=(), writes=()):
        self._deps(E, reads, writes)
        inst = emit()
        E.sem.val += 1
        inst.then_inc(E.sem.h, 1)
        v = E.sem.val
        for b in writes:
            b.w = {E.sem: v}
            b.r = {}
        for b in reads:
            b.r[E.sem] = v
        self.n_inst += 1
        return inst

    def dma(self, Q, out, in_, reads=(), writes=(), **kw):
        s = Q.dma_sems[Q.dma_rr % len(Q.dma_sems)]
        Q.dma_rr += 1
        self._wait(Q, s, s.val)
        self._deps(Q, reads, writes)
        inst = Q.eng.dma_start(out=out, in_=in_, **kw)
        s.val += 16
        inst.then_inc(s.h, 16)
        for b in writes:
            b.w = {s: s.val}
            b.r = {}
        for b in reads:
            b.r[s] = s.val
        self.n_inst += 1
        return s

    def merge(self, olds, news):
        w = {}
        for b in olds:
            for d in (b.w, b.r):
                for s, v in d.items():
                    if w.get(s, 0) < v:
                        w[s] = v
        for b in news:
            for s, v in w.items():
                if b.w.get(s, 0) < v:
                    b.w[s] = v

    def final_wait(self, E, bufs):
        need = {}
        for b in bufs:
            for s, v in b.w.items():
                need[s] = max(need.get(s, 0), v)
        for s, v in need.items():
            self._wait(E, s, v)


class Arr:
    def __init__(self, t, n, name):
        self.t = t
        self.n = n
        self.bufs = [Buf(f"{name}[{i}]") for i in range(n)]

    def b(self, i):
        return self.bufs[i]

    def all(self):
        return list(self.bufs)


class WSpec:
    def __init__(self, name, din, dout, sw, kgs=None):
        self.name = name
        self.din, self.dout, self.sw = din, dout, sw
        self.kc = din // 128
        self.kgs = kgs or self.kc
        self.nkg = self.kc // self.kgs
        self.ncg = dout // sw
        assert self.kgs * sw * 2 <= SLAB_BYTES + 2048, (name, self.kgs, sw)


def build_program(S, cfg):
    NT = S // TT
    n_layers = cfg.get("n_layers", 4)
    NA = cfg.get("n_a", 2)
    do_mix = cfg.get("mix", True)
    do_mlp = cfg.get("mlp", True)
    nc = bass.Bass("TRN2", target_bir_lowering=False, dynamic_dma_scratch_size=8192)
    es = ExitStack()
    p = Prog(nc, es)
    pe, act, dve, pool, sp = p.pe, p.act, p.dve, p.pool, p.sp
    p.in_names = []

    def dram_in(name, shape, dt=F32):
        p.in_names.append(name)
        return nc.dram_tensor(name, list(shape), dt, kind="ExternalInput").ap()

    def dram_tmp(name, shape, dt):
        return nc.dram_tensor(name, list(shape), dt, kind="Internal").ap()

    def sb(name, shape, dt):
        return es.enter_context(nc.sbuf_tensor(name, list(shape), dt))

    NL = n_layers
    NAL = min(NA, NL)
    xT = dram_in("xT", [D, S])
    outT = nc.dram_tensor("outT", [D, S], F32, kind="ExternalOutput").ap()
    c_pj = dram_in("c_pj", [128, DC])
    ada_w = dram_in("ada_w", [NL, D, 6 * D])
    ada_b = dram_in("ada_b", [NL, 128, 96])
    nmix = dram_in("norm_mix", [NL, 128, DC])
    nmlp = dram_in("norm_mlp", [NL, 128, DC])
    mlp_w1 = dram_in("mlp_w1", [NL, D, DFF])
    mlp_w2 = dram_in("mlp_w2", [NL, DFF, D])
    consts = dram_in("consts", [4, 128, 128])
    if do_mix and NAL > 0:
        hg_w_in = dram_in("hg_w_in", [NAL, D, 4 * D])
        hg_lower = dram_in("hg_lower", [3, D])
        hg_o_norm = dram_in("hg_o_norm", [NAL, 128, 1])
        hg_w_out = dram_in("hg_w_out", [NAL, D, D])

    do_mla = do_mix and NL > NA
    NBL = NL - NA if do_mla else 0
    if do_mla:
        kv_ada_w = dram_in("kv_ada_w", [D, 2 * D])
        kv_ada_b = dram_in("kv_ada_b", [128, 32])
        kv_in_norm = dram_in("kv_in_norm", [128, DC])
        w_dkv = dram_in("mla_w_dkv", [D, 576])
        kv_norm = dram_in("mla_kv_norm", [128, 4])
        w_ukv = dram_in("mla_w_ukv", [512, 4096])
        k_norm = dram_in("mla_k_norm", [128, 3])
        w_dq = dram_in("mla_w_dq", [NBL, D, 512])
        q_lat_norm = dram_in("mla_q_lat_norm", [NBL, 128, 4])
        w_uq = dram_in("mla_w_uq", [NBL, 512, 3072])
        q_norm = dram_in("mla_q_norm", [NBL, 128, 3])
        w_o = dram_in("mla_w_o", [NBL, D, D])
        positions = dram_in("positions", [S], I32)
        rope_c = dram_in("rope_c", [128, 130])

    cst = sb("cst", [128, 4, 128], F32)
    b_cst = Buf("cst")
    p.dma(sp, cst[:], consts.rearrange("a p c -> p a c"), writes=[b_cst])
    ones32 = cst[:, 0, :]
    Uinc = cst[:, 1, :]
    Vsuf = cst[:, 2, :]
    ident_bf = sb("ident_bf", [128, 128], BF16)
    b_ident = Buf("ident")
    p.op(dve, lambda: nc.vector.tensor_copy(out=ident_bf[:], in_=cst[:, 3, :]), reads=[b_cst], writes=[b_ident])
    ones_bf = sb("ones_bf", [128, 128], BF16)
    p.op(dve, lambda: nc.vector.tensor_copy(out=ones_bf[:], in_=cst[:, 0, :]), reads=[b_cst, b_ident], writes=[b_ident])
    b_ones32 = b_cst
    eps_t = sb("eps_t", [128, 1], F32)
    b_eps = Buf("eps")
    p.op(dve, lambda: nc.vector.memset(eps_t[:], EPS), writes=[b_eps])
    cond = sb("cond", [128, DC], F32)
    b_cond = Buf("cond")
    craw = sb("craw", [128, DC], F32)
    b_craw = Buf("craw")
    p.dma(sp, craw[:], c_pj, writes=[b_craw])
    p.op(act, lambda: nc.scalar.activation(out=cond[:], in_=craw[:], func=AF.Silu),
         reads=[b_craw], writes=[b_cond])

    modv = [sb(f"modv{l}", [128, 96], F32) for l in range(NL)]
    b_modv = [Buf(f"modv{l}") for l in range(NL)]
    A1m = [sb(f"A1_{l}", [128, DC], F32) for l in range(NL)]
    A2m = [sb(f"A2_{l}", [128, DC], F32) for l in range(NL)]
    b_A = [Buf(f"A{l}") for l in range(NL)]

    NB = 8
    banks = [es.enter_context(nc.psum_tensor(f"bank{i}", [128, 512], F32)) for i in range(NB)]
    bank_b = [Buf(f"bank{i}") for i in range(NB)]
    free_banks = list(range(NB))

    def balloc():
        assert free_banks, "out of PSUM banks"
        return free_banks.pop(0)

    def bfree(i):
        free_banks.append(i)

    NSLAB = cfg.get("nslab", 3)
    slab_t = [sb(f"slab{i}", [128, SLAB_BYTES // 2], BF16) for i in range(NSLAB)]
    slab_b = [Buf(f"slab{i}") for i in range(NSLAB)]
    rr = {}

    def nxt(key, n):
        i = rr.get(key, 0)
        rr[key] = i + 1
        return i % n

    wsc = {}

    def cast_weight(key, src2d, spec):
        dst = dram_tmp("wb_" + key, [spec.ncg, spec.nkg, 128, spec.kgs, spec.sw], BF16)
        bufs = {}
        for cg in range(spec.ncg):
            for kg in range(spec.nkg):
                src = src2d[kg * spec.kgs * 128:(kg + 1) * spec.kgs * 128,
                            cg * spec.sw:(cg + 1) * spec.sw].rearrange("(k p) c -> p k c", p=128)
                b = Buf(f"wb_{key}_{cg}_{kg}")
                p.dma(pool, dst[cg, kg], src, writes=[b])
                bufs[(cg, kg)] = b
        wsc[key] = (dst, spec, bufs)

    def load_slab(key, cg, kg=0, q=None):
        dst, spec, bufs = wsc[key]
        i = nxt("slab", NSLAB)
        view = slab_t[i][:, 0:spec.kgs * spec.sw].rearrange("p (k c) -> p k c", c=spec.sw)
        p.dma(q or sp, view, dst[cg, kg], reads=[bufs[(cg, kg)]], writes=[slab_b[i]])
        return view, slab_b[i]

    def cast_layer(l):
        if do_mix and l < NAL:
            cast_weight(f"win_{l}", hg_w_in[l], WSpec("win", D, 4 * D, 512))
            cast_weight(f"wout_{l}", hg_w_out[l], WSpec("wout", D, D, 512))
        if do_mla and l == NA:
            cast_weight("wdkv_c", w_dkv[:, 0:512], WSpec("wdkv_c", D, 512, 512))
            cast_weight("wdkv_r", w_dkv[:, 512:576], WSpec("wdkv_r", D, 64, 64))
            cast_weight("wukv", w_ukv, WSpec("wukv", 512, 4096, 2048))
        if do_mla and l >= NA:
            cast_weight(f"wdq_{l}", w_dq[l - NA], WSpec("wdq", D, 512, 512))
            cast_weight(f"wuq_{l}", w_uq[l - NA], WSpec("wuq", 512, 3072, 1536))
            cast_weight(f"wo_{l}", w_o[l - NA], WSpec("wo", D, D, 512))
        if do_mlp:
            cast_weight(f"w1_{l}", mlp_w1[l], WSpec("w1", D, DFF, 512))
            cast_weight(f"w2_{l}", mlp_w2[l], WSpec("w2", DFF, D, 512, kgs=16))

    N_UPFRONT = min(2, n_layers)
    for l_ in range(N_UPFRONT):
        cast_layer(l_)

    R_BYTES = 66 * 1024
    R_t = sb("R", [128, R_BYTES // 2], BF16)

    def R_f32(off, n):
        return R_t[:, off // 2: off // 2 + 2 * n].bitcast(F32)

    def R_bf(off, n):
        return R_t[:, off // 2: off // 2 + n]

    stage_b = []
    adab_t = sb("adab_t", [128, 96], F32)
    b_adab = Buf("adab")
    gain_t = sb("gain_t", [128, 2 * DC], F32)
    b_gain = Buf("gain")
    rowbuf = [sb(f"rowbuf{i}", [1, 256], F32) for i in range(2)]
    row_b = [Buf(f"row{i}") for i in range(2)]
    one1 = cst[0:1, 0, 0:1]
    if do_mla:
        kvmodv = sb("kvmodv", [128, 32], F32)
        Akv = sb("Akv", [128, DC], F32)
        b_kvmod = Buf("kvmod")

    def ada_block(l):
        is_kv = (l == n_layers)
        ncol = 32 if is_kv else 96
        wsrc = kv_ada_w if is_kv else ada_w[l]
        mb = balloc()
        pend_tr = []
        for sg_ in range(ncol // 2):
            i = nxt("slab", NSLAB)
            sv = slab_t[i][:].bitcast(F32).rearrange("p (k c) -> p k c", c=256)
            p.dma(sp, sv, wsrc[:, sg_ * 256:(sg_ + 1) * 256].rearrange("(k p) c -> p k c", p=128),
                  writes=[slab_b[i]])
            rb = balloc()
            for k in range(DC):
                p.op(pe, lambda: nc.tensor.matmul(
                    banks[rb][0:1, 0:256], lhsT=cond[:, k:k + 1], rhs=sv[:, k, :],
                    start=(k == 0), stop=(k == DC - 1)), reads=[slab_b[i], b_cond], writes=[bank_b[rb]])
            ri = nxt("row", 2)
            p.op(act, lambda: nc.scalar.copy(out=rowbuf[ri][0:1, :], in_=banks[rb][0:1, 0:256]),
                 reads=[bank_b[rb]], writes=[row_b[ri]])
            bfree(rb)

            def tr(ri=ri, sg_=sg_):
                for oc in range(2):
                    col = sg_ * 2 + oc
                    p.op(pe, lambda: nc.tensor.matmul(
                        banks[mb][:, col:col + 1], lhsT=rowbuf[ri][0:1, oc * 128:(oc + 1) * 128], rhs=one1,
                        start=True, stop=True), reads=[row_b[ri], b_cst], writes=[bank_b[mb]])
            if pend_tr:
                pend_tr.pop(0)()
            pend_tr.append(tr)
        pend_tr.pop(0)()
        if is_kv:
            p.dma(sp, adab_t[:, 0:32], kv_ada_b, writes=[b_adab])
            p.op(dve, lambda: nc.vector.tensor_tensor(out=kvmodv[:], in0=banks[mb][:, 0:32],
                                                      in1=adab_t[:, 0:32], op=ALU.add),
                 reads=[bank_b[mb], b_adab], writes=[b_kvmod])
            bfree(mb)
            p.dma(sp, gain_t[:, 0:DC], kv_in_norm, writes=[b_gain])
            p.op(dve, lambda: nc.vector.scalar_tensor_tensor(
                out=Akv[:], in0=kvmodv[:, 16:32], scalar=1.0, in1=gain_t[:, 0:DC],
                op0=ALU.add, op1=ALU.mult), reads=[b_kvmod, b_gain], writes=[b_kvmod])
            return
        p.dma(sp, adab_t[:], ada_b[l], writes=[b_adab])
        p.op(dve, lambda: nc.vector.tensor_tensor(out=modv[l][:], in0=banks[mb][:, 0:96],
                                                  in1=adab_t[:], op=ALU.add),
             reads=[bank_b[mb], b_adab], writes=[b_modv[l]])
        bfree(mb)
        p.dma(sp, gain_t[:, 0:DC], nmix[l], writes=[b_gain])
        p.dma(sp, gain_t[:, DC:2 * DC], nmlp[l], writes=[b_gain])
        p.op(dve, lambda: nc.vector.scalar_tensor_tensor(
            out=A1m[l][:], in0=modv[l][:, 16:32], scalar=1.0, in1=gain_t[:, 0:DC],
            op0=ALU.add, op1=ALU.mult), reads=[b_modv[l], b_gain], writes=[b_A[l]])
        p.op(dve, lambda: nc.vector.scalar_tensor_tensor(
            out=A2m[l][:], in0=modv[l][:, 64:80], scalar=1.0, in1=gain_t[:, DC:2 * DC],
            op0=ALU.add, op1=ALU.mult), reads=[b_modv[l], b_gain], writes=[b_A[l]])

    pro_bufs = list(stage_b)

    if do_mix and NAL > 0:
        lbs = dram_tmp("lbs", [2, 128, D], F32)
        b_lbs = Buf("lbs")
        OFF = 40 * 1024
        r3 = R_f32(OFF, 3 * D)
        r3v = r3.rearrange("p (a n) -> p a n", n=D)
        b_r3 = Buf("r3")
        p.merge(stage_b, [b_r3])
        p.dma(sp, r3, hg_lower.rearrange("a n -> (a n)").partition_broadcast(128), writes=[b_r3])
        mx = R_f32(0, D)
        sm = R_f32(8192, D)
        b_mx = Buf("mx")
        p.merge(stage_b, [b_mx])
        p.op(dve, lambda: nc.vector.tensor_tensor(out=mx, in0=r3v[:, 0, :], in1=r3v[:, 1, :], op=ALU.max),
             reads=[b_r3], writes=[b_mx])
        p.op(dve, lambda: nc.vector.tensor_tensor(out=mx, in0=mx, in1=r3v[:, 2, :], op=ALU.max),
             reads=[b_r3, b_mx], writes=[b_mx])
        for a_ in range(3):
            p.op(dve, lambda a_=a_: nc.vector.tensor_tensor(out=r3v[:, a_, :], in0=r3v[:, a_, :], in1=mx,
                                                            op=ALU.subtract), reads=[b_r3, b_mx], writes=[b_r3])
        p.op(act, lambda: nc.scalar.activation(out=r3, in_=r3, func=AF.Exp), reads=[b_r3], writes=[b_r3])
        p.op(dve, lambda: nc.vector.tensor_tensor(out=r3v[:, 1, :], in0=r3v[:, 0, :], in1=r3v[:, 1, :], op=ALU.add),
             reads=[b_r3], writes=[b_r3])
        p.op(dve, lambda: nc.vector.tensor_tensor(out=sm, in0=r3v[:, 1, :], in1=r3v[:, 2, :], op=ALU.add),
             reads=[b_r3, b_mx], writes=[b_mx])
        p.op(dve, lambda: nc.vector.reciprocal(out=sm, in_=sm), reads=[b_mx], writes=[b_mx])
        for a_ in range(2):
            p.op(dve, lambda a_=a_: nc.vector.tensor_tensor(out=r3v[:, a_, :], in0=r3v[:, a_, :], in1=sm,
                                                            op=ALU.mult), reads=[b_r3, b_mx], writes=[b_r3])
        p.dma(sp, lbs.rearrange("a p n -> p a n"), r3v[:, 0:2, :], reads=[b_r3], writes=[b_lbs])
        pro_bufs += [b_r3, b_mx]
        ogain = sb("ogain", [128, NAL], F32)
        b_ogain = Buf("ogain")
        for l in range(NAL):
            p.dma(sp, ogain[:, l:l + 1], hg_o_norm[l], writes=[b_ogain])

    x_t = sb("x_t", [128, DC, TT], F32)
    x = Arr(x_t, DC, "x")
    h_t = sb("h_t", [128, DC, TT], BF16)
    h = Arr(h_t, DC, "h")
    a_view = R_bf(0, FC * TT).rearrange("p (c t) -> p c t", t=TT)
    a = Arr(None, FC, "a")
    p.merge(pro_bufs, a.all())
    rstd_t = sb("rstd", [128, TT], F32)
    b_rstd = Buf("rstd")
    NTMP = 5
    tmp_t = [sb(f"tmp{i}", [128, TT], F32) for i in range(NTMP)]
    tmp_b = [Buf(f"tmp{i}") for i in range(NTMP)]

    def rms_stats(chunks, n_feat, out_ap, out_b):
        bk = balloc()
        n = len(chunks)
        for i, (ap, b, npart) in enumerate(chunks):
            if npart < 0:
                npart = -npart
                p.op(pe, lambda i=i, npart=npart, ap=ap: nc.tensor.matmul(
                    banks[bk][:], lhsT=ones_bf[0:npart, :], rhs=ap,
                    start=(i == 0), stop=(i == n - 1)), reads=[b, b_ident], writes=[bank_b[bk]])
                continue
            si = nxt("tmp", NTMP)
            sqv = tmp_t[si][:, 0:TT // 2].bitcast(BF16)
            p.op(act, lambda ap=ap, sqv=sqv, npart=npart: nc.scalar.activation(
                out=sqv[0:npart, :], in_=ap, func=AF.Square), reads=[b], writes=[tmp_b[si]])
            p.op(pe, lambda sqv=sqv, i=i, npart=npart: nc.tensor.matmul(
                banks[bk][:], lhsT=ones_bf[0:npart, :], rhs=sqv[0:npart, :],
                start=(i == 0), stop=(i == n - 1)), reads=[tmp_b[si], b_ident], writes=[bank_b[bk]])
        p.op(act, lambda: nc.scalar.activation(out=out_ap, in_=banks[bk][:], func=AF.Ln,
                                               scale=1.0 / n_feat, bias=eps_t[:, 0:1]),
             reads=[bank_b[bk], b_eps], writes=[out_b])
        bfree(bk)
        p.op(act, lambda: nc.scalar.activation(out=out_ap, in_=out_ap, func=AF.Exp, scale=-0.5),
             reads=[out_b], writes=[out_b])

    def norm_mod(Acols, shcols, bA, bsh):
        rms_stats([(x_t[:, c, :], x.b(c), 128) for c in range(DC)], D, rstd_t[:], b_rstd)
        for c in range(DC):
            ti = nxt("tmp", NTMP)
            p.op(dve, lambda c=c, ti=ti: nc.vector.scalar_tensor_tensor(
                out=tmp_t[ti][:], in0=x_t[:, c, :], scalar=Acols[:, c:c + 1], in1=rstd_t[:],
                op0=ALU.mult, op1=ALU.mult), reads=[x.b(c), bA, b_rstd], writes=[tmp_b[ti]])
            p.op(act, lambda c=c, ti=ti: nc.scalar.activation(
                out=h_t[:, c, :], in_=tmp_t[ti][:], func=AF.Identity, bias=shcols[:, c:c + 1]),
                reads=[tmp_b[ti], bsh], writes=[h.b(c)])

    def proj_epilogue(key, l, rhs_view, rhs_bufs, gcol0):
        for cg in range(4):
            wv, wb = load_slab(key, cg)
            for dcl in range(4):
                dc = cg * 4 + dcl
                bk = balloc()
                for k in range(16):
                    p.op(pe, lambda k=k, dcl=dcl, wv=wv, bk=bk: nc.tensor.matmul(
                        banks[bk][:], lhsT=wv[:, k, dcl * 128:(dcl + 1) * 128], rhs=rhs_view[:, k, :],
                        start=(k == 0), stop=(k == 15)), reads=[wb, rhs_bufs[k]], writes=[bank_b[bk]])
                p.op(dve, lambda dc=dc, bk=bk: nc.vector.scalar_tensor_tensor(
                    out=x_t[:, dc, :], in0=banks[bk][:], scalar=modv[l][:, gcol0 + dc:gcol0 + dc + 1],
                    in1=x_t[:, dc, :], op0=ALU.mult, op1=ALU.add),
                    reads=[bank_b[bk], b_modv[l], x.b(dc)], writes=[x.b(dc)])
                bfree(bk)

    def mlp_block(l, on_chunk_done=None):
        norm_mod(A2m[l], modv[l][:, 48:64], b_A[l], b_modv[l])
        for s_ in range(16):
            wv, wb = load_slab(f"w1_{l}", s_)
            for oc in range(4):
                bk = balloc()
                for k in range(DC):
                    p.op(pe, lambda k=k, oc=oc, wv=wv, bk=bk: nc.tensor.matmul(
                        banks[bk][:], lhsT=wv[:, k, oc * 128:(oc + 1) * 128], rhs=h_t[:, k, :],
                        start=(k == 0), stop=(k == DC - 1)),
                        reads=[wb, h.b(k)], writes=[bank_b[bk]])
                fc = s_ * 4 + oc
                ti = nxt("tmp", NTMP)
                p.op(act, lambda bk=bk, ti=ti: nc.scalar.activation(
                    out=tmp_t[ti][:], in_=banks[bk][:], func=AF.Relu),
                    reads=[bank_b[bk]], writes=[tmp_b[ti]])
                bfree(bk)
                p.op(dve, lambda fc=fc, ti=ti: nc.vector.tensor_tensor(
                    out=a_view[:, fc, :], in0=tmp_t[ti][:], in1=tmp_t[ti][:], op=ALU.mult),
                    reads=[tmp_b[ti]], writes=[a.b(fc)])
        for cg in range(4):
            bks = [balloc() for _ in range(4)]
            for kg in range(4):
                wv, wb = load_slab(f"w2_{l}", cg, kg)
                for dcl in range(4):
                    for k in range(16):
                        fc = kg * 16 + k
                        p.op(pe, lambda k=k, dcl=dcl, wv=wv, fc=fc, bk=bks[dcl], kg=kg: nc.tensor.matmul(
                            banks[bk][:], lhsT=wv[:, k, dcl * 128:(dcl + 1) * 128], rhs=a_view[:, fc, :],
                            start=(kg == 0 and k == 0), stop=(kg == 3 and k == 15)),
                            reads=[wb, a.b(fc)], writes=[bank_b[bks[dcl]]])
            for dcl in range(4):
                dc = cg * 4 + dcl
                p.op(dve, lambda dc=dc, bk=bks[dcl]: nc.vector.scalar_tensor_tensor(
                    out=x_t[:, dc, :], in0=banks[bk][:], scalar=modv[l][:, 80 + dc:81 + dc],
                    in1=x_t[:, dc, :], op0=ALU.mult, op1=ALU.add),
                    reads=[bank_b[bks[dcl]], b_modv[l], x.b(dc)], writes=[x.b(dc)])
                bfree(bks[dcl])
                if on_chunk_done is not None:
                    on_chunk_done(dc)

    if do_mix and NAL > 0:
        KB = 1024
        hA = [[R_f32((j * 4 + r * 2) * KB, 512) for r in range(2)] for j in range(4)]
        hA_b = [[Buf(f"hA{j}_{r}") for r in range(2)] for j in range(4)]
        ktil = R_bf(16 * KB, 4 * 512).rearrange("p (a c) -> p a c", c=512)
        khat = R_bf(20 * KB, 4 * 512).rearrange("p (a c) -> p a c", c=512)
        ivt = R_bf(24 * KB, 4 * 512).rearrange("p (a c) -> p a c", c=512)
        ktil_b = [Buf(f"ktil{i}") for i in range(4)]
        khat_b = [Buf(f"khat{i}") for i in range(4)]
        ivt_b = [Buf(f"ivt{i}") for i in range(4)]
        E_T = R_f32(28 * KB, 4 * 512).rearrange("p (a c) -> p a c", c=512)
        E_b = [Buf(f"E{i}") for i in range(4)]
        qtil = R_bf(36 * KB, 4 * 512).rearrange("p (a c) -> p a c", c=512)
        ktilT = R_bf(40 * KB, 4 * 512).rearrange("p (a c) -> p a c", c=512)
        sgt = R_bf(44 * KB, 4 * 512).rearrange("p (a c) -> p a c", c=512)
        qtil_b = [Buf(f"qtil{i}") for i in range(4)]
        ktilT_b = [Buf(f"ktilT{i}") for i in range(4)]
        sgt_b = [Buf(f"sg{i}") for i in range(4)]
        og = R_bf(48 * KB, 16 * 512).rearrange("p (a c) -> p a c", c=512)
        og_b = [Buf(f"og{i}") for i in range(16)]
        Pblk = [R_bf(64 * KB + i * 256, 128) for i in range(4)]
        Pblk_b = [Buf(f"P{i}") for i in range(4)]
        hg_bufs = ([b for row in hA_b for b in row] + ktil_b + khat_b + ivt_b + E_b + qtil_b + ktilT_b
                   + sgt_b + og_b + Pblk_b)
        lb_t = [sb(f"lb{i}", [128, 512], F32) for i in range(1)]
        lb_b = [Buf(f"lb{i}") for i in range(1)]
        Sb_t = sb("Sb_t", [128, 4, 128], F32)
        Sb_b = [Buf(f"Sb{i}") for i in range(4)]
        Sbm = R_bf(65 * 1024, 4 * 128).rearrange("p (a c) -> p a c", c=128)
        Sbm_b = [Buf(f"Sbm{i}") for i in range(4)]
        hg_bufs += Sbm_b
        rs_t = [sb(f"rs{i}", [128, TT], F32) for i in range(2)]
        rs_b = [Buf(f"rs{i}") for i in range(2)]
        S_t = [sb(f"S{l}", [128, NH, 128], F32) for l in range(NAL)]
        S_b = [[Buf(f"S{l}_{hd}") for hd in range(NH)] for l in range(NAL)]
        Sbf = sb("Sbf", [128, NH, 128], BF16)
        Sbf_b = [Buf(f"Sbf{hd}") for hd in range(NH)]
        for l in range(NAL):
            p.op(pool, lambda l=l: nc.gpsimd.memset(S_t[l][:], 0.0), writes=S_b[l])

    def hgrn_block(l):
        p.merge(a.all(), hg_bufs)
        p.op(act, lambda: nc.scalar.copy(out=Sbf[:], in_=S_t[l][:]), reads=S_b[l], writes=Sbf_b)
        norm_mod(A1m[l], modv[l][:, 0:16], b_A[l], b_modv[l])
        key = f"win_{l}"
        for g in range(4):
            li = nxt("lb", 1)
            p.dma(sp, lb_t[li][:], lbs[l][:, g * 512:(g + 1) * 512], reads=[b_lbs], writes=[lb_b[li]])
            slabs = {}

            def Zproj(tb):
                if "z" not in slabs:
                    slabs["z"] = load_slab(key, 4 + g)
                zv, zb = slabs["z"]
                r = tb % 2
                a1, a2 = hA[0][r], hA[1][r]
                b1, b2 = hA_b[0][r], hA_b[1][r]
                bk = balloc()
                for k in range(DC):
                    p.op(pe, lambda: nc.tensor.matmul(
                        banks[bk][:], lhsT=h_t[:, k, tb * 128:(tb + 1) * 128], rhs=zv[:, k, :],
                        start=(k == 0), stop=(k == DC - 1)), reads=[h.b(k), zb], writes=[bank_b[bk]])
                p.op(act, lambda: nc.scalar.activation(out=a1, in_=banks[bk][:], func=AF.Sigmoid),
                     reads=[bank_b[bk]], writes=[b1])
                bfree(bk)
                p.op(dve, lambda: nc.vector.scalar_tensor_tensor(
                    out=a2, in0=a1, scalar=-1.0, in1=lb_t[li][:], op0=ALU.add, op1=ALU.mult),
                    reads=[b1, lb_b[li]], writes=[b2])
                p.op(dve, lambda: nc.vector.tensor_tensor(out=a1, in0=a1, in1=a2, op=ALU.subtract),
                     reads=[b1, b2], writes=[b1])
                p.op(act, lambda: nc.scalar.activation(out=a2, in_=a1, func=AF.Ln), reads=[b1], writes=[b2])
                p.op(pool, lambda: nc.gpsimd.tensor_scalar(out=a1, in0=a1, scalar1=-1.0, scalar2=1.0,
                                                           op0=ALU.mult, op1=ALU.add), reads=[b1], writes=[b1])

            def cum(tb):
                r = tb % 2
                a1, a2, a3, a4 = (hA[j][r] for j in range(4))
                b1, b2, b3, b4 = (hA_b[j][r] for j in range(4))
                bk = balloc()
                p.op(pe, lambda: nc.tensor.matmul(banks[bk][:], lhsT=Uinc, rhs=a2, start=True, stop=True),
                     reads=[b_cst, b2], writes=[bank_b[bk]])
                p.op(act, lambda: nc.scalar.activation(out=a3, in_=banks[bk][:], func=AF.Exp, scale=-1.0),
                     reads=[bank_b[bk]], writes=[b3])
                bfree(bk)
                bk = balloc()
                p.op(pe, lambda: nc.tensor.matmul(banks[bk][:], lhsT=Vsuf, rhs=a2, start=True, stop=True),
                     reads=[b_cst, b2], writes=[bank_b[bk]])
                p.op(act, lambda: nc.scalar.activation(out=a4, in_=banks[bk][:], func=AF.Exp),
                     reads=[bank_b[bk]], writes=[b4])
                bfree(bk)
                bk = balloc()
                for hh in range(4):
                    p.op(pe, lambda: nc.tensor.matmul(
                        banks[bk][:, hh * 128:(hh + 1) * 128], lhsT=a2[:, hh * 128:(hh + 1) * 128], rhs=Uinc,
                        start=True, stop=True), reads=[b_cst, b2], writes=[bank_b[bk]])
                p.op(act, lambda: nc.scalar.activation(
                    out=E_T[:, :, tb * 128:(tb + 1) * 128],
                    in_=banks[bk][:].rearrange("p (a c) -> p a c", c=128), func=AF.Exp),
                    reads=[bank_b[bk]], writes=E_b)
                bfree(bk)
                p.op(dve, lambda: nc.vector.tensor_tensor(out=ktil[:, tb, :], in0=a1, in1=a3, op=ALU.mult),
                     reads=[b1, b3], writes=[ktil_b[tb]])
                p.op(pool, lambda: nc.gpsimd.tensor_tensor(out=khat[:, tb, :], in0=a1, in1=a4, op=ALU.mult),
                     reads=[b1, b4], writes=[khat_b[tb]])

            def Iproj(tb):
                if "i" not in slabs:
                    slabs["i"] = load_slab(key, 8 + g)
                iv_, ib = slabs["i"]
                bk = balloc()
                for k in range(DC):
                    p.op(pe, lambda: nc.tensor.matmul(
                        banks[bk][:], lhsT=h_t[:, k, tb * 128:(tb + 1) * 128], rhs=iv_[:, k, :],
                        start=(k == 0), stop=(k == DC - 1)), reads=[h.b(k), ib], writes=[bank_b[bk]])
                p.op(act, lambda: nc.scalar.copy(out=ivt[:, tb, :], in_=banks[bk][:]),
                     reads=[bank_b[bk]], writes=[ivt_b[tb]])
                bfree(bk)

            def Gproj(hh):
                if "g" not in slabs:
                    slabs["g"] = load_slab(key, 12 + g)
                gv, gb = slabs["g"]
                bk = balloc()
                for k in range(DC):
                    p.op(pe, lambda: nc.tensor.matmul(
                        banks[bk][:], lhsT=gv[:, k, hh * 128:(hh + 1) * 128], rhs=h_t[:, k, :],
                        start=(k == 0), stop=(k == DC - 1)), reads=[h.b(k), gb], writes=[bank_b[bk]])
                p.op(act, lambda: nc.scalar.activation(out=sgt[:, hh, :], in_=banks[bk][:], func=AF.Silu),
                     reads=[bank_b[bk]], writes=[sgt_b[hh]])
                bfree(bk)

            def Qproj(hh):
                if "q" not in slabs:
                    slabs["q"] = load_slab(key, g)
                qv, qb = slabs["q"]
                bk = balloc()
                for k in range(DC):
                    p.op(pe, lambda: nc.tensor.matmul(
                        banks[bk][:], lhsT=qv[:, k, hh * 128:(hh + 1) * 128], rhs=h_t[:, k, :],
                        start=(k == 0), stop=(k == DC - 1)), reads=[h.b(k), qb], writes=[bank_b[bk]])
                p.op(dve, lambda: nc.vector.scalar_tensor_tensor(
                    out=qtil[:, hh, :], in0=banks[bk][:], scalar=128 ** -0.5, in1=E_T[:, hh, :],
                    op0=ALU.mult, op1=ALU.mult), reads=[bank_b[bk], E_b[hh]], writes=[qtil_b[hh]])
                bfree(bk)

            def Ttrans(hh):
                bk = balloc()
                bv = banks[bk][:].bitcast(BF16)
                for tb in range(4):
                    p.op(pe, lambda: nc.tensor.transpose(
                        out=bv[:, tb * 128:(tb + 1) * 128], in_=ktil[:, tb, hh * 128:(hh + 1) * 128],
                        identity=ident_bf[:]), reads=[ktil_b[tb], b_ident], writes=[bank_b[bk]])
                p.op(dve, lambda: nc.vector.tensor_copy(out=ktilT[:, hh, :], in_=bv[:, 0:512]),
                     reads=[bank_b[bk]], writes=[ktilT_b[hh]])
                bfree(bk)

            Zproj(0); Zproj(1); Iproj(0); cum(0)
            Zproj(2); Iproj(1); cum(1)
            Zproj(3); Iproj(2); cum(2)
            Gproj(0); Iproj(3); cum(3)
            Gproj(1); Gproj(2); Gproj(3)
            for hh in range(4):
                Qproj(hh)
            for hh in range(4):
                Ttrans(hh)
            obk = [balloc() for _ in range(4)]
            for tb in range(4):
                sbk = balloc()
                for hh in range(4):
                    p.op(pe, lambda: nc.tensor.matmul(
                        banks[sbk][:, hh * 128:(hh + 1) * 128], lhsT=ktilT[:, hh, tb * 128:(tb + 1) * 128],
                        rhs=qtil[:, hh, tb * 128:(tb + 1) * 128], start=True, stop=True),
                        reads=[ktilT_b[hh], qtil_b[hh]], writes=[bank_b[sbk]])
                ubk = [balloc(), balloc()]
                for hh in range(4):
                    for hf in range(2):
                        uc = hh * 128
                        p.op(pe, lambda: nc.tensor.matmul(
                            banks[ubk[hf]][:, uc:uc + 128],
                            lhsT=khat[hf * 64:(hf + 1) * 64, tb, hh * 128:(hh + 1) * 128],
                            rhs=ivt[hf * 64:(hf + 1) * 64, tb, hh * 128:(hh + 1) * 128], start=True, stop=True),
                            reads=[khat_b[tb], ivt_b[tb]], writes=[bank_b[ubk[hf]]])
                for hh in range(4):
                    p.op(dve, lambda: nc.vector.tensor_tensor(out=Pblk[hh], in0=banks[sbk][:, hh * 128:(hh + 1) * 128],
                                                              in1=Uinc, op=ALU.mult),
                         reads=[bank_b[sbk], b_cst], writes=[Pblk_b[hh]])
                bfree(sbk)
                for hh in range(4):
                    head = 4 * g + hh
                    c0 = tb * 128
                    p.op(pe, lambda: nc.tensor.matmul(
                        banks[obk[hh]][:, c0:c0 + 128], lhsT=ivt[:, tb, hh * 128:(hh + 1) * 128],
                        rhs=Pblk[hh], start=True, stop=False),
                        reads=[ivt_b[tb], Pblk_b[hh]], writes=[bank_b[obk[hh]]])
                    p.op(pe, lambda: nc.tensor.matmul(
                        banks[obk[hh]][:, c0:c0 + 64], lhsT=Sbf[:, head, :], rhs=qtil[:, hh, c0:c0 + 64],
                        start=False, stop=False), reads=[Sbf_b[head], qtil_b[hh]], writes=[bank_b[obk[hh]]])
                for hh in range(4):
                    head = 4 * g + hh
                    c0 = tb * 128
                    uc = hh * 128
                    p.op(dve, lambda: nc.vector.scalar_tensor_tensor(
                        out=Sb_t[:, hh, :], in0=S_t[l][:, head, :], scalar=E_T[:, hh, c0 + 63:c0 + 64],
                        in1=banks[ubk[0]][:, uc:uc + 128], op0=ALU.mult, op1=ALU.add),
                        reads=[S_b[l][head], E_b[hh], bank_b[ubk[0]]], writes=[Sb_b[hh]])
                    p.op(act, lambda: nc.scalar.copy(out=Sbm[:, hh, :], in_=Sb_t[:, hh, :]),
                         reads=[Sb_b[hh]], writes=[Sbm_b[hh]])
                    p.op(dve, lambda: nc.vector.scalar_tensor_tensor(
                        out=S_t[l][:, head, :], in0=Sb_t[:, hh, :], scalar=E_T[:, hh, c0 + 127:c0 + 128],
                        in1=banks[ubk[1]][:, uc:uc + 128], op0=ALU.mult, op1=ALU.add),
                        reads=[Sb_b[hh], E_b[hh], bank_b[ubk[1]]], writes=[S_b[l][head]])
                    p.op(act, lambda: nc.scalar.copy(out=Sbf[:, head, :], in_=S_t[l][:, head, :]),
                         reads=[S_b[l][head]], writes=[Sbf_b[head]])
                bfree(ubk[0])
                bfree(ubk[1])
                for hh in range(4):
                    c0 = tb * 128 + 64
                    p.op(pe, lambda: nc.tensor.matmul(
                        banks[obk[hh]][:, c0:c0 + 64], lhsT=Sbm[:, hh, :], rhs=qtil[:, hh, c0:c0 + 64],
                        start=False, stop=True), reads=[Sbm_b[hh], qtil_b[hh]], writes=[bank_b[obk[hh]]])
            for hh in range(4):
                head = 4 * g + hh
                ri = nxt("rs", 2)
                rms_stats([(banks[obk[hh]][:], bank_b[obk[hh]], 128)], 128, rs_t[ri][:], rs_b[ri])
                ti = nxt("tmp", NTMP)
                p.op(dve, lambda: nc.vector.scalar_tensor_tensor(
                    out=tmp_t[ti][:], in0=banks[obk[hh]][:], scalar=ogain[:, l:l + 1], in1=rs_t[ri][:],
                    op0=ALU.mult, op1=ALU.mult), reads=[bank_b[obk[hh]], b_ogain, rs_b[ri]], writes=[tmp_b[ti]])
                bfree(obk[hh])
                p.op(pool, lambda: nc.gpsimd.tensor_tensor(out=og[:, head, :], in0=tmp_t[ti][:], in1=sgt[:, hh, :], op=ALU.mult),
                     reads=[tmp_b[ti], sgt_b[hh]], writes=[og_b[head]])
        proj_epilogue(f"wout_{l}", l, og, og_b, 32)
        p.merge(hg_bufs, a.all())

    if do_mla:
        KB = 1024
        SM_SCALE = QK ** -0.5
        ropec = sb("ropec", [128, 130], F32)
        b_ropec = Buf("ropec")
        p.dma(sp, ropec[:], rope_c, writes=[b_ropec])
        Rm = ropec[0:64, 0:64]
        mg = sb("mla_gains", [128, 4 + 3 + 4 * NBL + 3 * NBL], F32)
        b_mg = Buf("mg")
        p.dma(sp, mg[:, 0:4], kv_norm, writes=[b_mg])
        p.dma(sp, mg[:, 4:7], k_norm, writes=[b_mg])
        for j in range(NBL):
            p.dma(sp, mg[:, 7 + 7 * j:11 + 7 * j], q_lat_norm[j], writes=[b_mg])
            p.dma(sp, mg[:, 11 + 7 * j:14 + 7 * j], q_norm[j], writes=[b_mg])
        cs_t = sb("cossin", [64, 2, TT], F32)
        b_cs = Buf("cossin")
        negpi = sb("negpi", [128, 1], F32)
        utri_bf = sb("utri_bf", [128, 128], BF16)
        b_mc = Buf("mla_consts")
        p.op(dve, lambda: nc.vector.memset(negpi[:], -math.pi), writes=[b_mc])
        p.op(dve, lambda: nc.vector.tensor_copy(out=utri_bf[:], in_=cst[:, 1, :]), reads=[b_cst, b_mc], writes=[b_mc])
        p.op(dve, lambda: nc.vector.tensor_copy(out=utri_bf[0:64, 64:128], in_=cst[0:64, 0, 64:128]),
             reads=[b_cst, b_mc], writes=[b_mc])
        KN = dram_tmp("KN", [NH, 128, S], BF16)
        KR = dram_tmp("KR", [NH, 64, S], BF16)
        VV = dram_tmp("VV", [S // 128, 128, NH * 128], BF16)
        kvd_b = [[[Buf(f"kv{t}_{hp}_{i}") for i in range(3)] for hp in range(8)] for t in range(NT)]
        ao = R_bf(0, 16 * 512).rearrange("p (a c) -> p a c", c=512)
        ao_b = [Buf(f"ao{i}") for i in range(16)]
        latn = R_bf(16 * KB, 4 * 512).rearrange("p (a c) -> p a c", c=512)
        latn_b = [Buf(f"latn{i}") for i in range(4)]
        qn2 = [R_bf(20 * KB + r * 2 * KB, 2 * 512).rearrange("p (a c) -> p a c", c=512) for r in range(2)]
        qr2 = [R_bf(24 * KB + r * 2 * KB, 2 * 512).rearrange("p (a c) -> p a c", c=512) for r in range(2)]
        q2_b = [Buf(f"q2_{r}") for r in range(2)]
        kvc = [[R_bf(base + i * 2 * KB, 1024) for i in range(3)] for base in (28 * KB, 34 * KB, 60 * KB)]
        kvc_b = [[Buf(f"kvc{r}_{i}") for i in range(3)] for r in range(3)]
        NPT = 4
        ATT_DEPTH = cfg.get("att_depth", 2)
        Pt = [R_bf(40 * KB + i * KB, 512) for i in range(NPT)]
        Pt_b = [Buf(f"Pt{i}") for i in range(NPT)]
        csg = R_f32(44 * KB, 2 * 512).rearrange("p (a c) -> p a c", c=512)
        b_csg = Buf("csg")
        kr0 = R_f32(48 * KB, 512)
        b_kr0 = Buf("kr0")
        sqpe = R_bf(50 * KB, 512)
        b_sqpe = Buf("sqpe")
        rawr = [R_f32(52 * KB + i * 2 * KB, 512) for i in range(2)]
        rawr_b = [Buf(f"rawr{i}") for i in range(2)]
        rsh = [R_f32(56 * KB + i * 2 * KB, 512) for i in range(2)]
        rsh_b = [Buf(f"rsh{i}") for i in range(2)]
        mla_bufs = (ao_b + latn_b + q2_b + [b for row in kvc_b for b in row] + Pt_b
                    + [b_csg, b_kr0, b_sqpe] + rawr_b + rsh_b)

    def rope_tables(t):
        ti = nxt("tmp", NTMP)
        pos_i = tmp_t[ti][0:64, :].bitcast(I32)
        p.dma(sp, pos_i, positions[t * TT:(t + 1) * TT].partition_broadcast(64), writes=[tmp_b[ti]])
        ai = nxt("tmp", NTMP)
        ang = tmp_t[ai][0:64, :]
        p.op(dve, lambda: nc.vector.tensor_copy(out=ang, in_=pos_i), reads=[tmp_b[ti]], writes=[tmp_b[ai]])
        p.op(dve, lambda: nc.vector.tensor_scalar(out=ang, in0=ang, scalar1=ropec[0:64, 64:65], scalar2=None,
                                                  op0=ALU.mult), reads=[tmp_b[ai], b_ropec], writes=[tmp_b[ai]])
        for which, phase in ((0, 0.25), (1, 0.0)):
            yi_ = nxt("tmp", NTMP)
            y = tmp_t[yi_][0:64, :]
            p.op(dve, lambda: nc.vector.tensor_scalar(out=y, in0=ang, scalar1=1.0 / (2 * math.pi), scalar2=phase + 0.5,
                                                      op0=ALU.mult, op1=ALU.add), reads=[tmp_b[ai]], writes=[tmp_b[yi_]])
            ni_ = nxt("tmp", NTMP)
            n_i = tmp_t[ni_][0:64, :].bitcast(I32)
            nf_ = nxt("tmp", NTMP)
            n_f = tmp_t[nf_][0:64, :]
            p.op(dve, lambda: nc.vector.tensor_copy(out=n_i, in_=y), reads=[tmp_b[yi_]], writes=[tmp_b[ni_]])
            p.op(dve, lambda: nc.vector.tensor_copy(out=n_f, in_=n_i), reads=[tmp_b[ni_]], writes=[tmp_b[nf_]])
            p.op(dve, lambda: nc.vector.tensor_tensor(out=y, in0=y, in1=n_f, op=ALU.subtract),
                 reads=[tmp_b[yi_], tmp_b[nf_]], writes=[tmp_b[yi_]])
            p.op(dve, lambda: nc.vector.scalar_tensor_tensor(out=y, in0=y, scalar=0.0, in1=y, op0=ALU.is_lt, op1=ALU.add),
                 reads=[tmp_b[yi_]], writes=[tmp_b[yi_]])
            p.op(dve, lambda: nc.vector.tensor_scalar(out=y, in0=y, scalar1=2 * math.pi, scalar2=-math.pi,
                                                      op0=ALU.mult, op1=ALU.add), reads=[tmp_b[yi_]], writes=[tmp_b[yi_]])
            p.op(dve, lambda: nc.vector.tensor_scalar(out=y, in0=y, scalar1=-3.1415925, scalar2=3.1415925,
                                                      op0=ALU.max, op1=ALU.min), reads=[tmp_b[yi_]], writes=[tmp_b[yi_]])
            p.op(act, lambda: nc.scalar.activation(out=cs_t[:, which, :], in_=y, func=AF.Sin),
                 reads=[tmp_b[yi_]], writes=[b_cs])

    def fold_gain(gcol):
        p.op(dve, lambda: nc.vector.tensor_scalar(out=csg[0:64, 0, :], in0=cs_t[:, 0, :], scalar1=mg[0:64, gcol + 1:gcol + 2],
                                                  scalar2=None, op0=ALU.mult), reads=[b_cs, b_mg], writes=[b_csg])
        p.op(dve, lambda: nc.vector.tensor_scalar(out=csg[0:64, 1, :], in0=cs_t[:, 1, :], scalar1=mg[0:64, gcol + 2:gcol + 3],
                                                  scalar2=None, op0=ALU.mult), reads=[b_cs, b_mg, b_csg], writes=[b_csg])

    def roped(raw_bank, out_ap, out_reads, out_writes, rstd_ap, rstd_b):
        ri = nxt("rawr", 2)
        p.op(act, lambda: nc.scalar.copy(out=rawr[ri][0:64, :], in_=banks[raw_bank][0:64, :]),
             reads=[bank_b[raw_bank]], writes=[rawr_b[ri]])
        rb = balloc()
        p.op(pe, lambda: nc.tensor.matmul(banks[rb][0:64, :], lhsT=Rm, rhs=rawr[ri][0:64, :], start=True, stop=True),
             reads=[rawr_b[ri], b_ropec], writes=[bank_b[rb]])
        t1 = nxt("tmp", NTMP)
        p.op(dve, lambda: nc.vector.tensor_tensor(out=tmp_t[t1][0:64, :], in0=banks[rb][0:64, :], in1=csg[0:64, 1, :], op=ALU.mult),
             reads=[bank_b[rb], b_csg], writes=[tmp_b[t1]])
        bfree(rb)
        p.op(pool, lambda: nc.gpsimd.tensor_tensor(out=rawr[ri][0:64, :], in0=rawr[ri][0:64, :], in1=csg[0:64, 0, :], op=ALU.mult),
             reads=[rawr_b[ri], b_csg], writes=[rawr_b[ri]])
        p.op(dve, lambda: nc.vector.tensor_tensor(out=tmp_t[t1][0:64, :], in0=tmp_t[t1][0:64, :], in1=rawr[ri][0:64, :], op=ALU.add),
             reads=[tmp_b[t1], rawr_b[ri]], writes=[tmp_b[t1]])
        if rstd_ap is None:
            p.op(act, lambda: nc.scalar.copy(out=out_ap, in_=tmp_t[t1][0:64, :]), reads=[tmp_b[t1]] + out_reads, writes=out_writes)
        else:
            p.op(dve, lambda: nc.vector.tensor_tensor(out=out_ap, in0=tmp_t[t1][0:64, :], in1=rstd_ap, op=ALU.mult),
                 reads=[tmp_b[t1], rstd_b] + out_reads, writes=out_writes)

    def lat_proj(key, gcol0):
        wv, wb = load_slab(key, 0)
        bks = []
        for cc in range(4):
            bk = balloc()
            bks.append(bk)
            for k in range(DC):
                p.op(pe, lambda k=k, cc=cc, bk=bk: nc.tensor.matmul(
                    banks[bk][:], lhsT=wv[:, k, cc * 128:(cc + 1) * 128], rhs=h_t[:, k, :],
                    start=(k == 0), stop=(k == DC - 1)), reads=[wb, h.b(k)], writes=[bank_b[bk]])
        rms_stats([(banks[bk][:], bank_b[bk], 128) for bk in bks], 512, rstd_t[:], b_rstd)
        for cc in range(4):
            p.op(dve, lambda cc=cc: nc.vector.scalar_tensor_tensor(
                out=latn[:, cc, :], in0=banks[bks[cc]][:], scalar=mg[:, gcol0 + cc:gcol0 + cc + 1], in1=rstd_t[:],
                op0=ALU.mult, op1=ALU.mult), reads=[bank_b[bks[cc]], b_mg, b_rstd], writes=[latn_b[cc]])
            bfree(bks[cc])

    def head_stats(nope_bank, rope_sq_ap, rope_sq_b, ri):
        rms_stats([(banks[nope_bank][:], bank_b[nope_bank], 128), (rope_sq_ap, rope_sq_b, -64)], QK, rsh[ri], rsh_b[ri])

    def kv_block(t):
        p.merge(a.all(), mla_bufs)
        norm_mod(Akv, kvmodv[:, 0:16], b_kvmod, b_kvmod)
        lat_proj("wdkv_c", 0)
        wv, wb = load_slab("wdkv_r", 0)
        pb = balloc()
        for k in range(DC):
            p.op(pe, lambda k=k: nc.tensor.matmul(banks[pb][0:64, :], lhsT=wv[:, k, 0:64], rhs=h_t[:, k, :],
                                                  start=(k == 0), stop=(k == DC - 1)),
                 reads=[wb, h.b(k)], writes=[bank_b[pb]])
        p.op(act, lambda: nc.scalar.activation(out=sqpe[0:64, :], in_=banks[pb][0:64, :], func=AF.Square),
             reads=[bank_b[pb]], writes=[b_sqpe])
        fold_gain(4)
        roped(pb, kr0[0:64, :], [], [b_kr0], None, None)
        bfree(pb)
        uv = [load_slab("wukv", 0), None]
        for hp in range(8):
            if hp == 4:
                uv[1] = load_slab("wukv", 1)
            wv, wb = uv[hp // 4]
            r = nxt("kvc", 3)
            kn2s = kvc[r][0].rearrange("p (a c) -> p a c", c=512)
            kr2s = kvc[r][1].rearrange("p (a c) -> p a c", c=512)
            v2s = kvc[r][2].rearrange("p (a c) -> p a c", c=256)
            for hh in range(2):
                hd = 2 * hp + hh
                c0 = (hd % 8) * 256
                bk = balloc()
                for kc in range(4):
                    p.op(pe, lambda kc=kc: nc.tensor.matmul(
                        banks[bk][:], lhsT=wv[:, kc, c0:c0 + 128], rhs=latn[:, kc, :],
                        start=(kc == 0), stop=(kc == 3)), reads=[wb, latn_b[kc]], writes=[bank_b[bk]])
                ri = nxt("rsh", 2)
                head_stats(bk, sqpe[0:64, :], b_sqpe, ri)
                p.op(dve, lambda: nc.vector.scalar_tensor_tensor(
                    out=kn2s[:, hh, :], in0=banks[bk][:], scalar=mg[:, 4:5], in1=rsh[ri],
                    op0=ALU.mult, op1=ALU.mult), reads=[bank_b[bk], b_mg, rsh_b[ri]], writes=[kvc_b[r][0]])
                bfree(bk)
                p.op(pool, lambda: nc.gpsimd.tensor_tensor(out=kr2s[0:64, hh, :], in0=kr0[0:64, :], in1=rsh[ri][0:64, :], op=ALU.mult),
                     reads=[b_kr0, rsh_b[ri]], writes=[kvc_b[r][1]])
            c0 = ((2 * hp) % 8) * 256
            for tb in range(4):
                bk = balloc()
                rhs_v = wv[:, :, c0:c0 + 512].rearrange("p k (h c) -> p k h c", c=256)
                for kc in range(4):
                    p.op(pe, lambda kc=kc: nc.tensor.matmul(
                        banks[bk][:, 0:256], lhsT=latn[:, kc, tb * 128:(tb + 1) * 128], rhs=rhs_v[:, kc, :, 128:256],
                        start=(kc == 0), stop=(kc == 3)), reads=[wb, latn_b[kc]], writes=[bank_b[bk]])
                p.op(act, lambda: nc.scalar.copy(out=v2s[:, tb, :], in_=banks[bk][:, 0:256]),
                     reads=[bank_b[bk]], writes=[kvc_b[r][2]])
                bfree(bk)
            p.dma(pool, KN[2 * hp:2 * hp + 2, :, t * TT:(t + 1) * TT].rearrange("a p s -> p a s"), kn2s,
                  reads=[kvc_b[r][0]], writes=[kvd_b[t][hp][0]])
            p.dma(pool, KR[2 * hp:2 * hp + 2, :, t * TT:(t + 1) * TT].rearrange("a p s -> p a s"), kr2s[0:64],
                  reads=[kvc_b[r][1]], writes=[kvd_b[t][hp][1]])
            p.dma(pool, VV[4 * t:4 * t + 4, :, hp * 256:(hp + 1) * 256].rearrange("a p c -> p a c"), v2s,
                  reads=[kvc_b[r][2]], writes=[kvd_b[t][hp][2]])
        p.merge(mla_bufs, a.all())

    def mla_block(l, t):
        j = l - NA
        g0 = 7 + 7 * j
        p.merge(a.all(), mla_bufs)
        for r_ in range(3):
            p.op(pool, lambda: nc.gpsimd.memset(kvc[r_][1][64:128, :], 0.0), writes=[kvc_b[r_][1]])
        for r_ in range(2):
            p.op(pool, lambda: nc.gpsimd.memset(qr2[r_][64:128, :, :], 0.0), writes=[q2_b[r_]])
        norm_mod(A1m[l], modv[l][:, 0:16], b_A[l], b_modv[l])
        lat_proj(f"wdq_{l}", g0)
        fold_gain(g0 + 4)
        uq = [load_slab(f"wuq_{l}", 0), None]

        def load_chunk(hp, jt):
            r = nxt("kvc", 3)
            kn2c = kvc[r][0].rearrange("p (a c) -> p a c", c=512)
            kr2c = kvc[r][1].rearrange("p (a c) -> p a c", c=512)
            v2c = kvc[r][2].rearrange("p (a c) -> p a c", c=256)
            p.dma(sp, kn2c, KN[2 * hp:2 * hp + 2, :, jt * TT:(jt + 1) * TT].rearrange("a p s -> p a s"),
                  reads=[kvd_b[jt][hp][0]], writes=[kvc_b[r][0]])
            p.dma(sp, kr2c[0:64], KR[2 * hp:2 * hp + 2, :, jt * TT:(jt + 1) * TT].rearrange("a p s -> p a s"),
                  reads=[kvd_b[jt][hp][1]], writes=[kvc_b[r][1]])
            p.dma(sp, v2c, VV[4 * jt:4 * jt + 4, :, hp * 256:(hp + 1) * 256].rearrange("a p c -> p a c"),
                  reads=[kvd_b[jt][hp][2]], writes=[kvc_b[r][2]])
            return (r, kn2c, kr2c, v2c)

        def qproj(hp):
            if hp == 4:
                uq[1] = load_slab(f"wuq_{l}", 1)
            wv, wb = uq[hp // 4]
            qi = nxt("q2", 2)
            for hh in range(2):
                hd = 2 * hp + hh
                c0 = (hd % 8) * 192
                bn = balloc()
                for kc in range(4):
                    p.op(pe, lambda kc=kc: nc.tensor.matmul(
                        banks[bn][:], lhsT=wv[:, kc, c0:c0 + 128], rhs=latn[:, kc, :],
                        start=(kc == 0), stop=(kc == 3)), reads=[wb, latn_b[kc]], writes=[bank_b[bn]])
                br = balloc()
                for kc in range(4):
                    p.op(pe, lambda kc=kc: nc.tensor.matmul(
                        banks[br][0:64, :], lhsT=wv[:, kc, c0 + 128:c0 + 192], rhs=latn[:, kc, :],
                        start=(kc == 0), stop=(kc == 3)), reads=[wb, latn_b[kc]], writes=[bank_b[br]])
                ri = nxt("rsh", 2)
                rms_stats([(banks[bn][:], bank_b[bn], 128), (banks[br][0:64, :], bank_b[br], 64)], QK, rsh[ri], rsh_b[ri])
                p.op(dve, lambda: nc.vector.scalar_tensor_tensor(
                    out=qn2[qi][:, hh, :], in0=banks[bn][:], scalar=mg[:, g0 + 4:g0 + 5], in1=rsh[ri],
                    op0=ALU.mult, op1=ALU.mult), reads=[bank_b[bn], b_mg, rsh_b[ri]], writes=[q2_b[qi]])
                bfree(bn)
                roped(br, qr2[qi][0:64, hh, :], [], [q2_b[qi]], rsh[ri][0:64, :], rsh_b[ri])
                bfree(br)
            return qi

        def att(hp, qi):
            ch = {0: load_chunk(hp, 0)}
            obk = [balloc() for _ in range(2)]
            dbk = [balloc() for _ in range(2)]
            pend = []

            def flush_one():
                (pi, hh, kb, q0, first, last, v2c, rv) = pend.pop(0)
                p.op(pe, lambda: nc.tensor.matmul(
                    banks[obk[hh]][:, q0:TT], lhsT=v2c[:, kb, hh * 128:(hh + 1) * 128], rhs=Pt[pi][:, q0:TT],
                    start=first, stop=last), reads=[kvc_b[rv][2], Pt_b[pi]], writes=[bank_b[obk[hh]]])
                p.op(pe, lambda: nc.tensor.matmul(
                    banks[dbk[hh]][:, q0:TT], lhsT=ones_bf[:], rhs=Pt[pi][:, q0:TT],
                    start=first, stop=last), reads=[b_ident, Pt_b[pi]], writes=[bank_b[dbk[hh]]])

            for jt in range(t + 1):
                if jt + 1 <= t:
                    ch[jt + 1] = load_chunk(hp, jt + 1)
                r, kn2c, kr2c, v2c = ch[jt]
                for hh in range(2):
                    for kb in range(4):
                        diag = (jt == t)
                        q0 = kb * 128 if diag else 0
                        first = (jt == 0 and kb == 0)
                        last = (jt == t and kb == 3)
                        sbk = balloc()
                        p.op(pe, lambda: nc.tensor.matmul(
                            banks[sbk][:, q0:TT], lhsT=kn2c[:, hh, kb * 128:(kb + 1) * 128], rhs=qn2[qi][:, hh, q0:TT],
                            start=True, stop=False), reads=[kvc_b[r][0], q2_b[qi]], writes=[bank_b[sbk]])
                        p.op(pe, lambda: nc.tensor.matmul(
                            banks[sbk][:, q0:TT], lhsT=kr2c[:, hh, kb * 128:(kb + 1) * 128], rhs=qr2[qi][:, hh, q0:TT],
                            start=False, stop=True), reads=[kvc_b[r][1], q2_b[qi]], writes=[bank_b[sbk]])
                        pi = nxt("Pt", NPT)
                        p.op(act, lambda: nc.scalar.activation(out=Pt[pi][:, q0:TT], in_=banks[sbk][:, q0:TT],
                                                               func=AF.Exp, scale=SM_SCALE),
                             reads=[bank_b[sbk]], writes=[Pt_b[pi]])
                        bfree(sbk)
                        if diag:
                            p.op(pool, lambda: nc.gpsimd.tensor_tensor(out=Pt[pi][:, q0:q0 + 128], in0=Pt[pi][:, q0:q0 + 128],
                                                                       in1=utri_bf[:], op=ALU.mult),
                                 reads=[Pt_b[pi], b_mc], writes=[Pt_b[pi]])
                        pend.append((pi, hh, kb, q0, first, last, v2c, r))
                        if len(pend) > ATT_DEPTH:
                            flush_one()
            while pend:
                flush_one()
            for hh in range(2):
                hd = 2 * hp + hh
                ti = nxt("tmp", NTMP)
                p.op(act, lambda: nc.scalar.activation(out=tmp_t[ti][:], in_=banks[dbk[hh]][:], func=AF.Ln),
                     reads=[bank_b[dbk[hh]]], writes=[tmp_b[ti]])
                bfree(dbk[hh])
                p.op(act, lambda: nc.scalar.activation(out=tmp_t[ti][:], in_=tmp_t[ti][:], func=AF.Exp, scale=-1.0),
                     reads=[tmp_b[ti]], writes=[tmp_b[ti]])
                p.op(dve, lambda: nc.vector.tensor_tensor(out=ao[:, hd, :], in0=banks[obk[hh]][:], in1=tmp_t[ti][:], op=ALU.mult),
                     reads=[bank_b[obk[hh]], tmp_b[ti]], writes=[ao_b[hd]])
                bfree(obk[hh])

        qi_cur = qproj(0)
        for hp in range(8):
            qi_next = qproj(hp + 1) if hp + 1 < 8 else None
            att(hp, qi_cur)
            qi_cur = qi_next
        proj_epilogue(f"wo_{l}", l, ao, ao_b, 32)
        p.merge(mla_bufs, a.all())

    out_bufs = []
    ada_block(0)
    src0 = xT[:, 0:TT].rearrange("(c p) s -> p c s", p=128)
    p.dma(sp, x_t[:], src0, writes=x.all())
    for t in range(NT):
        if do_mla:
            rope_tables(t)

        def chunk_done(dc, t=t):
            ob = Buf(f"out{t}_{dc}")
            p.dma(act, outT[dc * 128:(dc + 1) * 128, t * TT:(t + 1) * TT], x_t[:, dc, :], reads=[x.b(dc)], writes=[ob])
            out_bufs.append(ob)
            if t + 1 < NT:
                p.dma(act, x_t[:, dc, :], xT[dc * 128:(dc + 1) * 128, (t + 1) * TT:(t + 2) * TT], writes=[x.b(dc)])

        for l in range(n_layers):
            last = (l == n_layers - 1)
            if do_mla and l == NA:
                kv_block(t)
            if do_mix and l < NAL:
                hgrn_block(l)
            if do_mla and l >= NA:
                mla_block(l, t)
            if t == 0 and l == 0:
                for l_ in range(N_UPFRONT, n_layers):
                    cast_layer(l_)
            if t == 0 and l + 1 < n_layers:
                ada_block(l + 1)
            if do_mlp:
                mlp_block(l, chunk_done if last else None)
            if t == 0 and do_mla and l == 0:
                ada_block(n_layers)
        if not do_mlp:
            for dc in range(DC):
                chunk_done(dc)
    p.final_wait(sp, out_bufs)
    es.close()
    return nc, p


def _pj(v):
    v = np.asarray(v)
    n = v.shape[-1] // 128
    return np.ascontiguousarray(np.swapaxes(v.reshape(v.shape[:-1] + (n, 128)), -1, -2))


def _consts():
    i = np.arange(128)
    same = (i[:, None] // 64) == (i[None, :] // 64)
    ones = np.ones((128, 128), np.float32)
    uinc = (same & (i[:, None] <= i[None, :])).astype(np.float32)
    vsuf = (same & (i[:, None] > i[None, :])).astype(np.float32)
    ident = np.eye(128, dtype=np.float32)
    return np.stack([ones, uinc, vsuf, ident]).astype(np.float32)


def _qk_gain(g):
    g = np.asarray(g, np.float32)
    out = np.ones((128, 3), np.float32)
    out[:, 0] = g[0:128]
    out[0:64, 1] = g[128:192]
    out[0:64, 2] = np.concatenate([g[160:192], g[128:160]])
    return out


def _rope_consts():
    out = np.zeros((128, 130), np.float32)
    for m in range(32):
        out[m + 32, m] = -1.0
        out[m, m + 32] = 1.0
    inv = (1.0 / (np.float32(10000.0) ** (np.arange(0, 64, 2, dtype=np.float32) / np.float32(64)))).astype(np.float32)
    out[0:32, 64] = inv
    out[32:64, 64] = inv
    return out


def prep_core_inputs(inp, b, S, cfg, names=None):
    NL = cfg.get("n_layers", 4)
    NA = cfg.get("n_a", 2)
    gen = {
        "xT": lambda: np.ascontiguousarray(inp["x"][b, :S].T),
        "c_pj": lambda: _pj(inp["c"][b]),
        "ada_w": lambda: np.ascontiguousarray(inp["ada_w"][:NL]),
        "ada_b": lambda: _pj(inp["ada_b"][:NL]),
        "norm_mix": lambda: _pj(inp["norm_mix"][:NL]),
        "norm_mlp": lambda: _pj(inp["norm_mlp"][:NL]),
        "mlp_w1": lambda: np.ascontiguousarray(inp["mlp_w1"][:NL]),
        "mlp_w2": lambda: np.ascontiguousarray(inp["mlp_w2"][:NL]),
        "consts": lambda: _consts(),
        "kv_ada_w": lambda: np.ascontiguousarray(inp["kv_ada_w"]),
        "kv_ada_b": lambda: _pj(inp["kv_ada_b"]),
        "kv_in_norm": lambda: _pj(inp["kv_in_norm"]),
        "mla_w_dkv": lambda: np.ascontiguousarray(inp["mla_w_dkv"]),
        "mla_kv_norm": lambda: _pj(inp["mla_kv_norm"]),
        "mla_w_ukv": lambda: np.ascontiguousarray(inp["mla_w_ukv"]),
        "mla_k_norm": lambda: _qk_gain(inp["mla_k_norm"]),
        "mla_w_dq": lambda: np.ascontiguousarray(inp["mla_w_dq"][:NL - NA]),
        "mla_q_lat_norm": lambda: _pj(inp["mla_q_lat_norm"][:NL - NA]),
        "mla_w_uq": lambda: np.ascontiguousarray(inp["mla_w_uq"][:NL - NA]),
        "mla_q_norm": lambda: np.stack([_qk_gain(inp["mla_q_norm"][j]) for j in range(NL - NA)]),
        "mla_w_o": lambda: np.ascontiguousarray(inp["mla_w_o"][:NL - NA]),
        "positions": lambda: np.ascontiguousarray(inp["positions"][b, :S]).astype(np.int32),
        "rope_c": lambda: _rope_consts(),
        "hg_w_in": lambda: np.ascontiguousarray(inp["hg_w_in"][:min(NL, 2)]),
        "hg_lower": lambda: np.ascontiguousarray(inp["hg_lower"]),
        "hg_o_norm": lambda: np.ascontiguousarray(inp["hg_o_norm"][:min(NL, 2)].reshape(-1, 128, 1)),
        "hg_w_out": lambda: np.ascontiguousarray(inp["hg_w_out"][:min(NL, 2)]),
    }
    names = names or list(gen.keys())
    return {k: gen[k]() for k in names}


_CACHE = {}


def kernel(**inputs):
    S = 4096
    cfg = {}
    if "prog" not in _CACHE:
        _CACHE["prog"] = build_program(S, cfg)
    nc, p = _CACHE["prog"]
    inp = {k: np.asarray(v) for k, v in inputs.items()}
    in_maps = [prep_core_inputs(inp, b, S, cfg, p.in_names) for b in range(8)]
    res = run_bass_kernel_spmd(nc, in_maps, core_ids=list(range(8)))
    out = np.empty((8, S, D), np.float32)
    for b in range(8):
        out[b] = res.results[b]["outT"].T
    return out
```

```python
import math
from contextlib import ExitStack

import numpy as np
import concourse.bass as bass
import concourse.mybir as mybir
from concourse.bass_utils import run_bass_kernel_spmd

F32 = mybir.dt.float32
BF16 = mybir.dt.bfloat16
I32 = mybir.dt.int32
AF = mybir.ActivationFunctionType
ALU = mybir.AluOpType

D = 2048
DC = 16
TT = 512
DFF = 8192
FC = 64
NH = 16
EPS = 1e-6
QK = 192
SLAB_BYTES = 16384


class Sem:
    __slots__ = ("h", "val", "name")

    def __init__(self, h, name):
        self.h = h
        self.val = 0
        self.name = name


class Buf:
    __slots__ = ("w", "r", "name")

    def __init__(self, name=""):
        self.w = {}
        self.r = {}
        self.name = name


class Eng:
    def __init__(self, name, eng, sem, is_pe=False):
        self.name = name
        self.eng = eng
        self.sem = sem
        self.waited = {}
        self.is_pe = is_pe
        self.dma_sems = []
        self.dma_rr = 0


class Prog:
    def __init__(self, nc, es, n_dma_sems=20):
        self.nc = nc
        self.es = es
        mk = lambda n: Sem(es.enter_context(nc.semaphore(n)), n)
        self.pe = Eng("pe", nc.tensor, mk("s_pe"), is_pe=True)
        self.act = Eng("act", nc.scalar, mk("s_act"))
        self.dve = Eng("dve", nc.vector, mk("s_dve"))
        self.pool = Eng("pool", nc.gpsimd, mk("s_pool"))
        self.sp = Eng("sp", nc.sync, mk("s_sp"))
        for q in (self.sp, self.pool, self.act):
            q.dma_sems = [mk(f"d_{q.name}{i}") for i in range(n_dma_sems)]
        self.n_inst = 0

    def _wait(self, E, sem, val):
        if E.waited.get(sem, 0) < val:
            E.eng.wait_ge(sem.h, val)
            E.waited[sem] = val

    def _deps(self, E, reads, writes):
        need = {}
        for b in reads:
            for s, v in b.w.items():
                if need.get(s, 0) < v:
                    need[s] = v
        for b in writes:
            for s, v in b.w.items():
                if need.get(s, 0) < v:
                    need[s] = v
            for s, v in b.r.items():
                if s is E.sem:
                    continue
                if need.get(s, 0) < v:
                    need[s] = v
        for s, v in need.items():
            if E.is_pe and s is E.sem:
                continue
            self._wait(E, s, v)

    def op(self, E, emit, reads=(), writes=()):
        self._deps(E, reads, writes)
        inst = emit()
        E.sem.val += 1
        inst.then_inc(E.sem.h, 1)
        v = E.sem.val
        for b in writes:
            b.w = {E.sem: v}
            b.r = {}
        for b in reads:
            b.r[E.sem] = v
        self.n_inst += 1
        return inst

    def dma(self, Q, out, in_, reads=(), writes=(), **kw):
        s = Q.dma_sems[Q.dma_rr % len(Q.dma_sems)]
        Q.dma_rr += 1
        self._wait(Q, s, s.val)
        self._deps(Q, reads, writes)
        inst = Q.eng.dma_start(out=out, in_=in_, **kw)
        s.val += 16
        inst.then_inc(s.h, 16)
        for b in writes:
            b.w = {s: s.val}
            b.r = {}
        for b in reads:
            b.r[s] = s.val
        self.n_inst += 1
        return s

    def merge(self, olds, news):
        w = {}
        for b in olds:
            for d in (b.w, b.r):
                for s, v in d.items():
                    if w.get(s, 0) < v:
                        w[s] = v
        for b in news:
            for s, v in w.items():
                if b.w.get(s, 0) < v:
                    b.w[s] = v

    def final_wait(self, E, bufs):
        need = {}
        for b in bufs:
            for s, v in b.w.items():
                need[s] = max(need.get(s, 0), v)
        for s, v in need.items():
            self._wait(E, s, v)


class Arr:
    def __init__(self, t, n, name):
        self.t = t
        self.n = n
        self.bufs = [Buf(f"{name}[{i}]") for i in range(n)]

    def b(self, i):
        return self.bufs[i]

    def all(self):
        return list(self.bufs)


class WSpec:
    def __init__(self, name, din, dout, sw, kgs=None):
        self.name = name
        self.din, self.dout, self.sw = din, dout, sw
        self.kc = din // 128
        self.kgs = kgs or self.kc
        self.nkg = self.kc // self.kgs
        self.ncg = dout // sw
        assert self.kgs * sw * 2 <= SLAB_BYTES + 2048, (name, self.kgs, sw)


def build_program(S, cfg):
    NT = S // TT
    n_layers = cfg.get("n_layers", 4)
    NA = cfg.get("n_a", 2)
    do_mix = cfg.get("mix", True)
    do_mlp = cfg.get("mlp", True)
    nc = bass.Bass("TRN2", target_bir_lowering=False, dynamic_dma_scratch_size=8192)
    es = ExitStack()
    p = Prog(nc, es)
    pe, act, dve, pool, sp = p.pe, p.act, p.dve, p.pool, p.sp
    p.in_names = []

    def dram_in(name, shape, dt=F32):
        p.in_names.append(name)
        return nc.dram_tensor(name, list(shape), dt, kind="ExternalInput").ap()

    def dram_tmp(name, shape, dt):
        return nc.dram_tensor(name, list(shape), dt, kind="Internal").ap()

    def sb(name, shape, dt):
        return es.enter_context(nc.sbuf_tensor(name, list(shape), dt))

    NL = n_layers
    NAL = min(NA, NL)
    xT = dram_in("xT", [D, S])
    outT = nc.dram_tensor("outT", [D, S], F32, kind="ExternalOutput").ap()
    c_pj = dram_in("c_pj", [128, DC])
    ada_w = dram_in("ada_w", [NL, D, 6 * D])
    ada_b = dram_in("ada_b", [NL, 128, 96])
    nmix = dram_in("norm_mix", [NL, 128, DC])
    nmlp = dram_in("norm_mlp", [NL, 128, DC])
    mlp_w1 = dram_in("mlp_w1", [NL, D, DFF])
    mlp_w2 = dram_in("mlp_w2", [NL, DFF, D])
    consts = dram_in("consts", [4, 128, 128])
    if do_mix and NAL > 0:
        hg_w_in = dram_in("hg_w_in", [NAL, D, 4 * D])
        hg_lower = dram_in("hg_lower", [3, D])
        hg_o_norm = dram_in("hg_o_norm", [NAL, 128, 1])
        hg_w_out = dram_in("hg_w_out", [NAL, D, D])

    do_mla = do_mix and NL > NA
    NBL = NL - NA if do_mla else 0
    if do_mla:
        kv_ada_w = dram_in("kv_ada_w", [D, 2 * D])
        kv_ada_b = dram_in("kv_ada_b", [128, 32])
        kv_in_norm = dram_in("kv_in_norm", [128, DC])
        w_dkv = dram_in("mla_w_dkv", [D, 576])
        kv_norm = dram_in("mla_kv_norm", [128, 4])
        w_ukv = dram_in("mla_w_ukv", [512, 4096])
        k_norm = dram_in("mla_k_norm", [128, 3])
        w_dq = dram_in("mla_w_dq", [NBL, D, 512])
        q_lat_norm = dram_in("mla_q_lat_norm", [NBL, 128, 4])
        w_uq = dram_in("mla_w_uq", [NBL, 512, 3072])
        q_norm = dram_in("mla_q_norm", [NBL, 128, 3])
        w_o = dram_in("mla_w_o", [NBL, D, D])
        positions = dram_in("positions", [S], I32)
        rope_c = dram_in("rope_c", [128, 130])

    cst = sb("cst", [128, 4, 128], F32)
    b_cst = Buf("cst")
    p.dma(sp, cst[:], consts.rearrange("a p c -> p a c"), writes=[b_cst])
    ones32 = cst[:, 0, :]
    Uinc = cst[:, 1, :]
    Vsuf = cst[:, 2, :]
    ident_bf = sb("ident_bf", [128, 128], BF16)
    b_ident = Buf("ident")
    p.op(dve, lambda: nc.vector.tensor_copy(out=ident_bf[:], in_=cst[:, 3, :]), reads=[b_cst], writes=[b_ident])
    ones_bf = sb("ones_bf", [128, 128], BF16)
    p.op(dve, lambda: nc.vector.tensor_copy(out=ones_bf[:], in_=cst[:, 0, :]), reads=[b_cst, b_ident], writes=[b_ident])
    b_ones32 = b_cst
    eps_t = sb("eps_t", [128, 1], F32)
    b_eps = Buf("eps")
    p.op(dve, lambda: nc.vector.memset(eps_t[:], EPS), writes=[b_eps])
    cond = sb("cond", [128, DC], F32)
    b_cond = Buf("cond")
    craw = sb("craw", [128, DC], F32)
    b_craw = Buf("craw")
    p.dma(sp, craw[:], c_pj, writes=[b_craw])
    p.op(act, lambda: nc.scalar.activation(out=cond[:], in_=craw[:], func=AF.Silu),
         reads=[b_craw], writes=[b_cond])

    modv = [sb(f"modv{l}", [128, 96], F32) for l in range(NL)]
    b_modv = [Buf(f"modv{l}") for l in range(NL)]
    A1m = [sb(f"A1_{l}", [128, DC], F32) for l in range(NL)]
    A2m = [sb(f"A2_{l}", [128, DC], F32) for l in range(NL)]
    b_A = [Buf(f"A{l}") for l in range(NL)]

    NB = 8
    banks = [es.enter_context(nc.psum_tensor(f"bank{i}", [128, 512], F32)) for i in range(NB)]
    bank_b = [Buf(f"bank{i}") for i in range(NB)]
    free_banks = list(range(NB))

    def balloc():
        assert free_banks, "out of PSUM banks"
        return free_banks.pop(0)

    def bfree(i):
        free_banks.append(i)

    NSLAB = cfg.get("nslab", 3)
    slab_t = [sb(f"slab{i}", [128, SLAB_BYTES // 2], BF16) for i in range(NSLAB)]
    slab_b = [Buf(f"slab{i}") for i in range(NSLAB)]
    rr = {}

    def nxt(key, n):
        i = rr.get(key, 0)
        rr[key] = i + 1
        return i % n

    wsc = {}

    def cast_weight(key, src2d, spec):
        dst = dram_tmp("wb_" + key, [spec.ncg, spec.nkg, 128, spec.kgs, spec.sw], BF16)
        bufs = {}
        for cg in range(spec.ncg):
            for kg in range(spec.nkg):
                src = src2d[kg * spec.kgs * 128:(kg + 1) * spec.kgs * 128,
                            cg * spec.sw:(cg + 1) * spec.sw].rearrange("(k p) c -> p k c", p=128)
                b = Buf(f"wb_{key}_{cg}_{kg}")
                p.dma(pool, dst[cg, kg], src, writes=[b])
                bufs[(cg, kg)] = b
        wsc[key] = (dst, spec, bufs)

    def load_slab(key, cg, kg=0, q=None):
        dst, spec, bufs = wsc[key]
        i = nxt("slab", NSLAB)
        view = slab_t[i][:, 0:spec.kgs * spec.sw].rearrange("p (k c) -> p k c", c=spec.sw)
        p.dma(q or sp, view, dst[cg, kg], reads=[bufs[(cg, kg)]], writes=[slab_b[i]])
        return view, slab_b[i]

    def cast_layer(l):
        if do_mix and l < NAL:
            cast_weight(f"win_{l}", hg_w_in[l], WSpec("win", D, 4 * D, 512))
            cast_weight(f"wout_{l}", hg_w_out[l], WSpec("wout", D, D, 512))
        if do_mla and l == NA:
            cast_weight("wdkv_c", w_dkv[:, 0:512], WSpec("wdkv_c", D, 512, 512))
            cast_weight("wdkv_r", w_dkv[:, 512:576], WSpec("wdkv_r", D, 64, 64))
            cast_weight("wukv", w_ukv, WSpec("wukv", 512, 4096, 2048))
        if do_mla and l >= NA:
            cast_weight(f"wdq_{l}", w_dq[l - NA], WSpec("wdq", D, 512, 512))
            cast_weight(f"wuq_{l}", w_uq[l - NA], WSpec("wuq", 512, 3072, 1536))
            cast_weight(f"wo_{l}", w_o[l - NA], WSpec("wo", D, D, 512))
        if do_mlp:
            cast_weight(f"w1_{l}", mlp_w1[l], WSpec("w1", D, DFF, 512))
            cast_weight(f"w2_{l}", mlp_w2[l], WSpec("w2", DFF, D, 512, kgs=16))

    for l_ in range(n_layers):
        cast_layer(l_)

    R_BYTES = 66 * 1024
    R_t = sb("R", [128, R_BYTES // 2], BF16)

    def R_f32(off, n):
        return R_t[:, off // 2: off // 2 + 2 * n].bitcast(F32)

    def R_bf(off, n):
        return R_t[:, off // 2: off // 2 + n]

    stage_b = []
    adab_t = sb("adab_t", [128, 96], F32)
    b_adab = Buf("adab")
    gain_t = sb("gain_t", [128, 2 * DC], F32)
    b_gain = Buf("gain")
    rowbuf = [sb(f"rowbuf{i}", [1, 256], F32) for i in range(2)]
    row_b = [Buf(f"row{i}") for i in range(2)]
    one1 = cst[0:1, 0, 0:1]
    if do_mla:
        kvmodv = sb("kvmodv", [128, 32], F32)
        Akv = sb("Akv", [128, DC], F32)
        b_kvmod = Buf("kvmod")

    def ada_block(l):
        is_kv = (l == n_layers)
        ncol = 32 if is_kv else 96
        wsrc = kv_ada_w if is_kv else ada_w[l]
        mb = balloc()
        pend_tr = []
        for sg_ in range(ncol // 2):
            i = nxt("slab", NSLAB)
            sv = slab_t[i][:].bitcast(F32).rearrange("p (k c) -> p k c", c=256)
            p.dma(sp, sv, wsrc[:, sg_ * 256:(sg_ + 1) * 256].rearrange("(k p) c -> p k c", p=128),
                  writes=[slab_b[i]])
            rb = balloc()
            for k in range(DC):
                p.op(pe, lambda: nc.tensor.matmul(
                    banks[rb][0:1, 0:256], lhsT=cond[:, k:k + 1], rhs=sv[:, k, :],
                    start=(k == 0), stop=(k == DC - 1)), reads=[slab_b[i], b_cond], writes=[bank_b[rb]])
            ri = nxt("row", 2)
            p.op(act, lambda: nc.scalar.copy(out=rowbuf[ri][0:1, :], in_=banks[rb][0:1, 0:256]),
                 reads=[bank_b[rb]], writes=[row_b[ri]])
            bfree(rb)

            def tr(ri=ri, sg_=sg_):
                for oc in range(2):
                    col = sg_ * 2 + oc
                    p.op(pe, lambda: nc.tensor.matmul(
                        banks[mb][:, col:col + 1], lhsT=rowbuf[ri][0:1, oc * 128:(oc + 1) * 128], rhs=one1,
                        start=True, stop=True), reads=[row_b[ri], b_cst], writes=[bank_b[mb]])
            if pend_tr:
                pend_tr.pop(0)()
            pend_tr.append(tr)
        pend_tr.pop(0)()
        if is_kv:
            p.dma(sp, adab_t[:, 0:32], kv_ada_b, writes=[b_adab])
            p.op(dve, lambda: nc.vector.tensor_tensor(out=kvmodv[:], in0=banks[mb][:, 0:32],
                                                      in1=adab_t[:, 0:32], op=ALU.add),
                 reads=[bank_b[mb], b_adab], writes=[b_kvmod])
            bfree(mb)
            p.dma(sp, gain_t[:, 0:DC], kv_in_norm, writes=[b_gain])
            p.op(dve, lambda: nc.vector.scalar_tensor_tensor(
                out=Akv[:], in0=kvmodv[:, 16:32], scalar=1.0, in1=gain_t[:, 0:DC],
                op0=ALU.add, op1=ALU.mult), reads=[b_kvmod, b_gain], writes=[b_kvmod])
            return
        p.dma(sp, adab_t[:], ada_b[l], writes=[b_adab])
        p.op(dve, lambda: nc.vector.tensor_tensor(out=modv[l][:], in0=banks[mb][:, 0:96],
                                                  in1=adab_t[:], op=ALU.add),
             reads=[bank_b[mb], b_adab], writes=[b_modv[l]])
        bfree(mb)
        p.dma(sp, gain_t[:, 0:DC], nmix[l], writes=[b_gain])
        p.dma(sp, gain_t[:, DC:2 * DC], nmlp[l], writes=[b_gain])
        p.op(dve, lambda: nc.vector.scalar_tensor_tensor(
            out=A1m[l][:], in0=modv[l][:, 16:32], scalar=1.0, in1=gain_t[:, 0:DC],
            op0=ALU.add, op1=ALU.mult), reads=[b_modv[l], b_gain], writes=[b_A[l]])
        p.op(dve, lambda: nc.vector.scalar_tensor_tensor(
            out=A2m[l][:], in0=modv[l][:, 64:80], scalar=1.0, in1=gain_t[:, DC:2 * DC],
            op0=ALU.add, op1=ALU.mult), reads=[b_modv[l], b_gain], writes=[b_A[l]])

    pro_bufs = list(stage_b)

    if do_mix and NAL > 0:
        lbs = dram_tmp("lbs", [2, 128, D], F32)
        b_lbs = Buf("lbs")
        OFF = 40 * 1024
        r3 = R_f32(OFF, 3 * D)
        r3v = r3.rearrange("p (a n) -> p a n", n=D)
        b_r3 = Buf("r3")
        p.merge(stage_b, [b_r3])
        p.dma(sp, r3, hg_lower.rearrange("a n -> (a n)").partition_broadcast(128), writes=[b_r3])
        mx = R_f32(0, D)
        sm = R_f32(8192, D)
        b_mx = Buf("mx")
        p.merge(stage_b, [b_mx])
        p.op(dve, lambda: nc.vector.tensor_tensor(out=mx, in0=r3v[:, 0, :], in1=r3v[:, 1, :], op=ALU.max),
             reads=[b_r3], writes=[b_mx])
        p.op(dve, lambda: nc.vector.tensor_tensor(out=mx, in0=mx, in1=r3v[:, 2, :], op=ALU.max),
             reads=[b_r3, b_mx], writes=[b_mx])
        for a_ in range(3):
            p.op(dve, lambda a_=a_: nc.vector.tensor_tensor(out=r3v[:, a_, :], in0=r3v[:, a_, :], in1=mx,
                                                            op=ALU.subtract), reads=[b_r3, b_mx], writes=[b_r3])
        p.op(act, lambda: nc.scalar.activation(out=r3, in_=r3, func=AF.Exp), reads=[b_r3], writes=[b_r3])
        p.op(dve, lambda: nc.vector.tensor_tensor(out=r3v[:, 1, :], in0=r3v[:, 0, :], in1=r3v[:, 1, :], op=ALU.add),
             reads=[b_r3], writes=[b_r3])
        p.op(dve, lambda: nc.vector.tensor_tensor(out=sm, in0=r3v[:, 1, :], in1=r3v[:, 2, :], op=ALU.add),
             reads=[b_r3, b_mx], writes=[b_mx])
        p.op(dve, lambda: nc.vector.reciprocal(out=sm, in_=sm), reads=[b_mx], writes=[b_mx])
        for a_ in range(2):
            p.op(dve, lambda a_=a_: nc.vector.tensor_tensor(out=r3v[:, a_, :], in0=r3v[:, a_, :], in1=sm,
                                                            op=ALU.mult), reads=[b_r3, b_mx], writes=[b_r3])
        p.dma(sp, lbs.rearrange("a p n -> p a n"), r3v[:, 0:2, :], reads=[b_r3], writes=[b_lbs])
        pro_bufs += [b_r3, b_mx]
        ogain = sb("ogain", [128, NAL], F32)
        b_ogain = Buf("ogain")
        for l in range(NAL):
            p.dma(sp, ogain[:, l:l + 1], hg_o_norm[l], writes=[b_ogain])

    x_t = sb("x_t", [128, DC, TT], F32)
    x = Arr(x_t, DC, "x")
    h_t = sb("h_t", [128, DC, TT], BF16)
    h = Arr(h_t, DC, "h")
    a_view = R_bf(0, FC * TT).rearrange("p (c t) -> p c t", t=TT)
    a = Arr(None, FC, "a")
    p.merge(pro_bufs, a.all())
    rstd_t = sb("rstd", [128, TT], F32)
    b_rstd = Buf("rstd")
    NTMP = 5
    tmp_t = [sb(f"tmp{i}", [128, TT], F32) for i in range(NTMP)]
    tmp_b = [Buf(f"tmp{i}") for i in range(NTMP)]

    def rms_stats(chunks, n_feat, out_ap, out_b):
        bk = balloc()
        n = len(chunks)
        for i, (ap, b, npart) in enumerate(chunks):
            if npart < 0:
                npart = -npart
                p.op(pe, lambda i=i, npart=npart, ap=ap: nc.tensor.matmul(
                    banks[bk][:], lhsT=ones_bf[0:npart, :], rhs=ap,
                    start=(i == 0), stop=(i == n - 1)), reads=[b, b_ident], writes=[bank_b[bk]])
                continue
            si = nxt("tmp", NTMP)
            sqv = tmp_t[si][:, 0:TT // 2].bitcast(BF16)
            p.op(act, lambda ap=ap, sqv=sqv, npart=npart: nc.scalar.activation(
                out=sqv[0:npart, :], in_=ap, func=AF.Square), reads=[b], writes=[tmp_b[si]])
            p.op(pe, lambda sqv=sqv, i=i, npart=npart: nc.tensor.matmul(
                banks[bk][:], lhsT=ones_bf[0:npart, :], rhs=sqv[0:npart, :],
                start=(i == 0), stop=(i == n - 1)), reads=[tmp_b[si], b_ident], writes=[bank_b[bk]])
        p.op(act, lambda: nc.scalar.activation(out=out_ap, in_=banks[bk][:], func=AF.Ln,
                                               scale=1.0 / n_feat, bias=eps_t[:, 0:1]),
             reads=[bank_b[bk], b_eps], writes=[out_b])
        bfree(bk)
        p.op(act, lambda: nc.scalar.activation(out=out_ap, in_=out_ap, func=AF.Exp, scale=-0.5),
             reads=[out_b], writes=[out_b])

    def norm_mod(Acols, shcols, bA, bsh):
        rms_stats([(x_t[:, c, :], x.b(c), 128) for c in range(DC)], D, rstd_t[:], b_rstd)
        for c in range(DC):
            ti = nxt("tmp", NTMP)
            p.op(dve, lambda c=c, ti=ti: nc.vector.scalar_tensor_tensor(
                out=tmp_t[ti][:], in0=x_t[:, c, :], scalar=Acols[:, c:c + 1], in1=rstd_t[:],
                op0=ALU.mult, op1=ALU.mult), reads=[x.b(c), bA, b_rstd], writes=[tmp_b[ti]])
            p.op(act, lambda c=c, ti=ti: nc.scalar.activation(
                out=h_t[:, c, :], in_=tmp_t[ti][:], func=AF.Identity, bias=shcols[:, c:c + 1]),
                reads=[tmp_b[ti], bsh], writes=[h.b(c)])

    def proj_epilogue(key, l, rhs_view, rhs_bufs, gcol0):
        for cg in range(4):
            wv, wb = load_slab(key, cg)
            for dcl in range(4):
                dc = cg * 4 + dcl
                bk = balloc()
                for k in range(16):
                    p.op(pe, lambda k=k, dcl=dcl, wv=wv, bk=bk: nc.tensor.matmul(
                        banks[bk][:], lhsT=wv[:, k, dcl * 128:(dcl + 1) * 128], rhs=rhs_view[:, k, :],
                        start=(k == 0), stop=(k == 15)), reads=[wb, rhs_bufs[k]], writes=[bank_b[bk]])
                p.op(dve, lambda dc=dc, bk=bk: nc.vector.scalar_tensor_tensor(
                    out=x_t[:, dc, :], in0=banks[bk][:], scalar=modv[l][:, gcol0 + dc:gcol0 + dc + 1],
                    in1=x_t[:, dc, :], op0=ALU.mult, op1=ALU.add),
                    reads=[bank_b[bk], b_modv[l], x.b(dc)], writes=[x.b(dc)])
                bfree(bk)

    def mlp_block(l, on_chunk_done=None):
        norm_mod(A2m[l], modv[l][:, 48:64], b_A[l], b_modv[l])
        for s_ in range(16):
            wv, wb = load_slab(f"w1_{l}", s_)
            for oc in range(4):
                bk = balloc()
                for k in range(DC):
                    p.op(pe, lambda k=k, oc=oc, wv=wv, bk=bk: nc.tensor.matmul(
                        banks[bk][:], lhsT=wv[:, k, oc * 128:(oc + 1) * 128], rhs=h_t[:, k, :],
                        start=(k == 0), stop=(k == DC - 1)),
                        reads=[wb, h.b(k)], writes=[bank_b[bk]])
                fc = s_ * 4 + oc
                ti = nxt("tmp", NTMP)
                p.op(act, lambda bk=bk, ti=ti: nc.scalar.activation(
                    out=tmp_t[ti][:], in_=banks[bk][:], func=AF.Relu),
                    reads=[bank_b[bk]], writes=[tmp_b[ti]])
                bfree(bk)
                p.op(dve, lambda fc=fc, ti=ti: nc.vector.tensor_tensor(
                    out=a_view[:, fc, :], in0=tmp_t[ti][:], in1=tmp_t[ti][:], op=ALU.mult),
                    reads=[tmp_b[ti]], writes=[a.b(fc)])
        for cg in range(4):
            bks = [balloc() for _ in range(4)]
            for kg in range(4):
                wv, wb = load_slab(f"w2_{l}", cg, kg)
                for dcl in range(4):
                    for k in range(16):
                        fc = kg * 16 + k
                        p.op(pe, lambda k=k, dcl=dcl, wv=wv, fc=fc, bk=bks[dcl], kg=kg: nc.tensor.matmul(
                            banks[bk][:], lhsT=wv[:, k, dcl * 128:(dcl + 1) * 128], rhs=a_view[:, fc, :],
                            start=(kg == 0 and k == 0), stop=(kg == 3 and k == 15)),
                            reads=[wb, a.b(fc)], writes=[bank_b[bks[dcl]]])
            for dcl in range(4):
                dc = cg * 4 + dcl
                p.op(dve, lambda dc=dc, bk=bks[dcl]: nc.vector.scalar_tensor_tensor(
                    out=x_t[:, dc, :], in0=banks[bk][:], scalar=modv[l][:, 80 + dc:81 + dc],
                    in1=x_t[:, dc, :], op0=ALU.mult, op1=ALU.add),
                    reads=[bank_b[bks[dcl]], b_modv[l], x.b(dc)], writes=[x.b(dc)])
                bfree(bks[dcl])
                if on_chunk_done is not None:
                    on_chunk_done(dc)

    if do_mix and NAL > 0:
        KB = 1024
        hA = [[R_f32((j * 4 + r * 2) * KB, 512) for r in range(2)] for j in range(4)]
        hA_b = [[Buf(f"hA{j}_{r}") for r in range(2)] for j in range(4)]
        ktil = R_bf(16 * KB, 4 * 512).rearrange("p (a c) -> p a c", c=512)
        khat = R_bf(20 * KB, 4 * 512).rearrange("p (a c) -> p a c", c=512)
        ivt = R_bf(24 * KB, 4 * 512).rearrange("p (a c) -> p a c", c=512)
        ktil_b = [Buf(f"ktil{i}") for i in range(4)]
        khat_b = [Buf(f"khat{i}") for i in range(4)]
        ivt_b = [Buf(f"ivt{i}") for i in range(4)]
        E_T = R_f32(28 * KB, 4 * 512).rearrange("p (a c) -> p a c", c=512)
        E_b = [Buf(f"E{i}") for i in range(4)]
        qtil = R_bf(36 * KB, 4 * 512).rearrange("p (a c) -> p a c", c=512)
        ktilT = R_bf(40 * KB, 4 * 512).rearrange("p (a c) -> p a c", c=512)
        sgt = R_bf(44 * KB, 4 * 512).rearrange("p (a c) -> p a c", c=512)
        qtil_b = [Buf(f"qtil{i}") for i in range(4)]
        ktilT_b = [Buf(f"ktilT{i}") for i in range(4)]
        sgt_b = [Buf(f"sg{i}") for i in range(4)]
        og = R_bf(48 * KB, 16 * 512).rearrange("p (a c) -> p a c", c=512)
        og_b = [Buf(f"og{i}") for i in range(16)]
        Pblk = [R_bf(64 * KB + i * 256, 128) for i in range(4)]
        Pblk_b = [Buf(f"P{i}") for i in range(4)]
        hg_bufs = ([b for row in hA_b for b in row] + ktil_b + khat_b + ivt_b + E_b + qtil_b + ktilT_b
                   + sgt_b + og_b + Pblk_b)
        lb_t = [sb(f"lb{i}", [128, 512], F32) for i in range(1)]
        lb_b = [Buf(f"lb{i}") for i in range(1)]
        Sb_t = sb("Sb_t", [128, 4, 128], F32)
        Sb_b = [Buf(f"Sb{i}") for i in range(4)]
        Sbm = R_bf(65 * 1024, 4 * 128).rearrange("p (a c) -> p a c", c=128)
        Sbm_b = [Buf(f"Sbm{i}") for i in range(4)]
        hg_bufs += Sbm_b
        rs_t = [sb(f"rs{i}", [128, TT], F32) for i in range(2)]
        rs_b = [Buf(f"rs{i}") for i in range(2)]
        S_t = [sb(f"S{l}", [128, NH, 128], F32) for l in range(NAL)]
        S_b = [[Buf(f"S{l}_{hd}") for hd in range(NH)] for l in range(NAL)]
        Sbf = sb("Sbf", [128, NH, 128], BF16)
        Sbf_b = [Buf(f"Sbf{hd}") for hd in range(NH)]
        for l in range(NAL):
            p.op(pool, lambda l=l: nc.gpsimd.memset(S_t[l][:], 0.0), writes=S_b[l])

    def hgrn_block(l):
        p.merge(a.all(), hg_bufs)
        p.op(act, lambda: nc.scalar.copy(out=Sbf[:], in_=S_t[l][:]), reads=S_b[l], writes=Sbf_b)
        norm_mod(A1m[l], modv[l][:, 0:16], b_A[l], b_modv[l])
        key = f"win_{l}"
        for g in range(4):
            li = nxt("lb", 1)
            p.dma(sp, lb_t[li][:], lbs[l][:, g * 512:(g + 1) * 512], reads=[b_lbs], writes=[lb_b[li]])
            slabs = {}

            def Zproj(tb):
                if "z" not in slabs:
                    slabs["z"] = load_slab(key, 4 + g)
                zv, zb = slabs["z"]
                r = tb % 2
                a1, a2 = hA[0][r], hA[1][r]
                b1, b2 = hA_b[0][r], hA_b[1][r]
                bk = balloc()
                for k in range(DC):
                    p.op(pe, lambda: nc.tensor.matmul(
                        banks[bk][:], lhsT=h_t[:, k, tb * 128:(tb + 1) * 128], rhs=zv[:, k, :],
                        start=(k == 0), stop=(k == DC - 1)), reads=[h.b(k), zb], writes=[bank_b[bk]])
                p.op(act, lambda: nc.scalar.activation(out=a1, in_=banks[bk][:], func=AF.Sigmoid),
                     reads=[bank_b[bk]], writes=[b1])
                bfree(bk)
                p.op(dve, lambda: nc.vector.scalar_tensor_tensor(
                    out=a2, in0=a1, scalar=-1.0, in1=lb_t[li][:], op0=ALU.add, op1=ALU.mult),
                    reads=[b1, lb_b[li]], writes=[b2])
                p.op(dve, lambda: nc.vector.tensor_tensor(out=a1, in0=a1, in1=a2, op=ALU.subtract),
                     reads=[b1, b2], writes=[b1])
                p.op(act, lambda: nc.scalar.activation(out=a2, in_=a1, func=AF.Ln), reads=[b1], writes=[b2])
                p.op(pool, lambda: nc.gpsimd.tensor_scalar(out=a1, in0=a1, scalar1=-1.0, scalar2=1.0,
                                                           op0=ALU.mult, op1=ALU.add), reads=[b1], writes=[b1])

            def cum(tb):
                r = tb % 2
                a1, a2, a3, a4 = (hA[j][r] for j in range(4))
                b1, b2, b3, b4 = (hA_b[j][r] for j in range(4))
                bk = balloc()
                p.op(pe, lambda: nc.tensor.matmul(banks[bk][:], lhsT=Uinc, rhs=a2, start=True, stop=True),
                     reads=[b_cst, b2], writes=[bank_b[bk]])
                p.op(act, lambda: nc.scalar.activation(out=a3, in_=banks[bk][:], func=AF.Exp, scale=-1.0),
                     reads=[bank_b[bk]], writes=[b3])
                bfree(bk)
                bk = balloc()
                p.op(pe, lambda: nc.tensor.matmul(banks[bk][:], lhsT=Vsuf, rhs=a2, start=True, stop=True),
                     reads=[b_cst, b2], writes=[bank_b[bk]])
                p.op(act, lambda: nc.scalar.activation(out=a4, in_=banks[bk][:], func=AF.Exp),
                     reads=[bank_b[bk]], writes=[b4])
                bfree(bk)
                bk = balloc()
                for hh in range(4):
                    p.op(pe, lambda: nc.tensor.matmul(
                        banks[bk][:, hh * 128:(hh + 1) * 128], lhsT=a2[:, hh * 128:(hh + 1) * 128], rhs=Uinc,
                        start=True, stop=True), reads=[b_cst, b2], writes=[bank_b[bk]])
                p.op(act, lambda: nc.scalar.activation(
                    out=E_T[:, :, tb * 128:(tb + 1) * 128],
                    in_=banks[bk][:].rearrange("p (a c) -> p a c", c=128), func=AF.Exp),
                    reads=[bank_b[bk]], writes=E_b)
                bfree(bk)
                p.op(dve, lambda: nc.vector.tensor_tensor(out=ktil[:, tb, :], in0=a1, in1=a3, op=ALU.mult),
                     reads=[b1, b3], writes=[ktil_b[tb]])
                p.op(pool, lambda: nc.gpsimd.tensor_tensor(out=khat[:, tb, :], in0=a1, in1=a4, op=ALU.mult),
                     reads=[b1, b4], writes=[khat_b[tb]])

            def Iproj(tb):
                if "i" not in slabs:
                    slabs["i"] = load_slab(key, 8 + g)
                iv_, ib = slabs["i"]
                bk = balloc()
                for k in range(DC):
                    p.op(pe, lambda: nc.tensor.matmul(
                        banks[bk][:], lhsT=h_t[:, k, tb * 128:(tb + 1) * 128], rhs=iv_[:, k, :],
                        start=(k == 0), stop=(k == DC - 1)), reads=[h.b(k), ib], writes=[bank_b[bk]])
                p.op(act, lambda: nc.scalar.copy(out=ivt[:, tb, :], in_=banks[bk][:]),
                     reads=[bank_b[bk]], writes=[ivt_b[tb]])
                bfree(bk)

            def Gproj(hh):
                if "g" not in slabs:
                    slabs["g"] = load_slab(key, 12 + g)
                gv, gb = slabs["g"]
                bk = balloc()
                for k in range(DC):
                    p.op(pe, lambda: nc.tensor.matmul(
                        banks[bk][:], lhsT=gv[:, k, hh * 128:(hh + 1) * 128], rhs=h_t[:, k, :],
                        start=(k == 0), stop=(k == DC - 1)), reads=[h.b(k), gb], writes=[bank_b[bk]])
                p.op(act, lambda: nc.scalar.activation(out=sgt[:, hh, :], in_=banks[bk][:], func=AF.Silu),
                     reads=[bank_b[bk]], writes=[sgt_b[hh]])
                bfree(bk)

            def Qproj(hh):
                if "q" not in slabs:
                    slabs["q"] = load_slab(key, g)
                qv, qb = slabs["q"]
                bk = balloc()
                for k in range(DC):
                    p.op(pe, lambda: nc.tensor.matmul(
                        banks[bk][:], lhsT=qv[:, k, hh * 128:(hh + 1) * 128], rhs=h_t[:, k, :],
                        start=(k == 0), stop=(k == DC - 1)), reads=[h.b(k), qb], writes=[bank_b[bk]])
                p.op(dve, lambda: nc.vector.scalar_tensor_tensor(
                    out=qtil[:, hh, :], in0=banks[bk][:], scalar=128 ** -0.5, in1=E_T[:, hh, :],
                    op0=ALU.mult, op1=ALU.mult), reads=[bank_b[bk], E_b[hh]], writes=[qtil_b[hh]])
                bfree(bk)

            def Ttrans(hh):
                bk = balloc()
                bv = banks[bk][:].bitcast(BF16)
                for tb in range(4):
                    p.op(pe, lambda: nc.tensor.transpose(
                        out=bv[:, tb * 128:(tb + 1) * 128], in_=ktil[:, tb, hh * 128:(hh + 1) * 128],
                        identity=ident_bf[:]), reads=[ktil_b[tb], b_ident], writes=[bank_b[bk]])
                p.op(dve, lambda: nc.vector.tensor_copy(out=ktilT[:, hh, :], in_=bv[:, 0:512]),
                     reads=[bank_b[bk]], writes=[ktilT_b[hh]])
                bfree(bk)

            Zproj(0); Zproj(1); Iproj(0); cum(0)
            Zproj(2); Iproj(1); cum(1)
            Zproj(3); Iproj(2); cum(2)
            Gproj(0); Iproj(3); cum(3)
            Gproj(1); Gproj(2); Gproj(3)
            for hh in range(4):
                Qproj(hh)
            for hh in range(4):
                Ttrans(hh)
            obk = [balloc() for _ in range(4)]
            for tb in range(4):
                sbk = balloc()
                for hh in range(4):
                    p.op(pe, lambda: nc.tensor.matmul(
                        banks[sbk][:, hh * 128:(hh + 1) * 128], lhsT=ktilT[:, hh, tb * 128:(tb + 1) * 128],
                        rhs=qtil[:, hh, tb * 128:(tb + 1) * 128], start=True, stop=True),
                        reads=[ktilT_b[hh], qtil_b[hh]], writes=[bank_b[sbk]])
                ubk = [balloc(), balloc()]
                for hh in range(4):
                    for hf in range(2):
                        uc = hh * 128
                        p.op(pe, lambda: nc.tensor.matmul(
                            banks[ubk[hf]][:, uc:uc + 128],
                            lhsT=khat[hf * 64:(hf + 1) * 64, tb, hh * 128:(hh + 1) * 128],
                            rhs=ivt[hf * 64:(hf + 1) * 64, tb, hh * 128:(hh + 1) * 128], start=True, stop=True),
                            reads=[khat_b[tb], ivt_b[tb]], writes=[bank_b[ubk[hf]]])
                for hh in range(4):
                    p.op(dve, lambda: nc.vector.tensor_tensor(out=Pblk[hh], in0=banks[sbk][:, hh * 128:(hh + 1) * 128],
                                                              in1=Uinc, op=ALU.mult),
                         reads=[bank_b[sbk], b_cst], writes=[Pblk_b[hh]])
                bfree(sbk)
                for hh in range(4):
                    head = 4 * g + hh
                    c0 = tb * 128
                    p.op(pe, lambda: nc.tensor.matmul(
                        banks[obk[hh]][:, c0:c0 + 128], lhsT=ivt[:, tb, hh * 128:(hh + 1) * 128],
                        rhs=Pblk[hh], start=True, stop=False),
                        reads=[ivt_b[tb], Pblk_b[hh]], writes=[bank_b[obk[hh]]])
                    p.op(pe, lambda: nc.tensor.matmul(
                        banks[obk[hh]][:, c0:c0 + 64], lhsT=Sbf[:, head, :], rhs=qtil[:, hh, c0:c0 + 64],
                        start=False, stop=False), reads=[Sbf_b[head], qtil_b[hh]], writes=[bank_b[obk[hh]]])
                for hh in range(4):
                    head = 4 * g + hh
                    c0 = tb * 128
                    uc = hh * 128
                    p.op(dve, lambda: nc.vector.scalar_tensor_tensor(
                        out=Sb_t[:, hh, :], in0=S_t[l][:, head, :], scalar=E_T[:, hh, c0 + 63:c0 + 64],
                        in1=banks[ubk[0]][:, uc:uc + 128], op0=ALU.mult, op1=ALU.add),
                        reads=[S_b[l][head], E_b[hh], bank_b[ubk[0]]], writes=[Sb_b[hh]])
                    p.op(act, lambda: nc.scalar.copy(out=Sbm[:, hh, :], in_=Sb_t[:, hh, :]),
                         reads=[Sb_b[hh]], writes=[Sbm_b[hh]])
                    p.op(dve, lambda: nc.vector.scalar_tensor_tensor(
                        out=S_t[l][:, head, :], in0=Sb_t[:, hh, :], scalar=E_T[:, hh, c0 + 127:c0 + 128],
                        in1=banks[ubk[1]][:, uc:uc + 128], op0=ALU.mult, op1=ALU.add),
                        reads=[Sb_b[hh], E_b[hh], bank_b[ubk[1]]], writes=[S_b[l][head]])
                    p.op(act, lambda: nc.scalar.copy(out=Sbf[:, head, :], in_=S_t[l][:, head, :]),
                         reads=[S_b[l][head]], writes=[Sbf_b[head]])
                bfree(ubk[0])
                bfree(ubk[1])
                for hh in range(4):
                    c0 = tb * 128 + 64
                    p.op(pe, lambda: nc.tensor.matmul(
                        banks[obk[hh]][:, c0:c0 + 64], lhsT=Sbm[:, hh, :], rhs=qtil[:, hh, c0:c0 + 64],
                        start=False, stop=True), reads=[Sbm_b[hh], qtil_b[hh]], writes=[bank_b[obk[hh]]])
            for hh in range(4):
                head = 4 * g + hh
                ri = nxt("rs", 2)
                rms_stats([(banks[obk[hh]][:], bank_b[obk[hh]], 128)], 128, rs_t[ri][:], rs_b[ri])
                ti = nxt("tmp", NTMP)
                p.op(dve, lambda: nc.vector.scalar_tensor_tensor(
                    out=tmp_t[ti][:], in0=banks[obk[hh]][:], scalar=ogain[:, l:l + 1], in1=rs_t[ri][:],
                    op0=ALU.mult, op1=ALU.mult), reads=[bank_b[obk[hh]], b_ogain, rs_b[ri]], writes=[tmp_b[ti]])
                bfree(obk[hh])
                p.op(pool, lambda: nc.gpsimd.tensor_tensor(out=og[:, head, :], in0=tmp_t[ti][:], in1=sgt[:, hh, :], op=ALU.mult),
                     reads=[tmp_b[ti], sgt_b[hh]], writes=[og_b[head]])
        proj_epilogue(f"wout_{l}", l, og, og_b, 32)
        p.merge(hg_bufs, a.all())

    if do_mla:
        KB = 1024
        SM_SCALE = QK ** -0.5
        ropec = sb("ropec", [128, 130], F32)
        b_ropec = Buf("ropec")
        p.dma(sp, ropec[:], rope_c, writes=[b_ropec])
        Rm = ropec[0:64, 0:64]
        mg = sb("mla_gains", [128, 4 + 3 + 4 * NBL + 3 * NBL], F32)
        b_mg = Buf("mg")
        p.dma(sp, mg[:, 0:4], kv_norm, writes=[b_mg])
        p.dma(sp, mg[:, 4:7], k_norm, writes=[b_mg])
        for j in range(NBL):
            p.dma(sp, mg[:, 7 + 7 * j:11 + 7 * j], q_lat_norm[j], writes=[b_mg])
            p.dma(sp, mg[:, 11 + 7 * j:14 + 7 * j], q_norm[j], writes=[b_mg])
        cs_t = sb("cossin", [64, 2, TT], F32)
        b_cs = Buf("cossin")
        negpi = sb("negpi", [128, 1], F32)
        utri_bf = sb("utri_bf", [128, 128], BF16)
        b_mc = Buf("mla_consts")
        p.op(dve, lambda: nc.vector.memset(negpi[:], -math.pi), writes=[b_mc])
        p.op(dve, lambda: nc.vector.tensor_copy(out=utri_bf[:], in_=cst[:, 1, :]), reads=[b_cst, b_mc], writes=[b_mc])
        p.op(dve, lambda: nc.vector.tensor_copy(out=utri_bf[0:64, 64:128], in_=cst[0:64, 0, 64:128]),
             reads=[b_cst, b_mc], writes=[b_mc])
        KN = dram_tmp("KN", [NH, 128, S], BF16)
        KR = dram_tmp("KR", [NH, 64, S], BF16)
        VV = dram_tmp("VV", [S // 128, 128, NH * 128], BF16)
        kvd_b = [[[Buf(f"kv{t}_{hp}_{i}") for i in range(3)] for hp in range(8)] for t in range(NT)]
        ao = R_bf(0, 16 * 512).rearrange("p (a c) -> p a c", c=512)
        ao_b = [Buf(f"ao{i}") for i in range(16)]
        latn = R_bf(16 * KB, 4 * 512).rearrange("p (a c) -> p a c", c=512)
        latn_b = [Buf(f"latn{i}") for i in range(4)]
        qn2 = [R_bf(20 * KB + r * 2 * KB, 2 * 512).rearrange("p (a c) -> p a c", c=512) for r in range(2)]
        qr2 = [R_bf(24 * KB + r * 2 * KB, 2 * 512).rearrange("p (a c) -> p a c", c=512) for r in range(2)]
        q2_b = [Buf(f"q2_{r}") for r in range(2)]
        kvc = [[R_bf(base + i * 2 * KB, 1024) for i in range(3)] for base in (28 * KB, 34 * KB, 60 * KB)]
        kvc_b = [[Buf(f"kvc{r}_{i}") for i in range(3)] for r in range(3)]
        NPT = 4
        ATT_DEPTH = cfg.get("att_depth", 3)
        Pt = [R_bf(40 * KB + i * KB, 512) for i in range(NPT)]
        Pt_b = [Buf(f"Pt{i}") for i in range(NPT)]
        csg = R_f32(44 * KB, 2 * 512).rearrange("p (a c) -> p a c", c=512)
        b_csg = Buf("csg")
        kr0 = R_f32(48 * KB, 512)
        b_kr0 = Buf("kr0")
        sqpe = R_bf(50 * KB, 512)
        b_sqpe = Buf("sqpe")
        rawr = [R_f32(52 * KB + i * 2 * KB, 512) for i in range(2)]
        rawr_b = [Buf(f"rawr{i}") for i in range(2)]
        rsh = [R_f32(56 * KB + i * 2 * KB, 512) for i in range(2)]
        rsh_b = [Buf(f"rsh{i}") for i in range(2)]
        mla_bufs = (ao_b + latn_b + q2_b + [b for row in kvc_b for b in row] + Pt_b
                    + [b_csg, b_kr0, b_sqpe] + rawr_b + rsh_b)

    def rope_tables(t):
        ti = nxt("tmp", NTMP)
        pos_i = tmp_t[ti][0:64, :].bitcast(I32)
        p.dma(sp, pos_i, positions[t * TT:(t + 1) * TT].partition_broadcast(64), writes=[tmp_b[ti]])
        ai = nxt("tmp", NTMP)
        ang = tmp_t[ai][0:64, :]
        p.op(dve, lambda: nc.vector.tensor_copy(out=ang, in_=pos_i), reads=[tmp_b[ti]], writes=[tmp_b[ai]])
        p.op(dve, lambda: nc.vector.tensor_scalar(out=ang, in0=ang, scalar1=ropec[0:64, 64:65], scalar2=None,
                                                  op0=ALU.mult), reads=[tmp_b[ai], b_ropec], writes=[tmp_b[ai]])
        for which, phase in ((0, 0.25), (1, 0.0)):
            yi_ = nxt("tmp", NTMP)
            y = tmp_t[yi_][0:64, :]
            p.op(dve, lambda: nc.vector.tensor_scalar(out=y, in0=ang, scalar1=1.0 / (2 * math.pi), scalar2=phase + 0.5,
                                                      op0=ALU.mult, op1=ALU.add), reads=[tmp_b[ai]], writes=[tmp_b[yi_]])
            ni_ = nxt("tmp", NTMP)
            n_i = tmp_t[ni_][0:64, :].bitcast(I32)
            nf_ = nxt("tmp", NTMP)
            n_f = tmp_t[nf_][0:64, :]
            p.op(dve, lambda: nc.vector.tensor_copy(out=n_i, in_=y), reads=[tmp_b[yi_]], writes=[tmp_b[ni_]])
            p.op(dve, lambda: nc.vector.tensor_copy(out=n_f, in_=n_i), reads=[tmp_b[ni_]], writes=[tmp_b[nf_]])
            p.op(dve, lambda: nc.vector.tensor_tensor(out=y, in0=y, in1=n_f, op=ALU.subtract),
                 reads=[tmp_b[yi_], tmp_b[nf_]], writes=[tmp_b[yi_]])
            p.op(dve, lambda: nc.vector.scalar_tensor_tensor(out=y, in0=y, scalar=0.0, in1=y, op0=ALU.is_lt, op1=ALU.add),
                 reads=[tmp_b[yi_]], writes=[tmp_b[yi_]])
            p.op(dve, lambda: nc.vector.tensor_scalar(out=y, in0=y, scalar1=2 * math.pi, scalar2=-math.pi,
                                                      op0=ALU.mult, op1=ALU.add), reads=[tmp_b[yi_]], writes=[tmp_b[yi_]])
            p.op(dve, lambda: nc.vector.tensor_scalar(out=y, in0=y, scalar1=-3.1415925, scalar2=3.1415925,
                                                      op0=ALU.max, op1=ALU.min), reads=[tmp_b[yi_]], writes=[tmp_b[yi_]])
            p.op(act, lambda: nc.scalar.activation(out=cs_t[:, which, :], in_=y, func=AF.Sin),
                 reads=[tmp_b[yi_]], writes=[b_cs])

    def fold_gain(gcol):
        p.op(dve, lambda: nc.vector.tensor_scalar(out=csg[0:64, 0, :], in0=cs_t[:, 0, :], scalar1=mg[0:64, gcol + 1:gcol + 2],
                                                  scalar2=None, op0=ALU.mult), reads=[b_cs, b_mg], writes=[b_csg])
        p.op(dve, lambda: nc.vector.tensor_scalar(out=csg[0:64, 1, :], in0=cs_t[:, 1, :], scalar1=mg[0:64, gcol + 2:gcol + 3],
                                                  scalar2=None, op0=ALU.mult), reads=[b_cs, b_mg, b_csg], writes=[b_csg])

    def roped(raw_bank, out_ap, out_reads, out_writes, rstd_ap, rstd_b):
        ri = nxt("rawr", 2)
        p.op(act, lambda: nc.scalar.copy(out=rawr[ri][0:64, :], in_=banks[raw_bank][0:64, :]),
             reads=[bank_b[raw_bank]], writes=[rawr_b[ri]])
        rb = balloc()
        p.op(pe, lambda: nc.tensor.matmul(banks[rb][0:64, :], lhsT=Rm, rhs=rawr[ri][0:64, :], start=True, stop=True),
             reads=[rawr_b[ri], b_ropec], writes=[bank_b[rb]])
        t1 = nxt("tmp", NTMP)
        p.op(dve, lambda: nc.vector.tensor_tensor(out=tmp_t[t1][0:64, :], in0=banks[rb][0:64, :], in1=csg[0:64, 1, :], op=ALU.mult),
             reads=[bank_b[rb], b_csg], writes=[tmp_b[t1]])
        bfree(rb)
        p.op(pool, lambda: nc.gpsimd.tensor_tensor(out=rawr[ri][0:64, :], in0=rawr[ri][0:64, :], in1=csg[0:64, 0, :], op=ALU.mult),
             reads=[rawr_b[ri], b_csg], writes=[rawr_b[ri]])
        p.op(dve, lambda: nc.vector.tensor_tensor(out=tmp_t[t1][0:64, :], in0=tmp_t[t1][0:64, :], in1=rawr[ri][0:64, :], op=ALU.add),
             reads=[tmp_b[t1], rawr_b[ri]], writes=[tmp_b[t1]])
        if rstd_ap is None:
            p.op(act, lambda: nc.scalar.copy(out=out_ap, in_=tmp_t[t1][0:64, :]), reads=[tmp_b[t1]] + out_reads, writes=out_writes)
        else:
            p.op(dve, lambda: nc.vector.tensor_tensor(out=out_ap, in0=tmp_t[t1][0:64, :], in1=rstd_ap, op=ALU.mult),
                 reads=[tmp_b[t1], rstd_b] + out_reads, writes=out_writes)

    def lat_proj(key, gcol0):
        wv, wb = load_slab(key, 0)
        bks = []
        for cc in range(4):
            bk = balloc()
            bks.append(bk)
            for k in range(DC):
                p.op(pe, lambda k=k, cc=cc, bk=bk: nc.tensor.matmul(
                    banks[bk][:], lhsT=wv[:, k, cc * 128:(cc + 1) * 128], rhs=h_t[:, k, :],
                    start=(k == 0), stop=(k == DC - 1)), reads=[wb, h.b(k)], writes=[bank_b[bk]])
        rms_stats([(banks[bk][:], bank_b[bk], 128) for bk in bks], 512, rstd_t[:], b_rstd)
        for cc in range(4):
            p.op(dve, lambda cc=cc: nc.vector.scalar_tensor_tensor(
                out=latn[:, cc, :], in0=banks[bks[cc]][:], scalar=mg[:, gcol0 + cc:gcol0 + cc + 1], in1=rstd_t[:],
                op0=ALU.mult, op1=ALU.mult), reads=[bank_b[bks[cc]], b_mg, b_rstd], writes=[latn_b[cc]])
            bfree(bks[cc])

    def head_stats(nope_bank, rope_sq_ap, rope_sq_b, ri):
        rms_stats([(banks[nope_bank][:], bank_b[nope_bank], 128), (rope_sq_ap, rope_sq_b, -64)], QK, rsh[ri], rsh_b[ri])

    def kv_block(t):
        p.merge(a.all(), mla_bufs)
        norm_mod(Akv, kvmodv[:, 0:16], b_kvmod, b_kvmod)
        lat_proj("wdkv_c", 0)
        wv, wb = load_slab("wdkv_r", 0)
        pb = balloc()
        for k in range(DC):
            p.op(pe, lambda k=k: nc.tensor.matmul(banks[pb][0:64, :], lhsT=wv[:, k, 0:64], rhs=h_t[:, k, :],
                                                  start=(k == 0), stop=(k == DC - 1)),
                 reads=[wb, h.b(k)], writes=[bank_b[pb]])
        p.op(act, lambda: nc.scalar.activation(out=sqpe[0:64, :], in_=banks[pb][0:64, :], func=AF.Square),
             reads=[bank_b[pb]], writes=[b_sqpe])
        fold_gain(4)
        roped(pb, kr0[0:64, :], [], [b_kr0], None, None)
        bfree(pb)
        uv = [load_slab("wukv", 0), None]
        for hp in range(8):
            if hp == 4:
                uv[1] = load_slab("wukv", 1)
            wv, wb = uv[hp // 4]
            r = nxt("kvc", 3)
            kn2s = kvc[r][0].rearrange("p (a c) -> p a c", c=512)
            kr2s = kvc[r][1].rearrange("p (a c) -> p a c", c=512)
            v2s = kvc[r][2].rearrange("p (a c) -> p a c", c=256)
            for hh in range(2):
                hd = 2 * hp + hh
                c0 = (hd % 8) * 256
                bk = balloc()
                for kc in range(4):
                    p.op(pe, lambda kc=kc: nc.tensor.matmul(
                        banks[bk][:], lhsT=wv[:, kc, c0:c0 + 128], rhs=latn[:, kc, :],
                        start=(kc == 0), stop=(kc == 3)), reads=[wb, latn_b[kc]], writes=[bank_b[bk]])
                ri = nxt("rsh", 2)
                head_stats(bk, sqpe[0:64, :], b_sqpe, ri)
                p.op(dve, lambda: nc.vector.scalar_tensor_tensor(
                    out=kn2s[:, hh, :], in0=banks[bk][:], scalar=mg[:, 4:5], in1=rsh[ri],
                    op0=ALU.mult, op1=ALU.mult), reads=[bank_b[bk], b_mg, rsh_b[ri]], writes=[kvc_b[r][0]])
                bfree(bk)
                p.op(pool, lambda: nc.gpsimd.tensor_tensor(out=kr2s[0:64, hh, :], in0=kr0[0:64, :], in1=rsh[ri][0:64, :], op=ALU.mult),
                     reads=[b_kr0, rsh_b[ri]], writes=[kvc_b[r][1]])
            c0 = ((2 * hp) % 8) * 256
            for tb in range(4):
                bk = balloc()
                rhs_v = wv[:, :, c0:c0 + 512].rearrange("p k (h c) -> p k h c", c=256)
                for kc in range(4):
                    p.op(pe, lambda kc=kc: nc.tensor.matmul(
                        banks[bk][:, 0:256], lhsT=latn[:, kc, tb * 128:(tb + 1) * 128], rhs=rhs_v[:, kc, :, 128:256],
                        start=(kc == 0), stop=(kc == 3)), reads=[wb, latn_b[kc]], writes=[bank_b[bk]])
                p.op(act, lambda: nc.scalar.copy(out=v2s[:, tb, :], in_=banks[bk][:, 0:256]),
                     reads=[bank_b[bk]], writes=[kvc_b[r][2]])
                bfree(bk)
            p.dma(pool, KN[2 * hp:2 * hp + 2, :, t * TT:(t + 1) * TT].rearrange("a p s -> p a s"), kn2s,
                  reads=[kvc_b[r][0]], writes=[kvd_b[t][hp][0]])
            p.dma(pool, KR[2 * hp:2 * hp + 2, :, t * TT:(t + 1) * TT].rearrange("a p s -> p a s"), kr2s[0:64],
                  reads=[kvc_b[r][1]], writes=[kvd_b[t][hp][1]])
            p.dma(pool, VV[4 * t:4 * t + 4, :, hp * 256:(hp + 1) * 256].rearrange("a p c -> p a c"), v2s,
                  reads=[kvc_b[r][2]], writes=[kvd_b[t][hp][2]])
        p.merge(mla_bufs, a.all())

    def mla_block(l, t):
        j = l - NA
        g0 = 7 + 7 * j
        p.merge(a.all(), mla_bufs)
        for r_ in range(3):
            p.op(pool, lambda: nc.gpsimd.memset(kvc[r_][1][64:128, :], 0.0), writes=[kvc_b[r_][1]])
        for r_ in range(2):
            p.op(pool, lambda: nc.gpsimd.memset(qr2[r_][64:128, :, :], 0.0), writes=[q2_b[r_]])
        norm_mod(A1m[l], modv[l][:, 0:16], b_A[l], b_modv[l])
        lat_proj(f"wdq_{l}", g0)
        fold_gain(g0 + 4)
        uq = [load_slab(f"wuq_{l}", 0), None]

        def load_chunk(hp, jt):
            r = nxt("kvc", 3)
            kn2c = kvc[r][0].rearrange("p (a c) -> p a c", c=512)
            kr2c = kvc[r][1].rearrange("p (a c) -> p a c", c=512)
            v2c = kvc[r][2].rearrange("p (a c) -> p a c", c=256)
            p.dma(sp, kn2c, KN[2 * hp:2 * hp + 2, :, jt * TT:(jt + 1) * TT].rearrange("a p s -> p a s"),
                  reads=[kvd_b[jt][hp][0]], writes=[kvc_b[r][0]])
            p.dma(sp, kr2c[0:64], KR[2 * hp:2 * hp + 2, :, jt * TT:(jt + 1) * TT].rearrange("a p s -> p a s"),
                  reads=[kvd_b[jt][hp][1]], writes=[kvc_b[r][1]])
            p.dma(sp, v2c, VV[4 * jt:4 * jt + 4, :, hp * 256:(hp + 1) * 256].rearrange("a p c -> p a c"),
                  reads=[kvd_b[jt][hp][2]], writes=[kvc_b[r][2]])
            return (r, kn2c, kr2c, v2c)

        def qproj(hp):
            if hp == 4:
                uq[1] = load_slab(f"wuq_{l}", 1)
            wv, wb = uq[hp // 4]
            qi = nxt("q2", 2)
            for hh in range(2):
                hd = 2 * hp + hh
                c0 = (hd % 8) * 192
                bn = balloc()
                for kc in range(4):
                    p.op(pe, lambda kc=kc: nc.tensor.matmul(
                        banks[bn][:], lhsT=wv[:, kc, c0:c0 + 128], rhs=latn[:, kc, :],
                        start=(kc == 0), stop=(kc == 3)), reads=[wb, latn_b[kc]], writes=[bank_b[bn]])
                br = balloc()
                for kc in range(4):
                    p.op(pe, lambda kc=kc: nc.tensor.matmul(
                        banks[br][0:64, :], lhsT=wv[:, kc, c0 + 128:c0 + 192], rhs=latn[:, kc, :],
                        start=(kc == 0), stop=(kc == 3)), reads=[wb, latn_b[kc]], writes=[bank_b[br]])
                ri = nxt("rsh", 2)
                rms_stats([(banks[bn][:], bank_b[bn], 128), (banks[br][0:64, :], bank_b[br], 64)], QK, rsh[ri], rsh_b[ri])
                p.op(dve, lambda: nc.vector.scalar_tensor_tensor(
                    out=qn2[qi][:, hh, :], in0=banks[bn][:], scalar=mg[:, g0 + 4:g0 + 5], in1=rsh[ri],
                    op0=ALU.mult, op1=ALU.mult), reads=[bank_b[bn], b_mg, rsh_b[ri]], writes=[q2_b[qi]])
                bfree(bn)
                roped(br, qr2[qi][0:64, hh, :], [], [q2_b[qi]], rsh[ri][0:64, :], rsh_b[ri])
                bfree(br)
            return qi

        def att(hp, qi):
            ch = {0: load_chunk(hp, 0)}
            obk = [balloc() for _ in range(2)]
            dbk = [balloc() for _ in range(2)]
            pend = []

            def flush_one():
                (pi, hh, kb, q0, first, last, v2c, rv) = pend.pop(0)
                p.op(pe, lambda: nc.tensor.matmul(
                    banks[obk[hh]][:, q0:TT], lhsT=v2c[:, kb, hh * 128:(hh + 1) * 128], rhs=Pt[pi][:, q0:TT],
                    start=first, stop=last), reads=[kvc_b[rv][2], Pt_b[pi]], writes=[bank_b[obk[hh]]])
                p.op(pe, lambda: nc.tensor.matmul(
                    banks[dbk[hh]][:, q0:TT], lhsT=ones_bf[:], rhs=Pt[pi][:, q0:TT],
                    start=first, stop=last), reads=[b_ident, Pt_b[pi]], writes=[bank_b[dbk[hh]]])

            for jt in range(t + 1):
                if jt + 1 <= t:
                    ch[jt + 1] = load_chunk(hp, jt + 1)
                r, kn2c, kr2c, v2c = ch[jt]
                for hh in range(2):
                    for kb in range(4):
                        diag = (jt == t)
                        q0 = kb * 128 if diag else 0
                        first = (jt == 0 and kb == 0)
                        last = (jt == t and kb == 3)
                        sbk = balloc()
                        p.op(pe, lambda: nc.tensor.matmul(
                            banks[sbk][:, q0:TT], lhsT=kn2c[:, hh, kb * 128:(kb + 1) * 128], rhs=qn2[qi][:, hh, q0:TT],
                            start=True, stop=False), reads=[kvc_b[r][0], q2_b[qi]], writes=[bank_b[sbk]])
                        p.op(pe, lambda: nc.tensor.matmul(
                            banks[sbk][:, q0:TT], lhsT=kr2c[:, hh, kb * 128:(kb + 1) * 128], rhs=qr2[qi][:, hh, q0:TT],
                            start=False, stop=True), reads=[kvc_b[r][1], q2_b[qi]], writes=[bank_b[sbk]])
                        pi = nxt("Pt", NPT)
                        p.op(act, lambda: nc.scalar.activation(out=Pt[pi][:, q0:TT], in_=banks[sbk][:, q0:TT],
                                                               func=AF.Exp, scale=SM_SCALE),
                             reads=[bank_b[sbk]], writes=[Pt_b[pi]])
                        bfree(sbk)
                        if diag:
                            p.op(pool, lambda: nc.gpsimd.tensor_tensor(out=Pt[pi][:, q0:q0 + 128], in0=Pt[pi][:, q0:q0 + 128],
                                                                       in1=utri_bf[:], op=ALU.mult),
                                 reads=[Pt_b[pi], b_mc], writes=[Pt_b[pi]])
                        pend.append((pi, hh, kb, q0, first, last, v2c, r))
                        if len(pend) > ATT_DEPTH:
                            flush_one()
            while pend:
                flush_one()
            for hh in range(2):
                hd = 2 * hp + hh
                ti = nxt("tmp", NTMP)
                p.op(act, lambda: nc.scalar.activation(out=tmp_t[ti][:], in_=banks[dbk[hh]][:], func=AF.Ln),
                     reads=[bank_b[dbk[hh]]], writes=[tmp_b[ti]])
                bfree(dbk[hh])
                p.op(act, lambda: nc.scalar.activation(out=tmp_t[ti][:], in_=tmp_t[ti][:], func=AF.Exp, scale=-1.0),
                     reads=[tmp_b[ti]], writes=[tmp_b[ti]])
                p.op(dve, lambda: nc.vector.tensor_tensor(out=ao[:, hd, :], in0=banks[obk[hh]][:], in1=tmp_t[ti][:], op=ALU.mult),
                     reads=[bank_b[obk[hh]], tmp_b[ti]], writes=[ao_b[hd]])
                bfree(obk[hh])

        qi_cur = qproj(0)
        for hp in range(8):
            qi_next = qproj(hp + 1) if hp + 1 < 8 else None
            att(hp, qi_cur)
            qi_cur = qi_next
        proj_epilogue(f"wo_{l}", l, ao, ao_b, 32)
        p.merge(mla_bufs, a.all())

    out_bufs = []
    ada_block(0)
    src0 = xT[:, 0:TT].rearrange("(c p) s -> p c s", p=128)
    p.dma(sp, x_t[:], src0, writes=x.all())
    for t in range(NT):
        if do_mla:
            rope_tables(t)

        def chunk_done(dc, t=t):
            ob = Buf(f"out{t}_{dc}")
            p.dma(act, outT[dc * 128:(dc + 1) * 128, t * TT:(t + 1) * TT], x_t[:, dc, :], reads=[x.b(dc)], writes=[ob])
            out_bufs.append(ob)
            if t + 1 < NT:
                p.dma(act, x_t[:, dc, :], xT[dc * 128:(dc + 1) * 128, (t + 1) * TT:(t + 2) * TT], writes=[x.b(dc)])

        for l in range(n_layers):
            last = (l == n_layers - 1)
            if do_mla and l == NA:
                kv_block(t)
            if do_mix and l < NAL:
                hgrn_block(l)
            if do_mla and l >= NA:
                mla_block(l, t)
            if t == 0 and l + 1 < n_layers:
                ada_block(l + 1)
            if do_mlp:
                mlp_block(l, chunk_done if last else None)
            if t == 0 and do_mla and l == 0:
                ada_block(n_layers)
        if not do_mlp:
            for dc in range(DC):
                chunk_done(dc)
    p.final_wait(sp, out_bufs)
    es.close()
    return nc, p


def _pj(v):
    v = np.asarray(v)
    n = v.shape[-1] // 128
    return np.ascontiguousarray(np.swapaxes(v.reshape(v.shape[:-1] + (n, 128)), -1, -2))


def _consts():
    i = np.arange(128)
    same = (i[:, None] // 64) == (i[None, :] // 64)
    ones = np.ones((128, 128), np.float32)
    uinc = (same & (i[:, None] <= i[None, :])).astype(np.float32)
    vsuf = (same & (i[:, None] > i[None, :])).astype(np.float32)
    ident = np.eye(128, dtype=np.float32)
    return np.stack([ones, uinc, vsuf, ident]).astype(np.float32)


def _qk_gain(g):
    g = np.asarray(g, np.float32)
    out = np.ones((128, 3), np.float32)
    out[:, 0] = g[0:128]
    out[0:64, 1] = g[128:192]
    out[0:64, 2] = np.concatenate([g[160:192], g[128:160]])
    return out


def _rope_consts():
    out = np.zeros((128, 130), np.float32)
    for m in range(32):
        out[m + 32, m] = -1.0
        out[m, m + 32] = 1.0
    inv = (1.0 / (np.float32(10000.0) ** (np.arange(0, 64, 2, dtype=np.float32) / np.float32(64)))).astype(np.float32)
    out[0:32, 64] = inv
    out[32:64, 64] = inv
    return out


def prep_core_inputs(inp, b, S, cfg, names=None):
    NL = cfg.get("n_layers", 4)
    NA = cfg.get("n_a", 2)
    gen = {
        "xT": lambda: np.ascontiguousarray(inp["x"][b, :S].T),
        "c_pj": lambda: _pj(inp["c"][b]),
        "ada_w": lambda: np.ascontiguousarray(inp["ada_w"][:NL]),
        "ada_b": lambda: _pj(inp["ada_b"][:NL]),
        "norm_mix": lambda: _pj(inp["norm_mix"][:NL]),
        "norm_mlp": lambda: _pj(inp["norm_mlp"][:NL]),
        "mlp_w1": lambda: np.ascontiguousarray(inp["mlp_w1"][:NL]),
        "mlp_w2": lambda: np.ascontiguousarray(inp["mlp_w2"][:NL]),
        "consts": lambda: _consts(),
        "kv_ada_w": lambda: np.ascontiguousarray(inp["kv_ada_w"]),
        "kv_ada_b": lambda: _pj(inp["kv_ada_b"]),
        "kv_in_norm": lambda: _pj(inp["kv_in_norm"]),
        "mla_w_dkv": lambda: np.ascontiguousarray(inp["mla_w_dkv"]),
        "mla_kv_norm": lambda: _pj(inp["mla_kv_norm"]),
        "mla_w_ukv": lambda: np.ascontiguousarray(inp["mla_w_ukv"]),
        "mla_k_norm": lambda: _qk_gain(inp["mla_k_norm"]),
        "mla_w_dq": lambda: np.ascontiguousarray(inp["mla_w_dq"][:NL - NA]),
        "mla_q_lat_norm": lambda: _pj(inp["mla_q_lat_norm"][:NL - NA]),
        "mla_w_uq": lambda: np.ascontiguousarray(inp["mla_w_uq"][:NL - NA]),
        "mla_q_norm": lambda: np.stack([_qk_gain(inp["mla_q_norm"][j]) for j in range(NL - NA)]),
        "mla_w_o": lambda: np.ascontiguousarray(inp["mla_w_o"][:NL - NA]),
        "positions": lambda: np.ascontiguousarray(inp["positions"][b, :S]).astype(np.int32),
        "rope_c": lambda: _rope_consts(),
        "hg_w_in": lambda: np.ascontiguousarray(inp["hg_w_in"][:min(NL, 2)]),
        "hg_lower": lambda: np.ascontiguousarray(inp["hg_lower"]),
        "hg_o_norm": lambda: np.ascontiguousarray(inp["hg_o_norm"][:min(NL, 2)].reshape(-1, 128, 1)),
        "hg_w_out": lambda: np.ascontiguousarray(inp["hg_w_out"][:min(NL, 2)]),
    }
    names = names or list(gen.keys())
    return {k: gen[k]() for k in names}


_CACHE = {}


def kernel(**inputs):
    S = 4096
    cfg = {}
    if "prog" not in _CACHE:
        _CACHE["prog"] = build_program(S, cfg)
    nc, p = _CACHE["prog"]
    inp = {k: np.asarray(v) for k, v in inputs.items()}
    in_maps = [prep_core_inputs(inp, b, S, cfg, p.in_names) for b in range(8)]
    res = run_bass_kernel_spmd(nc, in_maps, core_ids=list(range(8)))
    out = np.empty((8, S, D), np.float32)
    for b in range(8):
        out[b] = res.results[b]["outT"].T
    return out
```
